# Optimizing a Trainium2 kernel written in Bass

```python
import jax, jax.numpy as jnp
from jax import lax
import numpy as np

D_MODEL = 1024
BATCH = 2
SEQ = 8192
DEPTH = 1

CHUNK = 64
CONV_W = 4
EPS = 1e-6
NEG_INIT = -1e30
D_MIX_M = D_MODEL
H_M = 4
DV_M = D_MIX_M // H_M
DK_M = DV_M // 2
D_MIX_G = D_MODEL
H_G = 4
DV_G = D_MIX_G // H_G
DK_G = DV_G // 2
GATE_RANK = 16
GATE_NORM = 16.0
D_MIX = D_MIX_M + D_MIX_G
SPLIT_SIZES = (
    H_M * DK_M, H_M * DK_M, D_MIX_M, H_M, H_M, D_MIX_M, D_MIX_M,
    H_G * DK_G, H_G * DK_G, D_MIX_G, GATE_RANK, D_MIX_G,
)
D_IN = sum(SPLIT_SIZES)

kernel_name = "hymba_mlstm_gla_block"


def _rmsnorm(x, w):
    xf = x.astype(jnp.float32)
    r = lax.rsqrt(jnp.mean(xf * xf, axis=-1, keepdims=True) + EPS)
    return (xf * r).astype(x.dtype) * w


def _head_rmsnorm(h, w, n_heads):
    b, s, d = h.shape
    hf = h.astype(jnp.float32).reshape(b, s, n_heads, d // n_heads)
    r = lax.rsqrt(jnp.mean(hf * hf, axis=-1, keepdims=True) + EPS)
    return (hf * r).reshape(b, s, d).astype(h.dtype) * w


def _split_cols(t, sizes):
    out, start = [], 0
    for n in sizes:
        out.append(t[..., start:start + n])
        start += n
    return out


def _causal_conv(u, w, bias):
    s = u.shape[1]
    up = jnp.pad(u, ((0, 0), (CONV_W - 1, 0), (0, 0)))
    out = bias
    for j in range(CONV_W):
        out = out + w[j] * up[:, j:j + s]
    return out


def _to_chunks(t, n_heads):
    b, s, _ = t.shape
    t = t.reshape(b, s // CHUNK, CHUNK, n_heads, -1)
    return jnp.transpose(t, (1, 0, 3, 2, 4))


def _gate_chunks(t):
    b, s, h = t.shape
    return jnp.transpose(t.reshape(b, s // CHUNK, CHUNK, h), (1, 0, 3, 2))


def _from_chunks(t):
    nc, b, h, l, d = t.shape
    return jnp.transpose(t, (1, 0, 3, 2, 4)).reshape(b, nc * l, h * d)


def _mlstm_chunked(q, k, v, ig, lf):
    nc, b, h, l, dk = q.shape
    dv = v.shape[-1]
    tril = jnp.tril(jnp.ones((l, l), dtype=bool))

    def step(carry, inp):
        C, n, m = carry
        qc, kc, vc, igc, lfc = inp
        bcum = jnp.cumsum(lfc, axis=-1)
        D = bcum[..., :, None] - bcum[..., None, :] + igc[..., None, :]
        D = jnp.where(tril, D, -jnp.inf)
        m_inter = bcum + m[..., None]
        m_t = jnp.maximum(m_inter, jnp.max(D, axis=-1))
        Sc = jnp.einsum('bhtd,bhsd->bhts', qc, kc) * jnp.exp(D - m_t[..., None])
        inter = jnp.exp(m_inter - m_t)
        num = (jnp.einsum('bhts,bhse->bhte', Sc, vc)
               + inter[..., None] * jnp.einsum('bhtd,bhde->bhte', qc, C))
        den = jnp.sum(Sc, axis=-1) + inter * jnp.einsum('bhtd,bhd->bht', qc, n)
        hout = num / jnp.maximum(jnp.abs(den), jnp.exp(-m_t))[..., None]
        bL = bcum[..., -1]
        g = bL[..., None] - bcum + igc
        m_new = jnp.maximum(bL + m, jnp.max(g, axis=-1))
        a = jnp.exp(bL + m - m_new)
        w = jnp.exp(g - m_new[..., None])
        C = a[..., None, None] * C + jnp.einsum('bhs,bhsd,bhse->bhde', w, kc, vc)
        n = a[..., None] * n + jnp.einsum('bhs,bhsd->bhd', w, kc)
        return (C, n, m_new), hout

    init = (jnp.zeros((b, h, dk, dv), jnp.float32),
            jnp.zeros((b, h, dk), jnp.float32),
            jnp.full((b, h), NEG_INIT, jnp.float32))
    _, hs = lax.scan(step, init, (q, k, v, ig, lf))
    return hs


def _gla_chunked(q, k, v, gk):
    nc, b, h, l, dk = q.shape
    dv = v.shape[-1]
    tril = jnp.tril(jnp.ones((l, l), dtype=bool))

    def step(S, inp):
        qc, kc, vc, gc = inp
        Bc = jnp.cumsum(gc, axis=2)
        diff = Bc[:, :, :, None, :] - Bc[:, :, None, :, :]
        decay = jnp.exp(jnp.where(tril[..., None], diff, -jnp.inf))
        A = jnp.einsum('bhtd,bhsd,bhtsd->bhts', qc, kc, decay)
        o = (jnp.einsum('bhts,bhse->bhte', A, vc)
             + jnp.einsum('bhtd,bhde->bhte', qc * jnp.exp(Bc), S))
        BL = Bc[:, :, -1]
        kd = kc * jnp.exp(BL[:, :, None, :] - Bc)
        S = jnp.exp(BL)[..., None] * S + jnp.einsum('bhsd,bhse->bhde', kd, vc)
        return S, o

    init = jnp.zeros((b, h, dk, dv), jnp.float32)
    _, os_ = lax.scan(step, init, (q, k, v, gk))
    return os_


def setup_inputs(seed: int = 0) -> dict:
    key = jax.random.key(seed)
    ks = jax.random.split(key, 16)
    f32 = jnp.float32
    x = jax.random.normal(ks[0], (BATCH, SEQ, D_MODEL), f32)
    norm_w = 1.0 + 0.02 * jax.random.normal(ks[1], (D_MODEL,), f32)
    w_in = jax.random.normal(ks[2], (D_MODEL, D_IN), f32) * D_MODEL ** -0.5
    conv_w = jax.random.normal(ks[3], (CONV_W, 2 * H_M * DK_M), f32) * CONV_W ** -0.5
    conv_b = 0.02 * jax.random.normal(ks[4], (2 * H_M * DK_M,), f32)
    b_igate = 0.1 * jax.random.normal(ks[5], (H_M,), f32)
    b_fgate = jnp.linspace(3.0, 6.0, H_M, dtype=f32) + 0.01 * jax.random.normal(ks[6], (H_M,), f32)
    mlstm_norm_w = 1.0 + 0.02 * jax.random.normal(ks[7], (D_MIX_M,), f32)
    w_gk_up = jax.random.normal(ks[8], (GATE_RANK, H_G * DK_G), f32) * GATE_RANK ** -0.5
    b_gk = 0.1 * jax.random.normal(ks[9], (H_G * DK_G,), f32)
    gla_norm_w = 1.0 + 0.02 * jax.random.normal(ks[10], (D_MIX_G,), f32)
    w_out = jax.random.normal(ks[11], (D_MIX, D_MODEL), f32) * D_MIX ** -0.5
    final_norm_w = 1.0 + 0.02 * jax.random.normal(ks[12], (D_MODEL,), f32)
    return {"x": x, "norm_w": norm_w, "w_in": w_in, "conv_w": conv_w, "conv_b": conv_b,
            "b_igate": b_igate, "b_fgate": b_fgate, "mlstm_norm_w": mlstm_norm_w,
            "w_gk_up": w_gk_up, "b_gk": b_gk, "gla_norm_w": gla_norm_w,
            "w_out": w_out, "final_norm_w": final_norm_w}


def reference(x, norm_w, w_in, conv_w, conv_b, b_igate, b_fgate, mlstm_norm_w,
              w_gk_up, b_gk, gla_norm_w, w_out, final_norm_w):
    f32 = jnp.float32
    for _layer in range(DEPTH):
        hnorm = _rmsnorm(x, norm_w)
        proj = hnorm @ w_in
        (q_m, k_m, v_m, i_pre, f_pre, o_pre, z_m,
         q_g, k_g, v_g, gk_low, z_g) = _split_cols(proj, SPLIT_SIZES)

        qk_m = jax.nn.silu(_causal_conv(jnp.concatenate([q_m, k_m], axis=-1), conv_w, conv_b))
        q_m, k_m = qk_m[..., :H_M * DK_M], qk_m[..., H_M * DK_M:]
        qc = _to_chunks(q_m.astype(f32) * DK_M ** -0.5, H_M)
        kc = _to_chunks(k_m.astype(f32), H_M)
        vc = _to_chunks(v_m.astype(f32), H_M)
        ig = _gate_chunks((i_pre + b_igate).astype(f32))
        lf = _gate_chunks(jax.nn.log_sigmoid((f_pre + b_fgate).astype(f32)))
        h_m = _from_chunks(_mlstm_chunked(qc, kc, vc, ig, lf)).astype(x.dtype)
        h_m = _head_rmsnorm(h_m, mlstm_norm_w, H_M) * jax.nn.sigmoid(o_pre) * jax.nn.silu(z_m)

        gk = jax.nn.log_sigmoid((gk_low @ w_gk_up + b_gk).astype(f32)) / GATE_NORM
        qg = _to_chunks(q_g.astype(f32) * DK_G ** -0.5, H_G)
        kg = _to_chunks(k_g.astype(f32), H_G)
        vg = _to_chunks(v_g.astype(f32), H_G)
        gkc = _to_chunks(gk, H_G)
        h_g = _from_chunks(_gla_chunked(qg, kg, vg, gkc)).astype(x.dtype)
        h_g = _head_rmsnorm(h_g, gla_norm_w, H_G) * jax.nn.silu(z_g)

        x = x + jnp.concatenate([h_m, h_g], axis=-1) @ w_out
    return _rmsnorm(x, final_norm_w)
```

```python
import contextlib
import numpy as np
import ml_dtypes
import concourse.bass as bass
import concourse.mybir as mybir
from concourse.bass_utils import run_bass_kernel_spmd

F32 = mybir.dt.float32
BF16 = mybir.dt.bfloat16
AF = mybir.ActivationFunctionType
ALU = mybir.AluOpType
AX = mybir.AxisListType

D = 1024
EPS = 1e-6
NEG_INIT = -1e30
QSCALE = 128 ** -0.5
NCOL = 1810
FM_QM, FM_KM, FM_QG, FM_KG, FM_GKL = 0, 128, 256, 384, 512
TM0, TM1, TM2 = 528, 1040, 1552
SP_NORMW, SP_CONVW, SP_CONVB, SP_BGK, SP_BI, SP_BF = 0, 8, 16, 18, 19, 20
SP_MNW, SP_GNW, SP_FNW = 21, 277, 533
NSMALL = 789


class Sched:
    def __init__(self, nc, stack):
        self.nc = nc
        self.stack = stack
        self.eng = {"pe": nc.tensor, "act": nc.scalar, "dve": nc.vector,
                    "pool": nc.gpsimd, "sp": nc.sync}
        self.sem, self.cnt = {}, {}
        self.known = {e: {} for e in self.eng}
        self.bufs = {}
        self.ninst = 0
        self.prog = {e: [] for e in self.eng}
        self.cap = None
        self.marks = set()
        self.bank_of = {"xT_ps": 0, ("PB", 0): 1, ("PB", 1): 2, ("PB", 2): 3,
                        "urep": 4, "gc0": 6, "gc1": 6, "hTm_ps": 4, "hTg_ps": 4, "ATg": 4,
                        "ATm": 5, "num": 5, "kTm": 5, "dC": 6, "kTg": 6, "og": 7, "dS": 7}
        self.bank_last = {i: {} for i in range(8)}
        for e in self.eng:
            self._mk(e)

    def _mk(self, key):
        self.sem[key] = self.stack.enter_context(self.nc.semaphore("s_" + str(key)))
        self.cnt[key] = 0

    def _deps(self, eng, reads, writes, is_dma):
        deps = {}

        def add(k, v):
            if v > deps.get(k, 0):
                deps[k] = v
        for b in reads:
            st = self.bufs.get(b)
            if st and st["w"]:
                k, v = st["w"]
                if not (k == eng and eng == "pe" and not is_dma):
                    add(k, v)
        for b in writes:
            st = self.bufs.get(b)
            if st:
                if st["w"]:
                    k, v = st["w"]
                    if is_dma or k != eng or eng != "pe":
                        add(k, v)
                for k, v in st["r"].items():
                    if is_dma or k != eng or eng != "pe":
                        add(k, v)
        for b in list(reads) + list(writes):
            bank = self.bank_of.get(b)
            if bank is not None:
                for k, v in self.bank_last[bank].items():
                    if k != eng:
                        add(k, v)
        return deps

    def _wait(self, eng, deps):
        for k, v in deps.items():
            if self.known[eng].get(k, 0) >= v:
                continue
            self.prog[eng].append(("w", k, v))
            self.known[eng][k] = v

    def _record(self, key, val, reads, writes):
        for b in reads:
            st = self.bufs.setdefault(b, {"w": None, "r": {}})
            if st["r"].get(key, 0) < val:
                st["r"][key] = val
        for b in writes:
            self.bufs[b] = {"w": (key, val), "r": {}}

    def op(self, eng, fn, reads=(), writes=(), n=128):
        if self.cap is not None:
            self.cap.append(("op", eng, fn, tuple(reads), tuple(writes), n))
            return
        self.sim_commit(("op", eng, fn, tuple(reads), tuple(writes), n))
        self._wait(eng, self._deps(eng, reads, writes, False))
        self.cnt[eng] += 1
        self.prog[eng].append(("i", fn, eng, 1))
        self._record(eng, self.cnt[eng], reads, writes)
        for b in list(reads) + list(writes):
            bank = self.bank_of.get(b)
            if bank is not None:
                self.bank_last[bank][eng] = self.cnt[eng]
        self.ninst += 1

    def dma(self, q, key, fn, reads=(), writes=(), inc=16, n=128):
        if self.cap is not None:
            self.cap.append(("dma", q, key, fn, tuple(reads), tuple(writes), inc, n))
            return
        self.sim_commit(("dma", q, key, fn, tuple(reads), tuple(writes), inc, n))
        if key not in self.sem:
            self._mk(key)
        self._wait(q, self._deps(q, reads, writes, True))
        self.cnt[key] += inc
        self.prog[q].append(("i", fn, key, inc))
        self._record(key, self.cnt[key], reads, writes)
        self.ninst += 1

    def mark(self, key):
        if self.cap is not None:
            self.cap.append(("mark", key))

    def need(self, key):
        if self.cap is not None:
            self.cap.append(("need", key))

    def group_begin(self):
        if self.cap is not None:
            self._gstack = self.cap
            self.cap = []

    def group_end(self):
        if self.cap is not None:
            items = self.cap
            self.cap = self._gstack
            self.cap.append(("group", items))

    def capture(self, f, *a):
        assert self.cap is None
        self.cap = []
        f(*a)
        out, self.cap = self.cap, None
        return out

    def emit(self, it):
        if it[0] == "mark":
            self.marks.add(it[1])
        elif it[0] == "need":
            assert it[1] in self.marks, it
        elif it[0] == "group":
            for x in it[1]:
                self.emit(x)
        elif it[0] == "op":
            self.op(it[1], it[2], it[3], it[4], it[5])
        else:
            self.dma(it[1], it[2], it[3], it[4], it[5], it[6], it[7])

    FIX = {"pe": 0.06, "act": 0.2, "dve": 0.12, "pool": 0.3, "sp": 0.05}
    PER = {"pe": 0.00042, "act": 0.0009, "dve": 0.0012, "pool": 0.0008, "sp": 0.0}
    LAT = 0.45

    def _sim_init(self):
        if not hasattr(self, "ef"):
            self.ef = {e: 0.0 for e in self.eng}
            self.tw, self.trd = {}, {}
            self.bank_t = {i: {} for i in range(8)}

    def _first(self, it):
        while it[0] == "group":
            it = it[1][0]
        return it

    def est_start(self, it):
        self._sim_init()
        it = self._first(it)
        eng = it[1]
        reads, writes = (it[3], it[4]) if it[0] == "op" else (it[4], it[5])
        t = self.ef[eng]

        def rdy(tt_e):
            tt, e2 = tt_e
            return tt + (self.LAT if e2 != eng else 0.03)
        for b_ in reads:
            if b_ in self.tw:
                t = max(t, rdy(self.tw[b_]))
        for b_ in writes:
            if b_ in self.tw:
                t = max(t, rdy(self.tw[b_]))
            for e2, tt in self.trd.get(b_, {}).items():
                t = max(t, rdy((tt, e2)))
        for b_ in list(reads) + list(writes):
            bank = self.bank_of.get(b_)
            if bank is not None:
                for e2, tt in self.bank_t[bank].items():
                    if e2 != eng:
                        t = max(t, tt + self.LAT)
        return t

    def sim_commit(self, it):
        self._sim_init()
        eng = it[1]
        if it[0] == "op":
            reads, writes, n = it[3], it[4], it[5]
            start = self.est_start(it)
            end = start + self.FIX[eng] + self.PER[eng] * n
            self.ef[eng] = end
            who = eng
        else:
            reads, writes, n = it[4], it[5], it[7]
            start = self.est_start(it)
            self.ef[eng] = start + 0.06
            end = start + 2.0 + 0.002 * n
            who = "dma"
        for b_ in reads:
            d = self.trd.setdefault(b_, {})
            d[who] = max(d.get(who, 0.0), end)
        for b_ in writes:
            self.tw[b_] = (end, who)
            self.trd[b_] = {}
        for b_ in list(reads) + list(writes):
            bank = self.bank_of.get(b_)
            if bank is not None:
                self.bank_t[bank][eng] = end

    def interleave(self, streams, weights=None):
        pos = [0] * len(streams)
        marks = self.marks
        while True:
            best, bt = -1, 1e30
            for k, s in enumerate(streams):
                while pos[k] < len(s) and (s[pos[k]][0] == "mark" or
                                           (s[pos[k]][0] == "need" and s[pos[k]][1] in marks)):
                    if s[pos[k]][0] == "mark":
                        marks.add(s[pos[k]][1])
                    pos[k] += 1
                if pos[k] < len(s) and s[pos[k]][0] != "need":
                    t = self.est_start(s[pos[k]])
                    if t < bt:
                        best, bt = k, t
            if best < 0:
                assert all(pos[k] >= len(s) for k, s in enumerate(streams)), "interleave deadlock"
                return
            self.emit(streams[best][pos[best]])
            pos[best] += 1

    def wait_all(self, eng, keys):
        for k in keys:
            if self.cnt[k] > 0:
                self.prog[eng].append(("w", k, self.cnt[k]))

    def replay(self, eng, e):
        for it in self.prog[eng]:
            if it[0] == "w":
                e.wait_ge(self.sem[it[1]], it[2])
            else:
                inst = it[1](e)
                inst.then_inc(self.sem[it[2]], it[3])


def build_nc(S, NSEG, cc_inc=1):
    NT = S // 128
    NM = S // 512
    TS = NT // NSEG
    SEG = S // NSEG
    MPS = NM // NSEG
    assert NM * 512 == S and MPS * NSEG == NM

    nc = bass.Bass("TRN2", target_bir_lowering=False)
    x_d = nc.dram_tensor("x_b", [S, D], F32, kind="ExternalInput").ap()
    win_d = nc.dram_tensor("w_in_c", [D, NCOL], F32, kind="ExternalInput").ap()
    wup_d = nc.dram_tensor("wup_c", [16, 128], F32, kind="ExternalInput").ap()
    wout_d = nc.dram_tensor("wout_c", [2048, 256], F32, kind="ExternalInput").ap()
    xres_d = nc.dram_tensor("xres_c", [S, 256], F32, kind="ExternalInput").ap()
    small_d = nc.dram_tensor("small_c", [128, NSMALL], F32, kind="ExternalInput").ap()
    y_d = nc.dram_tensor("y_c", [S, 256], F32, kind="ExternalOutput").ap()
    hsend = [nc.dram_tensor(f"hsend{g}", [512, SEG], BF16) for g in range(NSEG)]
    hall = [nc.dram_tensor(f"hall{g}", [2048, SEG], BF16) for g in range(NSEG)]
    ssq_i = [nc.dram_tensor(f"ssqi{g}", [128, TS], F32) for g in range(NSEG)]
    ssq_o = [nc.dram_tensor(f"ssqo{g}", [128, TS], F32) for g in range(NSEG)]
    GROUPS = [[0, 1, 2, 3], [4, 5, 6, 7]]
    import os
    CCQOS = os.environ.get('KQOS', 'P2')
    CCQOS = None if CCQOS == 'none' else CCQOS

    with contextlib.ExitStack() as st:
        def sb(name, shape, dt):
            return st.enter_context(nc.sbuf_tensor(name, shape, dt))

        def ps(name, shape, dt):
            return st.enter_context(nc.psum_tensor(name, shape, dt))

        xT_ps = ps("xT_ps", [128, 8, 128], BF16)
        PB = [ps(f"PB{i}", [128, 512], F32) for i in range(3)]
        MB0 = ps("MB0", [128, 512], F32)
        MB1a = ps("MB1a", [128, 512], F32)
        MB2a = ps("MB2a", [128, 512], F32)
        MB3 = ps("MB3", [128, 512], F32)
        UREP, ATG = MB0[:, 0:128], MB0[:, 384:512]
        GC0, GC1 = MB2a[:, 260:262], MB2a[:, 262:264]
        HTPM = MB0[:, 128:256].bitcast(BF16).rearrange("p (b t) -> p b t", b=2)
        HTPG = MB0[:, 256:384].bitcast(BF16).rearrange("p (b t) -> p b t", b=2)
        ATM, NUM = MB1a[:, 0:128], MB1a[:, 128:386]
        KTM = MB1a[:, 448:512].bitcast(BF16)
        DC = MB2a[:, 0:258]
        KTG = MB2a[:, 448:512].bitcast(BF16)
        OG, DS = MB3[:, 0:256], MB3[:, 256:512]

        W = sb("W", [128, 8, NCOL], BF16)
        Wst = sb("Wst", [128, NCOL], F32)
        Wout = sb("Wout", [128, 16, 256], BF16)
        wup = sb("wup", [16, 128], BF16)
        small = sb("small", [128, NSMALL], F32)
        ident_b = sb("ident_b", [128, 128], BF16)
        ident_f = sb("ident_f", [128, 128], F32)
        tri_f = sb("tri_f", [128, 128], F32)
        ones_f = sb("ones_f", [128, 128], F32)
        iot = sb("iot", [128, 128], F32)
        nbgk = sb("nbgk", [128, 1], F32)
        nbf = sb("nbf", [128, 1], F32)
        nwh = sb("nwh", [128, 256], F32)
        EPS_AP = sb("eps_ap", [128, 1], F32)
        ONE_AP = sb("one_ap", [128, 1], F32)

        xt = [sb(f"xt{i}", [128, D], F32) for i in range(2)]
        junk = sb("junk", [128, D], BF16)
        xs = [sb(f"xs{i}", [128, D], BF16) for i in range(2)]
        xT = [sb(f"xT{i}", [128, 8, 512], BF16) for i in range(2)]
        st_small = sb("st_small", [128, 8], F32)
        qkraw = [sb(f"qkraw{s}", [128, 2, 515], F32) for s in range(2)]
        cv = [sb(f"cv{s}", [128, 2, 512], F32) for s in range(2)]
        q_bf = [sb(f"q_bf{s}", [128, 512], BF16) for s in range(2)]
        kh_bf = [sb(f"kh_bf{s}", [128, 512], BF16) for s in range(2)]
        qg_bf = [sb(f"qg_bf{s}", [128, 512], BF16) for s in range(2)]
        kg_bf = [sb(f"kg_bf{s}", [128, 512], BF16) for s in range(2)]
        eBLc = [sb(f"eBLc{s}", [128, 4], F32) for s in range(3)]
        gkl_bf = sb("gkl_bf", [16, 512], BF16)
        spg = sb("spg", [128, 512], F32)
        nbc = sb("nbc", [128, 512], F32)
        eB = sb("eB", [128, 512], F32)
        eNB = sb("eNB", [128, 512], F32)
        Vm = [[sb(f"Vm{s}_{i}", [128, 258], BF16) for i in range(4)] for s in range(2)]
        Vg = [[sb(f"Vg{s}_{i}", [128, 256], BF16) for i in range(4)] for s in range(2)]
        zbuf = [sb(f"zbuf{s}", [128, 4, 512], F32) for s in range(2)]
        obuf = [sb(f"obuf{s}", [128, 4, 256], F32) for s in range(2)]
        iftok = [[sb(f"iftok{s}_{i}", [128, 2], F32) for i in range(4)] for s in range(2)]
        gsc = [sb(f"gsc{i}", [128, 16], F32) for i in range(4)]
        igd = sb("igd", [128, 128], F32)
        rmat = sb("rmat", [128, 128], F32)
        wrep = [sb(f"wrep{i}", [128, 128], F32) for i in range(4)]
        oscg = sb("oscg", [128, 16], F32)
        numS = [sb(f"numS{i}", [128, 258], F32) for i in range(2)]
        ogS = [sb(f"ogS{i}", [128, 256], F32) for i in range(2)]
        mst = [sb(f"mst{i}", [128, 1], F32) for i in range(2)]
        Cst = sb("Cst", [128, 258], F32)
        Ch_bf = sb("Ch_bf", [128, 258], BF16)
        Ust = sb("Ust", [128, 256], F32)
        Sh_bf = sb("Sh_bf", [128, 256], BF16)
        ATm_bf = sb("ATm_bf", [128, 128], BF16)
        ATg_bf = sb("ATg_bf", [128, 128], BF16)
        km_tok = sb("km_tok", [128, 128], BF16)
        kg_tok = sb("kg_tok", [128, 128], BF16)
        htile = sb("htile", [128, 512], BF16)
        osc = sb("osc", [128, 16], F32)
        hTst = [sb(f"hTst{i}", [128, 4, SEG], BF16) for i in range(2)]
        hTall = sb("hTall", [128, 16, SEG], BF16)
        xr = [sb(f"xr{i}", [128, 256], F32) for i in range(TS)]
        ypre = [sb(f"ypre{i}", [128, TS, 256], F32) for i in range(2)]
        ssq = [sb(f"ssq{i}", [128, TS], F32) for i in range(2)]
        ssr = sb("ssr", [128, TS], F32)
        rfin = sb("rfin", [128, TS], F32)
        yo = [sb(f"yo{i}", [128, 256], F32) for i in range(TS)]

        S_ = Sched(nc, st)
        op, dma = S_.op, S_.dma

        def smallc(c0, n=1):
            return small[:, c0:c0 + n]

        dma("sp", "wld", lambda e: e.dma_start(out=small[:, :], in_=small_d[:, :]),
            writes=["small"])
        dma("pool", "wout_ld", lambda e: e.dma_start(out=Wout[:, :, :],
                                                 in_=wout_d.rearrange("(k p) n -> p k n", p=128)),
            writes=["Wout"])
        dma("pool", "wup_ld", lambda e: e.dma_start(out=wup[:, :], in_=wup_d[:, :]), writes=["wup"])
        op("pool", lambda e: e.iota(iot[:, :], [[1, 128]], base=0, channel_multiplier=-1,
                                    allow_small_or_imprecise_dtypes=True), writes=["iot"])
        op("dve", lambda e: e.tensor_single_scalar(tri_f[:, :], iot[:, :], 0.0, ALU.is_ge),
           reads=["iot"], writes=["tri_f"])
        op("dve", lambda e: e.tensor_single_scalar(ident_f[:, :], iot[:, :], 0.0, ALU.is_equal),
           reads=["iot"], writes=["ident_f"])
        op("dve", lambda e: e.tensor_copy(ident_b[:, :], ident_f[:, :]),
           reads=["ident_f"], writes=["ident_b"])
        op("pool", lambda e: e.memset(ones_f[:, :], 1.0), writes=["ones_f"])
        op("pool", lambda e: e.memset(qkraw[1][:, :, 512:515], 0.0), writes=[("qkraw", 1, 0), ("qkraw", 1, 1)])
        op("pool", lambda e: e.memset(Cst[:, :], 0.0), writes=["Cst"])
        op("pool", lambda e: e.memset(Ust[:, :], 0.0), writes=["Ust"])
        op("pool", lambda e: e.memset(mst[0][:, :], NEG_INIT), writes=[("mst", 0)])
        op("pool", lambda e: e.memset(eBLc[2][:, :], 1.0), writes=[("eBLc", 2)])
        op("pool", lambda e: e.memset(EPS_AP[:, :], EPS), writes=["eps_ap"])
        op("pool", lambda e: e.memset(ONE_AP[:, :], 1.0), writes=["one_ap"])
        for s_ in range(2):
            for i in range(4):
                op("pool", lambda e, s_=s_, i=i: e.memset(Vm[s_][i][:, 256:258], 1.0),
                   writes=[("Vm1", s_, i)])
        op("dve", lambda e: e.tensor_scalar(nbgk[:, :], smallc(SP_BGK), -1.0, None, ALU.mult),
           reads=["small"], writes=["nbgk"])
        op("dve", lambda e: e.tensor_scalar(nbf[:, :], smallc(SP_BF), -1.0, None, ALU.mult),
           reads=["small"], writes=["nbf"])
        op("dve", lambda e: e.tensor_scalar(nwh[:, :], smallc(SP_MNW, 256), 0.5, None, ALU.mult),
           reads=["small"], writes=["nwh"])
        win_v = win_d.rearrange("(k p) n -> k p n", p=128)
        for kt in range(8):
            dma("sp", "wst", lambda e, kt=kt: e.dma_start(out=Wst[:, :], in_=win_v[kt]),
                writes=["Wst"])
            op("dve", lambda e, kt=kt: e.tensor_scalar(
                W[:, kt, :], Wst[:, :], smallc(SP_NORMW + kt), None, ALU.mult),
               reads=["Wst", "small"], writes=[("W", kt)])
        Wall = [("W", kt) for kt in range(8)]
        for e_ in ("act", "dve", "pe", "pool"):
            S_._wait(e_, {"pool": S_.cnt["pool"], "dve": S_.cnt["dve"]})

        pb_rr = [0]

        def next_pb():
            b = pb_rr[0]
            pb_rr[0] = (b + 1) % 2
            return b

        def a_tile(m, i):
            slot = m % 2
            ti = 4 * m + i
            xs_ = ti % 2
            dma("sp", ("xld", xs_), lambda e: e.dma_start(
                out=xt[xs_][:, :], in_=x_d[ti * 128:(ti + 1) * 128, :]), writes=[("xt", xs_)], n=1024)
            op("act", lambda e: e.activation(
                junk[:, :], xt[xs_][:, :], AF.Square, accum_out=st_small[:, 0:1]),
               reads=[("xt", xs_)], writes=["junk", "junk2", "junk3", "sa0"], n=1024)
            op("act", lambda e: e.activation(
                st_small[:, 1:2], st_small[:, 0:1], AF.Ln, bias=EPS_AP[:, :], scale=1.0 / D),
               reads=["sa0", "eps_ap"], writes=["sa1"])
            op("act", lambda e: e.activation(
                st_small[:, 2:3], st_small[:, 1:2], AF.Exp, scale=-0.5),
               reads=["sa1"], writes=["sa2"])
            op("pool", lambda e: e.tensor_scalar(
                xs[xs_][:, :], xt[xs_][:, :], st_small[:, 2:3], 1.0, ALU.mult, ALU.mult),
               reads=[("xt", xs_), "sa2"], writes=[("xs", xs_)], n=1024)
            for kt in range(8):
                op("pe", lambda e, kt=kt: e.transpose(
                    xT_ps[:, kt, :], xs[xs_][:, kt * 128:(kt + 1) * 128], ident_b[:, :]),
                   reads=[("xs", xs_), "ident_b"], writes=["xT_ps"])
            op("act", lambda e: e.activation(
                xT[slot][:, :, i * 128:(i + 1) * 128], xT_ps[:, :, :], AF.Copy),
               reads=["xT_ps"], writes=[("xT", slot, i)], n=1024)

        def fm_group(m, col0, ncols):
            slot = m % 2
            b = next_pb()
            S_.group_begin()
            for kt in range(8):
                op("pe", lambda e, kt=kt, b=b: e.matmul(
                    PB[b][0:ncols, :], W[:, kt, col0:col0 + ncols], xT[slot][:, kt, :],
                    start=(kt == 0), stop=(kt == 7)),
                   reads=Wall + [("xT", slot, i) for i in range(4)], writes=[("PB", b)], n=512)
            S_.group_end()
            return b

        def macro_fm(m):
            s = m % 2
            b = fm_group(m, FM_GKL, 16)
            op("act", lambda e, b=b: e.activation(gkl_bf[:, :], PB[b][0:16, :], AF.Copy),
               reads=[("PB", b)], writes=["gkl_bf"])
            b = next_pb()
            op("pe", lambda e, b=b: e.matmul(PB[b][:, :], wup[:, :], gkl_bf[:, :],
                                             start=True, stop=True),
               reads=["wup", "gkl_bf"], writes=[("PB", b)], n=512)
            op("act", lambda e, b=b: e.activation(spg[:, :], PB[b][:, :], AF.Exp,
                                                  bias=nbgk[:, :], scale=-1.0),
               reads=[("PB", b), "nbgk"], writes=["spg"], n=512)
            op("act", lambda e: e.activation(spg[:, :], spg[:, :], AF.Ln, bias=ONE_AP[:, :], scale=1.0),
               reads=["spg", "one_ap"], writes=["spg"], n=512)
            for c in range(4):
                op("dve", lambda e, c=c: e.tensor_tensor_scan(
                    nbc[:, c * 128:(c + 1) * 128], ones_f[:, :], spg[:, c * 128:(c + 1) * 128],
                    0.0, ALU.mult, ALU.add),
                   reads=["spg", "ones_f"], writes=[("nbc", c)], n=256)
            nbcs = [("nbc", c) for c in range(4)]
            op("act", lambda e: e.activation(eB[:, :], nbc[:, :], AF.Exp, scale=-1.0 / 16.0),
               reads=nbcs, writes=["eB"], n=512)
            op("act", lambda e: e.activation(eNB[:, :], nbc[:, :], AF.Exp, scale=1.0 / 16.0),
               reads=nbcs, writes=["eNB"], n=512)
            op("pool", lambda e: e.tensor_copy(
                eBLc[m % 3][:, :], eB[:, :].rearrange("p (c t) -> p c t", t=128)[:, :, 127]),
               reads=["eB"], writes=[("eBLc", m % 3)])
            b = fm_group(m, FM_QG, 128)
            op("dve", lambda e, b=b: e.scalar_tensor_tensor(
                qg_bf[s][:, :], PB[b][:, :], QSCALE, eB[:, :], ALU.mult, ALU.mult),
               reads=[("PB", b), "eB"], writes=[("qg_bf", s)], n=512)
            b = fm_group(m, FM_KG, 128)
            op("dve", lambda e, b=b: e.tensor_tensor(kg_bf[s][:, :], PB[b][:, :], eNB[:, :], ALU.mult),
               reads=[("PB", b), "eNB"], writes=[("kg_bf", s)], n=512)
            b = fm_group(m, FM_QM, 128)
            op("act", lambda e, b=b: e.activation(qkraw[s][:, 0, 3:515], PB[b][:, :], AF.Copy),
               reads=[("PB", b)], writes=[("qkraw", s, 0)], n=512)
            b = fm_group(m, FM_KM, 128)
            op("act", lambda e, b=b: e.activation(qkraw[s][:, 1, 3:515], PB[b][:, :], AF.Copy),
               reads=[("PB", b)], writes=[("qkraw", s, 1)], n=512)
            op("pool", lambda e: e.tensor_copy(qkraw[s][:, :, 0:3], qkraw[1 - s][:, :, 512:515]),
               reads=[("qkraw", 1 - s, 0), ("qkraw", 1 - s, 1)], writes=[("qkraw_h", s)])
            for w_ in range(2):
                op("dve", lambda e, w_=w_: e.tensor_scalar(
                    cv[s][:, w_, :], qkraw[s][:, w_, 0:512], smallc(SP_CONVW + 4 * w_),
                    smallc(SP_CONVB + w_), ALU.mult, ALU.add),
                   reads=[("qkraw", s, w_), ("qkraw_h", s), "small"], writes=[("cv", s, w_)], n=512)
                for j in range(1, 4):
                    op("dve", lambda e, w_=w_, j=j: e.scalar_tensor_tensor(
                        cv[s][:, w_, :], qkraw[s][:, w_, j:j + 512], smallc(SP_CONVW + 4 * w_ + j),
                        cv[s][:, w_, :], ALU.mult, ALU.add),
                       reads=[("qkraw", s, w_), ("qkraw_h", s), "small", ("cv", s, w_)],
                       writes=[("cv", s, w_)], n=512)

        def tm_tile(m, i):
            s = m % 2
            tsl = slice(i * 128, (i + 1) * 128)
            for (c0, n) in ((TM0, 512), (TM1, 512), (TM2, 258)):
                b = next_pb()
                S_.group_begin()
                for kt in range(8):
                    op("pe", lambda e, kt=kt, b=b, c0=c0, n=n: e.matmul(
                        PB[b][:, 0:n], xT[s][:, kt, tsl], W[:, kt, c0:c0 + n],
                        start=(kt == 0), stop=(kt == 7)),
                       reads=Wall + [("xT", s, i)], writes=[("PB", b)], n=n)
                S_.group_end()
                if c0 == TM0:
                    op("dve", lambda e, b=b: e.tensor_copy(Vm[s][i][:, 0:256], PB[b][:, 0:256]),
                       reads=[("PB", b)], writes=[("Vm", s, i)], n=256)
                    op("act", lambda e, b=b: e.activation(obuf[s][:, i, :], PB[b][:, 256:512], AF.Copy),
                       reads=[("PB", b)], writes=[("obuf", s, i)], n=256)
                elif c0 == TM1:
                    op("act", lambda e, b=b: e.activation(zbuf[s][:, i, 0:256], PB[b][:, 0:256], AF.Copy),
                       reads=[("PB", b)], writes=[("zm", s, i)], n=256)
                    op("dve", lambda e, b=b: e.tensor_copy(Vg[s][i][:, :], PB[b][:, 256:512]),
                       reads=[("PB", b)], writes=[("Vg", s, i)], n=256)
                else:
                    op("act", lambda e, b=b: e.activation(zbuf[s][:, i, 256:512], PB[b][:, 0:256], AF.Copy),
                       reads=[("PB", b)], writes=[("zg", s, i)], n=256)
                    op("dve", lambda e, b=b: e.tensor_copy(iftok[s][i][:, :], PB[b][:, 256:258]),
                       reads=[("PB", b)], writes=[("iftok", s, i)])

        def act18(m):
            s = m % 2
            zs = [("zm", s, i) for i in range(4)] + [("zg", s, i) for i in range(4)]
            os_ = [("obuf", s, i) for i in range(4)]
            S_.group_begin()
            op("act", lambda e: e.activation(cv[s][:, :, :], cv[s][:, :, :], AF.Silu),
               reads=[("cv", s, 0), ("cv", s, 1)], writes=[("cv", s, 0), ("cv", s, 1)], n=2400)
            op("act", lambda e: e.activation(zbuf[s][:, :, :], zbuf[s][:, :, :], AF.Silu),
               reads=zs, writes=zs, n=2048)
            op("act", lambda e: e.activation(obuf[s][:, :, :], obuf[s][:, :, :], AF.Tanh, scale=0.5),
               reads=os_, writes=os_, n=2400)
            S_.group_end()
            op("pool", lambda e: e.tensor_scalar(q_bf[s][:, :], cv[s][:, 0, :], QSCALE, 1.0,
                                                 ALU.mult, ALU.mult),
               reads=[("cv", s, 0)], writes=[("q_bf", s)], n=512)
            for i in range(4):
                gprod(s, i)

        def gprod(s, i):
            op("pool", lambda e: e.tensor_tensor(zbuf[s][:, i, 0:256], zbuf[s][:, i, 0:256],
                                                 nwh[:, :], ALU.mult),
               reads=[("zm", s, i), "nwh"], writes=[("zm", s, i)], n=256)
            op("dve", lambda e: e.scalar_tensor_tensor(
                obuf[s][:, i, :], obuf[s][:, i, :], 1.0, zbuf[s][:, i, 0:256], ALU.add, ALU.mult),
               reads=[("obuf", s, i), ("zm", s, i)], writes=[("obuf", s, i)], n=256)
            op("pool", lambda e: e.tensor_tensor(zbuf[s][:, i, 256:512], zbuf[s][:, i, 256:512],
                                                 smallc(SP_GNW, 256), ALU.mult),
               reads=[("zg", s, i), "small"], writes=[("zg", s, i)], n=256)

        def stage_a_stream(m):
            for i in range(4):
                a_tile(m, i)

        def fmtm_stream(m):
            macro_fm(m)
            for i in range(4):
                tm_tile(m, i)
            act18(m)

        def cidx(m, i):
            return 4 * m + i

        def gates(m, i):
            s = m % 2
            cn = cidx(m, i)
            p = cn % 4
            G = gsc[p]
            mp, mn = mst[cn % 2], mst[(cn + 1) % 2]
            mpn, mnn = ("mst", cn % 2), ("mst", (cn + 1) % 2)
            if cn >= 4:
                S_.need(("M", cn - 4))
            op("act", lambda e: e.activation(G[:, 0:1], iftok[s][i][:, 1:2], AF.Exp,
                                             bias=nbf[:, :], scale=-1.0),
               reads=[("iftok", s, i), "nbf"], writes=[("g0", p)])
            op("act", lambda e: e.activation(G[:, 1:2], G[:, 0:1], AF.Ln, bias=ONE_AP[:, :], scale=1.0),
               reads=[("g0", p)], writes=[("g1", p)])
            op("dve", lambda e: e.tensor_scalar(G[:, 2:3], iftok[s][i][:, 0:1], smallc(SP_BI), None,
                                                ALU.add),
               reads=[("iftok", s, i), "small"], writes=[("g2", p)])
            op("pool", lambda e: e.tensor_scalar(igd[:, :], ident_f[:, :], G[:, 2:3], 1.0,
                                                 ALU.mult, ALU.mult),
               reads=["ident_f", ("g2", p)], writes=["igd"])
            op("dve", lambda e: e.scalar_tensor_tensor(rmat[:, :], tri_f[:, :], G[:, 1:2], igd[:, :],
                                                       ALU.mult, ALU.add),
               reads=["tri_f", ("g1", p), "igd"], writes=["rmat"])
            op("pe", lambda e: e.matmul(UREP, ones_f[:, :], rmat[:, :], start=True, stop=True),
               reads=["rmat", "ones_f"], writes=["urep"])
            op("pe", lambda e: e.matmul(GC0, tri_f[:, :], G[:, 1:3], start=True, stop=True),
               reads=["tri_f", ("g1", p), ("g2", p)], writes=["gc0"])
            op("pe", lambda e: e.matmul(GC1, ones_f[:, :], G[:, 1:3], start=True, stop=True),
               reads=["ones_f", ("g1", p), ("g2", p)], writes=["gc1"])
            op("dve", lambda e: e.tensor_reduce(G[:, 3:4], UREP, AX.X, ALU.max),
               reads=["urep"], writes=[("g3", p)])
            op("dve", lambda e: e.tensor_tensor(G[:, 4:5], G[:, 3:4], mp[:, :], ALU.max),
               reads=[("g3", p), mpn], writes=[("g4", p)])
            op("dve", lambda e: e.tensor_scalar(G[:, 5:6], G[:, 4:5], -1.0, None, ALU.mult),
               reads=[("g4", p)], writes=[("g5", p)])
            op("act", lambda e: e.activation(wrep[p][:, :], UREP, AF.Exp, bias=G[:, 5:6], scale=1.0),
               reads=["urep", ("g5", p)], writes=[("wrep", p)])
            op("act", lambda e: e.activation(G[:, 6:7], mp[:, :], AF.Exp, bias=G[:, 5:6], scale=1.0),
               reads=[mpn, ("g5", p)], writes=[("g6", p)])
            op("act", lambda e: e.activation(G[:, 7:8], MB2a[:, 260:261], AF.Exp, bias=G[:, 5:6], scale=1.0),
               reads=["gc0", ("g5", p)], writes=[("g7", p)])
            op("dve", lambda e: e.scalar_tensor_tensor(mn[:, :], MB2a[:, 262:263], -1.0, G[:, 4:5],
                                                       ALU.mult, ALU.add),
               reads=["gc1", ("g4", p)], writes=[mnn])
            S_.mark(("G", cn))

        def hsl(m, i):
            g = m // MPS
            return g % 2, (4 * m + i) * 128 - g * SEG

        def mlstm(m, i):
            s = m % 2
            p = cidx(m, i) % 4
            G = gsc[p]
            csl = slice(i * 128, (i + 1) * 128)
            hslot, tok0 = hsl(m, i)
            p2 = cidx(m, i) % 2
            S_.need(("G", cidx(m, i)))
            if cidx(m, i) >= 2:
                S_.need(("M", cidx(m, i) - 2))
            op("dve", lambda e: e.tensor_tensor(kh_bf[s][:, csl], cv[s][:, 1, csl], wrep[p][:, :], ALU.mult),
               reads=[("cv", s, 1), ("wrep", p)], writes=[("kh_bf", s, i)])
            op("pe", lambda e: e.matmul(ATM, kh_bf[s][:, csl], q_bf[s][:, csl], start=True, stop=True),
               reads=[("kh_bf", s, i), ("q_bf", s)], writes=["ATm"])
            op("pe", lambda e: e.transpose(KTM, kh_bf[s][:, csl], ident_b[:, :]),
               reads=[("kh_bf", s, i), "ident_b"], writes=["kTm"])
            op("dve", lambda e: e.tensor_tensor(ATm_bf[:, :], ATM, tri_f[:, :], ALU.mult),
               reads=["ATm", "tri_f"], writes=["ATm_bf"])
            op("act", lambda e: e.activation(km_tok[:, :], KTM, AF.Copy),
               reads=["kTm"], writes=["km_tok"])
            op("pool", lambda e: e.tensor_scalar(Ch_bf[:, :], Cst[:, :], G[:, 6:7], 1.0,
                                                 ALU.mult, ALU.mult),
               reads=["Cst", ("g6", p)], writes=["Ch_bf"], n=258)
            S_.group_begin()
            op("pe", lambda e: e.matmul(NUM, ATm_bf[:, :], Vm[s][i][:, :], start=True, stop=False),
               reads=["ATm_bf", ("Vm", s, i), ("Vm1", s, i)], writes=["num"], n=258)
            op("pe", lambda e: e.matmul(NUM, q_bf[s][:, csl], Ch_bf[:, :], start=False, stop=True),
               reads=[("q_bf", s), "Ch_bf"], writes=["num"], n=258)
            S_.group_end()
            op("act", lambda e: e.activation(numS[p2][:, :], NUM, AF.Copy),
               reads=["num"], writes=[("numS", p2)], n=258)
            op("pe", lambda e: e.matmul(DC, km_tok[:, :], Vm[s][i][:, :], start=True, stop=True),
               reads=["km_tok", ("Vm", s, i), ("Vm1", s, i)], writes=["dC"], n=258)
            op("dve", lambda e: e.scalar_tensor_tensor(Cst[:, :], Cst[:, :], G[:, 6:7], DC,
                                                       ALU.mult, ALU.add),
               reads=["Cst", ("g6", p), "dC", "Ch_bf"], writes=["Cst"], n=258)
            S_.mark(("A", cidx(m, i)))

        def mlstm_out(m, i):
            s = m % 2
            cn = cidx(m, i)
            p = cn % 4
            p2 = cn % 2
            G = gsc[p]
            hslot, tok0 = hsl(m, i)
            Gm_ = obuf[s][:, i, :]
            NS = numS[p2]
            nsn = ("numS", p2)
            S_.need(("A", cn))
            op("act", lambda e: e.activation(osc[:, 11:12], NS[:, 256:257], AF.Abs),
               reads=[nsn], writes=["o11"])
            op("dve", lambda e: e.tensor_tensor(osc[:, 0:1], osc[:, 11:12], G[:, 7:8], ALU.max),
               reads=["o11", ("g7", p)], writes=["o0"])
            op("dve", lambda e: e.reciprocal(osc[:, 1:2], osc[:, 0:1]), reads=["o0"], writes=["o1"])
            op("act", lambda e: e.activation(junk[:, 0:256], NS[:, 0:256], AF.Square,
                                             accum_out=osc[:, 2:3]),
               reads=[nsn], writes=["junk", "o2"], n=256)
            op("dve", lambda e: e.tensor_tensor(osc[:, 3:4], osc[:, 1:2], osc[:, 1:2], ALU.mult),
               reads=["o1"], writes=["o3"])
            op("dve", lambda e: e.tensor_tensor(osc[:, 4:5], osc[:, 2:3], osc[:, 3:4], ALU.mult),
               reads=["o2", "o3"], writes=["o4"])
            op("act", lambda e: e.activation(osc[:, 5:6], osc[:, 4:5], AF.Ln, bias=EPS_AP[:, :],
                                             scale=1.0 / 256.0),
               reads=["o4"], writes=["o5"])
            op("act", lambda e: e.activation(osc[:, 6:7], osc[:, 5:6], AF.Exp, scale=-0.5),
               reads=["o5"], writes=["o6"])
            op("dve", lambda e: e.tensor_tensor(osc[:, 7:8], osc[:, 1:2], osc[:, 6:7], ALU.mult),
               reads=["o1", "o6"], writes=["o7"])
            op("dve", lambda e: e.scalar_tensor_tensor(htile[:, 0:256], NS[:, 0:256], osc[:, 7:8],
                                                       Gm_, ALU.mult, ALU.mult),
               reads=[nsn, "o7", ("obuf", s, i)], writes=["htile_m"], n=256)
            S_.group_begin()
            for bk in range(2):
                op("pe", lambda e, bk=bk: e.transpose(HTPM[:, bk, :],
                                                      htile[:, bk * 128:(bk + 1) * 128], ident_b[:, :]),
                   reads=["htile_m", "ident_b"], writes=["hTm_ps"])
            S_.group_end()
            op("act", lambda e: e.activation(
                hTst[hslot][:, 0:2, tok0:tok0 + 128], HTPM[:, :, :], AF.Copy),
               reads=["hTm_ps"], writes=[("hTst", hslot, 0)], n=256)
            S_.mark(("M", cidx(m, i)))

        def gla(m, i):
            s = m % 2
            csl = slice(i * 128, (i + 1) * 128)
            hslot, tok0 = hsl(m, i)
            if i == 0:
                dec, decn = eBLc[(m + 2) % 3][:, 3:4], ("eBLc", (m + 2) % 3)
            else:
                dec, decn = eBLc[m % 3][:, i - 1:i], ("eBLc", m % 3)
            if cidx(m, i) >= 2:
                S_.need(("GM", cidx(m, i) - 2))
            op("pe", lambda e: e.matmul(ATG, kg_bf[s][:, csl], qg_bf[s][:, csl], start=True, stop=True),
               reads=[("kg_bf", s), ("qg_bf", s)], writes=["ATg"])
            op("pe", lambda e: e.transpose(KTG, kg_bf[s][:, csl], ident_b[:, :]),
               reads=[("kg_bf", s), "ident_b"], writes=["kTg"])
            op("dve", lambda e: e.tensor_tensor(ATg_bf[:, :], ATG, tri_f[:, :], ALU.mult),
               reads=["ATg", "tri_f"], writes=["ATg_bf"])
            op("act", lambda e: e.activation(kg_tok[:, :], KTG, AF.Copy),
               reads=["kTg"], writes=["kg_tok"])
            op("pool", lambda e: e.tensor_scalar(Sh_bf[:, :], Ust[:, :], dec, 1.0, ALU.mult, ALU.mult),
               reads=["Ust", decn], writes=["Sh_bf"], n=256)
            S_.group_begin()
            op("pe", lambda e: e.matmul(OG, ATg_bf[:, :], Vg[s][i][:, :], start=True, stop=False),
               reads=["ATg_bf", ("Vg", s, i)], writes=["og"], n=256)
            op("pe", lambda e: e.matmul(OG, qg_bf[s][:, csl], Sh_bf[:, :], start=False, stop=True),
               reads=[("qg_bf", s), "Sh_bf"], writes=["og"], n=256)
            S_.group_end()
            op("act", lambda e: e.activation(ogS[cidx(m, i) % 2][:, :], OG, AF.Copy),
               reads=["og"], writes=[("ogS", cidx(m, i) % 2)], n=256)
            op("pe", lambda e: e.matmul(DS, kg_tok[:, :], Vg[s][i][:, :], start=True, stop=True),
               reads=["kg_tok", ("Vg", s, i)], writes=["dS"], n=256)
            op("dve", lambda e: e.scalar_tensor_tensor(Ust[:, :], Ust[:, :], dec, DS, ALU.mult, ALU.add),
               reads=["Ust", decn, "dS", "Sh_bf"], writes=["Ust"], n=256)
            S_.mark(("GA", cidx(m, i)))

        def gla_out(m, i):
            s = m % 2
            cn = cidx(m, i)
            p2 = cn % 2
            hslot, tok0 = hsl(m, i)
            Gg_ = zbuf[s][:, i, 256:512]
            OS = ogS[p2]
            osn = ("ogS", p2)
            S_.need(("GA", cn))
            op("act", lambda e: e.activation(junk[:, 256:512], OS[:, :], AF.Square, accum_out=oscg[:, 0:1]),
               reads=[osn], writes=["junk2", "p0"], n=256)
            op("act", lambda e: e.activation(oscg[:, 1:2], oscg[:, 0:1], AF.Ln, bias=EPS_AP[:, :],
                                             scale=1.0 / 256.0),
               reads=["p0"], writes=["p1"])
            op("act", lambda e: e.activation(oscg[:, 2:3], oscg[:, 1:2], AF.Exp, scale=-0.5),
               reads=["p1"], writes=["p2"])
            op("dve", lambda e: e.scalar_tensor_tensor(htile[:, 256:512], OS[:, :], oscg[:, 2:3],
                                                       Gg_, ALU.mult, ALU.mult),
               reads=[osn, "p2", ("zg", s, i)], writes=["htile_g"], n=256)
            S_.group_begin()
            for bk in range(2):
                op("pe", lambda e, bk=bk: e.transpose(HTPG[:, bk, :],
                                                      htile[:, 256 + bk * 128:256 + (bk + 1) * 128],
                                                      ident_b[:, :]),
                   reads=["htile_g", "ident_b"], writes=["hTg_ps"])
            S_.group_end()
            op("act", lambda e: e.activation(
                hTst[hslot][:, 2:4, tok0:tok0 + 128], HTPG[:, :, :], AF.Copy),
               reads=["hTg_ps"], writes=[("hTst", hslot, 1)], n=256)
            S_.mark(("GM", cidx(m, i)))

        def gates_stream(m):
            for i in range(4):
                gates(m, i)

        def mlstm_stream(m):
            for i in range(4):
                mlstm(m, i)

        def gla_stream(m):
            for i in range(4):
                gla(m, i)

        def mlstm_out_stream(m):
            for i in range(4):
                mlstm_out(m, i)

        def gla_out_stream(m):
            for i in range(4):
                gla_out(m, i)

        def seg_send(g):
            hslot = g % 2
            dma("sp", ("hst", hslot), lambda e: e.dma_start(
                out=hsend[g].ap().rearrange("(b p) t -> p b t", p=128), in_=hTst[hslot][:, :, :]),
                reads=[("hTst", hslot, 0), ("hTst", hslot, 1)], writes=[("hsend", g)])
            dma("pool", "ccA%d" % (g % 2), lambda e: e.collective_compute(
                "AllGather", ALU.bypass, replica_groups=GROUPS,
                ins=[hsend[g].ap().opt()], outs=[hall[g].ap().opt()], dma_qos=CCQOS),
                reads=[("hsend", g)], writes=[("hall", g)], inc=cc_inc, n=15000)

        def outproj_loads(g):
            dma("sp", "hld", lambda e: e.dma_start(
                out=hTall[:, :, :], in_=hall[g].ap().rearrange("(k p) t -> p k t", p=128)),
                reads=[("hall", g)], writes=["hTall"])
            for tt in range(TS):
                xr_load(g, tt)

        def xr_load(g, tt):
            ti = g * TS + tt
            dma("sp", ("xrld", tt), lambda e: e.dma_start(
                out=xr[tt][:, :], in_=xres_d[ti * 128:(ti + 1) * 128, :]), writes=[("xr", tt)])

        def outproj_a(g):
            ys_ = g % 2
            for tt in range(TS):
                outproj_tile(g, tt)
            ssqs = [("ssq", ys_, tt) for tt in range(TS)]
            dma("sp", "sst", lambda e: e.dma_start(out=ssq_i[g].ap(), in_=ssq[ys_][:, :]),
                reads=ssqs, writes=[("ssq_i", g)])
            dma("pool", "ccR%d" % (g % 2), lambda e: e.collective_compute(
                "AllReduce", ALU.add, replica_groups=GROUPS,
                ins=[ssq_i[g].ap().opt()], outs=[ssq_o[g].ap().opt()], dma_qos=CCQOS),
                reads=[("ssq_i", g)], writes=[("ssq_o", g)], inc=cc_inc, n=12000)

        def outproj_tile(g, tt):
            ys_ = g % 2
            ti = g * TS + tt
            xs_ = tt
            for k in range(16):
                op("pe", lambda e, k=k: e.matmul(
                    PB[2][:, 0:256], hTall[:, k, tt * 128:(tt + 1) * 128], Wout[:, k, :],
                    start=(k == 0), stop=(k == 15)),
                   reads=["hTall", "Wout"], writes=[("PB", 2)], n=256)
            op("dve", lambda e: e.tensor_tensor(
                ypre[ys_][:, tt, :], PB[2][:, 0:256], xr[xs_][:, :], ALU.add),
               reads=[("PB", 2), ("xr", xs_)], writes=[("ypre", ys_, tt)], n=256)
            op("act", lambda e: e.activation(junk[:, 512:768], ypre[ys_][:, tt, :], AF.Square,
                                             accum_out=ssq[ys_][:, tt:tt + 1]),
               reads=[("ypre", ys_, tt)], writes=["junk3", ("ssq", ys_, tt)], n=256)

        def outproj_b(g):
            ys_ = g % 2
            dma("sp", "sld", lambda e: e.dma_start(out=ssr[:, :], in_=ssq_o[g].ap()),
                reads=[("ssq_o", g)], writes=["ssr"])
            op("act", lambda e: e.activation(rfin[:, :], ssr[:, :], AF.Ln, bias=EPS_AP[:, :], scale=1.0 / D),
               reads=["ssr"], writes=["rfin0"])
            op("act", lambda e: e.activation(rfin[:, :], rfin[:, :], AF.Exp, scale=-0.5),
               reads=["rfin0"], writes=["rfin"])
            for tt in range(TS):
                outproj_b_tile(g, tt)

        def outproj_b_tile(g, tt):
            ys_ = g % 2
            ti = g * TS + tt
            yb = tt
            op("dve", lambda e: e.scalar_tensor_tensor(
                yo[yb][:, :], ypre[ys_][:, tt, :], rfin[:, tt:tt + 1], smallc(SP_FNW, 256),
                ALU.mult, ALU.mult),
               reads=[("ypre", ys_, tt), "rfin", "small"], writes=[("yo", yb)], n=256)
            dma("sp", ("yst", yb), lambda e: e.dma_start(
                out=y_d[ti * 128:(ti + 1) * 128, :], in_=yo[yb][:, :]),
                reads=[("yo", yb)], writes=[("y", ti)])

        cap = S_.capture
        for it in cap(stage_a_stream, 0) + cap(fmtm_stream, 0):
            S_.emit(it)
        if NM > 1:
            for it in cap(stage_a_stream, 1):
                S_.emit(it)
        LAG_A, LAG_B = 2, 3
        for m in range(NM):
            g = m // MPS
            import os
            if os.environ.get("KSEQ") == "1":
                def seq_stream(m):
                    for i in range(4):
                        gates(m, i)
                        mlstm(m, i)
                        mlstm_out(m, i)
                        gla(m, i)
                        gla_out(m, i)
                streams = [cap(seq_stream, m)]
            else:
                streams = [cap(gates_stream, m), cap(mlstm_stream, m), cap(gla_stream, m),
                           cap(mlstm_out_stream, m), cap(gla_out_stream, m)]
            WG = float(os.environ.get("KWG", "0.5"))
            WP = float(os.environ.get("KWP", "1.0"))
            weights = [WG, 1.0, 1.0, 1.0, 1.0][:len(streams)] if len(streams) == 5 else [1.0]
            if m + 1 < NM:
                streams.append(cap(fmtm_stream, m + 1))
                weights.append(WP)
            last_of_seg = (m + 1) % MPS == 0
            if last_of_seg and g - LAG_A >= 0:
                streams.append(cap(outproj_a, g - LAG_A))
                weights.append(1.0)
            if last_of_seg and g - LAG_B >= 0:
                streams.append(cap(outproj_b, g - LAG_B))
                weights.append(1.0)
            if m + 2 < NM:
                streams.append(cap(stage_a_stream, m + 2))
                weights.append(1.0)
            S_.interleave(streams, weights)
            if last_of_seg:
                if 0 <= g + 1 - LAG_A and m + 1 < NM:
                    outproj_loads(g + 1 - LAG_A)
                seg_send(g)
        na, nb = max(0, NSEG - LAG_A), max(0, NSEG - LAG_B)
        while na < NSEG or nb < NSEG:
            if na < NSEG:
                outproj_loads(na)
                outproj_a(na)
                na += 1
            while nb < NSEG and (nb < na - 1 or na == NSEG):
                outproj_b(nb)
                nb += 1

        S_.wait_all("sp", list(S_.sem.keys()))
        with nc.Block() as block:
            block.tensor(lambda e: S_.replay("pe", e))
            block.vector(lambda e: S_.replay("dve", e))
            block.scalar(lambda e: S_.replay("act", e))
            block.gpsimd(lambda e: S_.replay("pool", e))
            block.sync(lambda e: S_.replay("sp", e))
        print("ninst", S_.ninst)
    return nc


def _prep_inputs(x, norm_w, w_in, conv_w, conv_b, b_igate, b_fgate, mlstm_norm_w,
                 w_gk_up, b_gk, gla_norm_w, w_out, final_norm_w):
    f = np.float32
    x = np.asarray(x, f)
    w_in = np.asarray(w_in, f)
    o = 0
    offs = {}
    for name, n in (("qm", 512), ("km", 512), ("vm", 1024), ("i", 4), ("f", 4), ("o", 1024),
                    ("zm", 1024), ("qg", 512), ("kg", 512), ("vg", 1024), ("gkl", 16), ("zg", 1024)):
        offs[name] = o
        o += n
    conv_w = np.asarray(conv_w, f)
    conv_b = np.asarray(conv_b, f)
    w_out = np.asarray(w_out, f)
    in_maps = []
    for c in range(8):
        b, j = c // 4, c % 4
        cols = np.concatenate([
            np.arange(offs["qm"] + 128 * j, offs["qm"] + 128 * (j + 1)),
            np.arange(offs["km"] + 128 * j, offs["km"] + 128 * (j + 1)),
            np.arange(offs["qg"] + 128 * j, offs["qg"] + 128 * (j + 1)),
            np.arange(offs["kg"] + 128 * j, offs["kg"] + 128 * (j + 1)),
            np.arange(offs["gkl"], offs["gkl"] + 16),
            np.arange(offs["vm"] + 256 * j, offs["vm"] + 256 * (j + 1)),
            np.arange(offs["o"] + 256 * j, offs["o"] + 256 * (j + 1)),
            np.arange(offs["zm"] + 256 * j, offs["zm"] + 256 * (j + 1)),
            np.arange(offs["vg"] + 256 * j, offs["vg"] + 256 * (j + 1)),
            np.arange(offs["zg"] + 256 * j, offs["zg"] + 256 * (j + 1)),
            np.array([offs["i"] + j, offs["f"] + j]),
        ])
        assert cols.size == NCOL
        w_in_c = np.ascontiguousarray(w_in[:, cols])
        wup_c = np.ascontiguousarray(np.asarray(w_gk_up, f)[:, 128 * j:128 * (j + 1)])
        rows = np.concatenate([np.concatenate([np.arange(256 * r, 256 * (r + 1)),
                                               np.arange(1024 + 256 * r, 1024 + 256 * (r + 1))])
                               for r in range(4)])
        wout_c = np.ascontiguousarray(w_out[rows][:, 256 * j:256 * (j + 1)])
        xres_c = np.ascontiguousarray(x[b][:, 256 * j:256 * (j + 1)])
        small = np.zeros((128, NSMALL), f)
        small[:, SP_NORMW:SP_NORMW + 8] = np.asarray(norm_w, f).reshape(8, 128).T
        small[:, SP_CONVW:SP_CONVW + 4] = conv_w[:, 128 * j:128 * (j + 1)].T
        small[:, SP_CONVW + 4:SP_CONVW + 8] = conv_w[:, 512 + 128 * j:512 + 128 * (j + 1)].T
        small[:, SP_CONVB] = conv_b[128 * j:128 * (j + 1)]
        small[:, SP_CONVB + 1] = conv_b[512 + 128 * j:512 + 128 * (j + 1)]
        small[:, SP_BGK] = np.asarray(b_gk, f)[128 * j:128 * (j + 1)]
        small[:, SP_BI] = np.asarray(b_igate, f)[j]
        small[:, SP_BF] = np.asarray(b_fgate, f)[j]
        small[:, SP_MNW:SP_MNW + 256] = np.asarray(mlstm_norm_w, f)[256 * j:256 * (j + 1)][None, :]
        small[:, SP_GNW:SP_GNW + 256] = np.asarray(gla_norm_w, f)[256 * j:256 * (j + 1)][None, :]
        small[:, SP_FNW:SP_FNW + 256] = np.asarray(final_norm_w, f)[256 * j:256 * (j + 1)][None, :]
        in_maps.append({"x_b": np.ascontiguousarray(x[b]), "w_in_c": w_in_c, "wup_c": wup_c,
                        "wout_c": wout_c, "xres_c": xres_c, "small_c": small})
    return in_maps


def run(inputs, NSEG=None, trace=False):
    x = np.asarray(inputs["x"])
    B, S, _ = x.shape
    assert B == 2
    if NSEG is None:
        NSEG = S // 512
    nc = build_nc(S, NSEG)
    in_maps = _prep_inputs(**inputs)
    res = run_bass_kernel_spmd(nc, in_maps, core_ids=list(range(8)), trace=trace)
    out = np.empty((B, S, D), np.float32)
    for c in range(8):
        b, j = c // 4, c % 4
        out[b, :, 256 * j:256 * (j + 1)] = np.asarray(res.results[c]["y_c"], np.float32)
    return out, res


def kernel(**inputs):
    out, _ = run(inputs)
    return out
```

```python
import contextlib
import numpy as np
import ml_dtypes
import concourse.bass as bass
import concourse.mybir as mybir
from concourse.bass_utils import run_bass_kernel_spmd

F32 = mybir.dt.float32
BF16 = mybir.dt.bfloat16
AF = mybir.ActivationFunctionType
ALU = mybir.AluOpType
AX = mybir.AxisListType

D = 1024
EPS = 1e-6
NEG_INIT = -1e30
QSCALE = 128 ** -0.5
NCOL = 1810
FM_QM, FM_KM, FM_QG, FM_KG, FM_GKL = 0, 128, 256, 384, 512
TM0, TM1, TM2 = 528, 1040, 1552
SP_NORMW, SP_CONVW, SP_CONVB, SP_BGK, SP_BI, SP_BF = 0, 8, 16, 18, 19, 20
SP_MNW, SP_GNW, SP_FNW = 21, 277, 533
SP_NWROW = 789
NSMALL = 789 + 1024


class Sched:
    def __init__(self, nc, stack):
        self.nc = nc
        self.stack = stack
        self.eng = {"pe": nc.tensor, "act": nc.scalar, "dve": nc.vector,
                    "pool": nc.gpsimd, "sp": nc.sync}
        self.sem, self.cnt = {}, {}
        self.known = {e: {} for e in self.eng}
        self.bufs = {}
        self.ninst = 0
        self.prog = {e: [] for e in self.eng}
        self.cap = None
        self.marks = set()
        self.bank_of = {"xT_ps": 0, ("PB", 0): 1, ("PB", 1): 2, ("PB", 2): 3,
                        "urep": 4, "gc0": 6, "gc1": 6, "hTm_ps": 4, "hTg_ps": 4, "ATg": 4,
                        "ATm": 5, "num": 5, "kTm": 5, "dC": 6, "kTg": 6, "og": 7, "dS": 7}
        self.bank_last = {i: {} for i in range(8)}
        for e in self.eng:
            self._mk(e)

    def _mk(self, key):
        self.sem[key] = self.stack.enter_context(self.nc.semaphore("s_" + str(key)))
        self.cnt[key] = 0

    def _deps(self, eng, reads, writes, is_dma):
        deps = {}

        def add(k, v):
            if v > deps.get(k, 0):
                deps[k] = v
        for b in reads:
            st = self.bufs.get(b)
            if st and st["w"]:
                k, v = st["w"]
                if not (k == eng and eng == "pe" and not is_dma):
                    add(k, v)
        for b in writes:
            st = self.bufs.get(b)
            if st:
                if st["w"]:
                    k, v = st["w"]
                    if is_dma or k != eng or eng != "pe":
                        add(k, v)
                for k, v in st["r"].items():
                    if is_dma or k != eng or eng != "pe":
                        add(k, v)
        for b in list(reads) + list(writes):
            bank = self.bank_of.get(b)
            if bank is not None:
                for k, v in self.bank_last[bank].items():
                    if k != eng:
                        add(k, v)
        return deps

    def _wait(self, eng, deps):
        for k, v in deps.items():
            if self.known[eng].get(k, 0) >= v:
                continue
            self.prog[eng].append(("w", k, v))
            self.known[eng][k] = v

    def _record(self, key, val, reads, writes):
        for b in reads:
            st = self.bufs.setdefault(b, {"w": None, "r": {}})
            if st["r"].get(key, 0) < val:
                st["r"][key] = val
        for b in writes:
            self.bufs[b] = {"w": (key, val), "r": {}}

    def op(self, eng, fn, reads=(), writes=(), n=128):
        if self.cap is not None:
            self.cap.append(("op", eng, fn, tuple(reads), tuple(writes), n))
            return
        self.sim_commit(("op", eng, fn, tuple(reads), tuple(writes), n))
        self._wait(eng, self._deps(eng, reads, writes, False))
        self.cnt[eng] += 1
        self.prog[eng].append(("i", fn, eng, 1))
        self._record(eng, self.cnt[eng], reads, writes)
        for b in list(reads) + list(writes):
            bank = self.bank_of.get(b)
            if bank is not None:
                self.bank_last[bank][eng] = self.cnt[eng]
        self.ninst += 1

    def dma(self, q, key, fn, reads=(), writes=(), inc=16, n=128):
        if self.cap is not None:
            self.cap.append(("dma", q, key, fn, tuple(reads), tuple(writes), inc, n))
            return
        self.sim_commit(("dma", q, key, fn, tuple(reads), tuple(writes), inc, n))
        if key not in self.sem:
            self._mk(key)
        self._wait(q, self._deps(q, reads, writes, True))
        self.cnt[key] += inc
        self.prog[q].append(("i", fn, key, inc))
        self._record(key, self.cnt[key], reads, writes)
        self.ninst += 1

    def mark(self, key):
        if self.cap is not None:
            self.cap.append(("mark", key))

    def need(self, key):
        if self.cap is not None:
            self.cap.append(("need", key))

    def group_begin(self):
        if self.cap is not None:
            self._gstack = self.cap
            self.cap = []

    def group_end(self):
        if self.cap is not None:
            items = self.cap
            self.cap = self._gstack
            self.cap.append(("group", items))

    def capture(self, f, *a):
        assert self.cap is None
        self.cap = []
        f(*a)
        out, self.cap = self.cap, None
        return out

    def emit(self, it):
        if it[0] == "mark":
            self.marks.add(it[1])
        elif it[0] == "need":
            assert it[1] in self.marks, it
        elif it[0] == "group":
            for x in it[1]:
                self.emit(x)
        elif it[0] == "op":
            self.op(it[1], it[2], it[3], it[4], it[5])
        else:
            self.dma(it[1], it[2], it[3], it[4], it[5], it[6], it[7])

    FIX = {"pe": 0.06, "act": 0.2, "dve": 0.12, "pool": 0.3, "sp": 0.05}
    PER = {"pe": 0.00042, "act": 0.0009, "dve": 0.0012, "pool": 0.0008, "sp": 0.0}
    LAT = 0.45

    def _sim_init(self):
        if not hasattr(self, "ef"):
            self.ef = {e: 0.0 for e in self.eng}
            self.tw, self.trd = {}, {}
            self.bank_t = {i: {} for i in range(8)}

    def _first(self, it):
        while it[0] == "group":
            it = it[1][0]
        return it

    def est_start(self, it):
        self._sim_init()
        it = self._first(it)
        eng = it[1]
        reads, writes = (it[3], it[4]) if it[0] == "op" else (it[4], it[5])
        t = self.ef[eng]

        def rdy(tt_e):
            tt, e2 = tt_e
            return tt + (self.LAT if e2 != eng else 0.03)
        for b_ in reads:
            if b_ in self.tw:
                t = max(t, rdy(self.tw[b_]))
        for b_ in writes:
            if b_ in self.tw:
                t = max(t, rdy(self.tw[b_]))
            for e2, tt in self.trd.get(b_, {}).items():
                t = max(t, rdy((tt, e2)))
        for b_ in list(reads) + list(writes):
            bank = self.bank_of.get(b_)
            if bank is not None:
                for e2, tt in self.bank_t[bank].items():
                    if e2 != eng:
                        t = max(t, tt + self.LAT)
        return t

    def sim_commit(self, it):
        self._sim_init()
        eng = it[1]
        if it[0] == "op":
            reads, writes, n = it[3], it[4], it[5]
            start = self.est_start(it)
            end = start + self.FIX[eng] + self.PER[eng] * n
            self.ef[eng] = end
            who = eng
        else:
            reads, writes, n = it[4], it[5], it[7]
            start = self.est_start(it)
            self.ef[eng] = start + 0.06
            end = start + 2.0 + 0.002 * n
            who = "dma"
        for b_ in reads:
            d = self.trd.setdefault(b_, {})
            d[who] = max(d.get(who, 0.0), end)
        for b_ in writes:
            self.tw[b_] = (end, who)
            self.trd[b_] = {}
        for b_ in list(reads) + list(writes):
            bank = self.bank_of.get(b_)
            if bank is not None:
                self.bank_t[bank][eng] = end

    def interleave(self, streams, weights=None):
        pos = [0] * len(streams)
        marks = self.marks
        while True:
            best, bt = -1, 1e30
            for k, s in enumerate(streams):
                while pos[k] < len(s) and (s[pos[k]][0] == "mark" or
                                           (s[pos[k]][0] == "need" and s[pos[k]][1] in marks)):
                    if s[pos[k]][0] == "mark":
                        marks.add(s[pos[k]][1])
                    pos[k] += 1
                if pos[k] < len(s) and s[pos[k]][0] != "need":
                    t = self.est_start(s[pos[k]])
                    if t < bt:
                        best, bt = k, t
            if best < 0:
                assert all(pos[k] >= len(s) for k, s in enumerate(streams)), "interleave deadlock"
                return
            self.emit(streams[best][pos[best]])
            pos[best] += 1

    def wait_all(self, eng, keys):
        for k in keys:
            if self.cnt[k] > 0:
                self.prog[eng].append(("w", k, self.cnt[k]))

    def replay(self, eng, e):
        for it in self.prog[eng]:
            if it[0] == "w":
                e.wait_ge(self.sem[it[1]], it[2])
            else:
                inst = it[1](e)
                inst.then_inc(self.sem[it[2]], it[3])


def build_nc(S, NSEG, cc_inc=1):
    NT = S // 128
    NM = S // 512
    TS = NT // NSEG
    SEG = S // NSEG
    MPS = NM // NSEG
    assert NM * 512 == S and MPS * NSEG == NM

    nc = bass.Bass("TRN2", target_bir_lowering=False)
    x_d = nc.dram_tensor("x_b", [S, D], F32, kind="ExternalInput").ap()
    win_d = nc.dram_tensor("w_in_c", [D, NCOL], F32, kind="ExternalInput").ap()
    wup_d = nc.dram_tensor("wup_c", [16, 128], F32, kind="ExternalInput").ap()
    wout_d = nc.dram_tensor("wout_c", [2048, 256], F32, kind="ExternalInput").ap()
    xres_d = nc.dram_tensor("xres_c", [S, 256], F32, kind="ExternalInput").ap()
    small_d = nc.dram_tensor("small_c", [128, NSMALL], F32, kind="ExternalInput").ap()
    y_d = nc.dram_tensor("y_c", [S, 256], F32, kind="ExternalOutput").ap()
    hsend = [nc.dram_tensor(f"hsend{g}", [512, SEG], BF16) for g in range(NSEG)]
    hall = [nc.dram_tensor(f"hall{g}", [2048, SEG], BF16) for g in range(NSEG)]
    ssq_i = [nc.dram_tensor(f"ssqi{g}", [128, TS], F32) for g in range(NSEG)]
    ssq_o = [nc.dram_tensor(f"ssqo{g}", [128, TS], F32) for g in range(NSEG)]
    GROUPS = [[0, 1, 2, 3], [4, 5, 6, 7]]
    import os
    CCQOS = os.environ.get('KQOS', 'P2')
    CCQOS = None if CCQOS == 'none' else CCQOS

    with contextlib.ExitStack() as st:
        def sb(name, shape, dt):
            return st.enter_context(nc.sbuf_tensor(name, shape, dt))

        def ps(name, shape, dt):
            return st.enter_context(nc.psum_tensor(name, shape, dt))

        xT_ps = ps("xT_ps", [128, 8, 128], BF16)
        PB = [ps(f"PB{i}", [128, 512], F32) for i in range(3)]
        MB0 = ps("MB0", [128, 512], F32)
        MB1a = ps("MB1a", [128, 512], F32)
        MB2a = ps("MB2a", [128, 512], F32)
        MB3 = ps("MB3", [128, 512], F32)
        UREP, ATG = MB0[:, 0:128], MB0[:, 384:512]
        GC0, GC1 = MB2a[:, 260:262], MB2a[:, 262:264]
        HTPM = MB0[:, 128:256].bitcast(BF16).rearrange("p (b t) -> p b t", b=2)
        HTPG = MB0[:, 256:384].bitcast(BF16).rearrange("p (b t) -> p b t", b=2)
        ATM, NUM = MB1a[:, 0:128], MB1a[:, 128:386]
        KTM = MB1a[:, 448:512].bitcast(BF16)
        DC = MB2a[:, 0:258]
        KTG = MB2a[:, 448:512].bitcast(BF16)
        OG, DS = MB3[:, 0:256], MB3[:, 256:512]

        W = sb("W", [128, 8, NCOL], BF16)
        Wout = sb("Wout", [128, 16, 256], BF16)
        wup = sb("wup", [16, 128], BF16)
        small = sb("small", [128, NSMALL], F32)
        ident_b = sb("ident_b", [128, 128], BF16)
        ident_f = sb("ident_f", [128, 128], F32)
        tri_f = sb("tri_f", [128, 128], F32)
        ones_f = sb("ones_f", [128, 128], F32)
        iot = sb("iot", [128, 128], F32)
        nbgk = sb("nbgk", [128, 1], F32)
        nbf = sb("nbf", [128, 1], F32)
        nwh = sb("nwh", [128, 256], F32)
        EPS_AP = sb("eps_ap", [128, 1], F32)
        ONE_AP = sb("one_ap", [128, 1], F32)

        xt = [sb(f"xt{i}", [128, D], F32) for i in range(2)]
        junk = sb("junk", [128, D], BF16)
        xs = [sb(f"xs{i}", [128, D], BF16) for i in range(2)]
        xT = [sb(f"xT{i}", [128, 8, 512], BF16) for i in range(2)]
        st_small = sb("st_small", [128, 8], F32)
        qkraw = [sb(f"qkraw{s}", [128, 2, 515], F32) for s in range(2)]
        cv = [sb(f"cv{s}", [128, 2, 512], F32) for s in range(2)]
        q_bf = [sb(f"q_bf{s}", [128, 512], BF16) for s in range(2)]
        kh_bf = [sb(f"kh_bf{s}", [128, 512], BF16) for s in range(2)]
        qg_bf = [sb(f"qg_bf{s}", [128, 512], BF16) for s in range(2)]
        kg_bf = [sb(f"kg_bf{s}", [128, 512], BF16) for s in range(2)]
        eBLc = [sb(f"eBLc{s}", [128, 4], F32) for s in range(3)]
        gkl_bf = sb("gkl_bf", [16, 512], BF16)
        spg = sb("spg", [128, 512], F32)
        nbc = sb("nbc", [128, 512], F32)
        eB = sb("eB", [128, 512], F32)
        eNB = sb("eNB", [128, 512], F32)
        Vm = [[sb(f"Vm{s}_{i}", [128, 258], BF16) for i in range(4)] for s in range(2)]
        Vg = [[sb(f"Vg{s}_{i}", [128, 256], BF16) for i in range(4)] for s in range(2)]
        zbuf = [sb(f"zbuf{s}", [128, 4, 512], F32) for s in range(2)]
        obuf = [sb(f"obuf{s}", [128, 4, 256], F32) for s in range(2)]
        iftok = [[sb(f"iftok{s}_{i}", [128, 2], F32) for i in range(4)] for s in range(2)]
        gsc = [sb(f"gsc{i}", [128, 16], F32) for i in range(4)]
        igd = sb("igd", [128, 128], F32)
        rmat = sb("rmat", [128, 128], F32)
        wrep = [sb(f"wrep{i}", [128, 128], F32) for i in range(4)]
        oscg = sb("oscg", [128, 16], F32)
        numS = [sb(f"numS{i}", [128, 258], F32) for i in range(2)]
        ogS = [sb(f"ogS{i}", [128, 256], F32) for i in range(2)]
        mst = [sb(f"mst{i}", [128, 1], F32) for i in range(2)]
        Cst = sb("Cst", [128, 258], F32)
        Ch_bf = sb("Ch_bf", [128, 258], BF16)
        Ust = sb("Ust", [128, 256], F32)
        Sh_bf = sb("Sh_bf", [128, 256], BF16)
        ATm_bf = sb("ATm_bf", [128, 128], BF16)
        ATg_bf = sb("ATg_bf", [128, 128], BF16)
        km_tok = sb("km_tok", [128, 128], BF16)
        kg_tok = sb("kg_tok", [128, 128], BF16)
        htile = sb("htile", [128, 512], BF16)
        osc = sb("osc", [128, 16], F32)
        hTst = [sb(f"hTst{i}", [128, 4, SEG], BF16) for i in range(2)]
        hTall = sb("hTall", [128, 16, SEG], BF16)
        xr = [sb(f"xr{i}", [128, 256], F32) for i in range(TS)]
        ypre = [sb(f"ypre{i}", [128, TS, 256], F32) for i in range(2)]
        ssq = [sb(f"ssq{i}", [128, TS], F32) for i in range(2)]
        ssr = sb("ssr", [128, TS], F32)
        rfin = sb("rfin", [128, TS], F32)
        yo = [sb(f"yo{i}", [128, 256], F32) for i in range(TS)]

        S_ = Sched(nc, st)
        op, dma = S_.op, S_.dma

        def smallc(c0, n=1):
            return small[:, c0:c0 + n]

        dma("sp", "wld", lambda e: e.dma_start(out=small[:, :], in_=small_d[:, :]),
            writes=["small"])
        dma("pool", "wout_ld", lambda e: e.dma_start(out=Wout[:, :, :],
                                                 in_=wout_d.rearrange("(k p) n -> p k n", p=128)),
            writes=["Wout"])
        dma("pool", "wup_ld", lambda e: e.dma_start(out=wup[:, :], in_=wup_d[:, :]), writes=["wup"])
        op("pool", lambda e: e.iota(iot[:, :], [[1, 128]], base=0, channel_multiplier=-1,
                                    allow_small_or_imprecise_dtypes=True), writes=["iot"])
        op("dve", lambda e: e.tensor_single_scalar(tri_f[:, :], iot[:, :], 0.0, ALU.is_ge),
           reads=["iot"], writes=["tri_f"])
        op("dve", lambda e: e.tensor_single_scalar(ident_f[:, :], iot[:, :], 0.0, ALU.is_equal),
           reads=["iot"], writes=["ident_f"])
        op("dve", lambda e: e.tensor_copy(ident_b[:, :], ident_f[:, :]),
           reads=["ident_f"], writes=["ident_b"])
        op("pool", lambda e: e.memset(ones_f[:, :], 1.0), writes=["ones_f"])
        op("pool", lambda e: e.memset(qkraw[1][:, :, 512:515], 0.0), writes=[("qkraw", 1, 0), ("qkraw", 1, 1)])
        op("pool", lambda e: e.memset(Cst[:, :], 0.0), writes=["Cst"])
        op("pool", lambda e: e.memset(Ust[:, :], 0.0), writes=["Ust"])
        op("pool", lambda e: e.memset(mst[0][:, :], NEG_INIT), writes=[("mst", 0)])
        op("pool", lambda e: e.memset(eBLc[2][:, :], 1.0), writes=[("eBLc", 2)])
        op("pool", lambda e: e.memset(EPS_AP[:, :], EPS), writes=["eps_ap"])
        op("pool", lambda e: e.memset(ONE_AP[:, :], 1.0), writes=["one_ap"])
        for s_ in range(2):
            for i in range(4):
                op("pool", lambda e, s_=s_, i=i: e.memset(Vm[s_][i][:, 256:258], 1.0),
                   writes=[("Vm1", s_, i)])
        op("dve", lambda e: e.tensor_scalar(nbgk[:, :], smallc(SP_BGK), -1.0, None, ALU.mult),
           reads=["small"], writes=["nbgk"])
        op("dve", lambda e: e.tensor_scalar(nbf[:, :], smallc(SP_BF), -1.0, None, ALU.mult),
           reads=["small"], writes=["nbf"])
        op("dve", lambda e: e.tensor_scalar(nwh[:, :], smallc(SP_MNW, 256), 0.5, None, ALU.mult),
           reads=["small"], writes=["nwh"])
        win_v = win_d.rearrange("(k p) n -> p k n", p=128)
        WG = [(0, 528), (528, 1040), (1040, 1552), (1552, NCOL)]
        for gi, (c0, c1) in enumerate(WG):
            dma("pool", "wg%d" % gi, lambda e, c0=c0, c1=c1: e.dma_start(
                out=W[:, :, c0:c1], in_=win_v[:, :, c0:c1]), writes=[("W", gi)], n=4000)
        Wall = [("W", gi) for gi in range(4)]
        for e_ in ("act", "dve", "pe", "pool"):
            S_._wait(e_, {"pool": S_.cnt["pool"], "dve": S_.cnt["dve"]})

        pb_rr = [0]

        def next_pb():
            b = pb_rr[0]
            pb_rr[0] = (b + 1) % 2
            return b

        def a_tile(m, i):
            slot = m % 2
            ti = 4 * m + i
            xs_ = ti % 2
            dma("sp", ("xld", xs_), lambda e: e.dma_start(
                out=xt[xs_][:, :], in_=x_d[ti * 128:(ti + 1) * 128, :]), writes=[("xt", xs_)], n=1024)
            op("act", lambda e: e.activation(
                junk[:, :], xt[xs_][:, :], AF.Square, accum_out=st_small[:, 0:1]),
               reads=[("xt", xs_)], writes=["junk", "junk2", "junk3", "sa0"], n=1024)
            op("act", lambda e: e.activation(
                st_small[:, 1:2], st_small[:, 0:1], AF.Ln, bias=EPS_AP[:, :], scale=1.0 / D),
               reads=["sa0", "eps_ap"], writes=["sa1"])
            op("act", lambda e: e.activation(
                st_small[:, 2:3], st_small[:, 1:2], AF.Exp, scale=-0.5),
               reads=["sa1"], writes=["sa2"])
            op("dve", lambda e: e.scalar_tensor_tensor(
                xs[xs_][:, :], xt[xs_][:, :], st_small[:, 2:3], smallc(SP_NWROW, 1024),
                ALU.mult, ALU.mult),
               reads=[("xt", xs_), "sa2", "small"], writes=[("xs", xs_)], n=1024)
            for kt in range(8):
                op("pe", lambda e, kt=kt: e.transpose(
                    xT_ps[:, kt, :], xs[xs_][:, kt * 128:(kt + 1) * 128], ident_b[:, :]),
                   reads=[("xs", xs_), "ident_b"], writes=["xT_ps"])
            op("act", lambda e: e.activation(
                xT[slot][:, :, i * 128:(i + 1) * 128], xT_ps[:, :, :], AF.Copy),
               reads=["xT_ps"], writes=[("xT", slot, i)], n=1024)

        def fm_group(m, col0, ncols):
            slot = m % 2
            b = next_pb()
            S_.group_begin()
            for kt in range(8):
                op("pe", lambda e, kt=kt, b=b: e.matmul(
                    PB[b][0:ncols, :], W[:, kt, col0:col0 + ncols], xT[slot][:, kt, :],
                    start=(kt == 0), stop=(kt == 7)),
                   reads=[("W", 0)] + [("xT", slot, i) for i in range(4)], writes=[("PB", b)], n=512)
            S_.group_end()
            return b

        def macro_fm(m):
            s = m % 2
            b = fm_group(m, FM_GKL, 16)
            op("act", lambda e, b=b: e.activation(gkl_bf[:, :], PB[b][0:16, :], AF.Copy),
               reads=[("PB", b)], writes=["gkl_bf"])
            b = next_pb()
            op("pe", lambda e, b=b: e.matmul(PB[b][:, :], wup[:, :], gkl_bf[:, :],
                                             start=True, stop=True),
               reads=["wup", "gkl_bf"], writes=[("PB", b)], n=512)
            op("act", lambda e, b=b: e.activation(spg[:, :], PB[b][:, :], AF.Exp,
                                                  bias=nbgk[:, :], scale=-1.0),
               reads=[("PB", b), "nbgk"], writes=["spg"], n=512)
            op("act", lambda e: e.activation(spg[:, :], spg[:, :], AF.Ln, bias=ONE_AP[:, :], scale=1.0),
               reads=["spg", "one_ap"], writes=["spg"], n=512)
            for c in range(4):
                op("dve", lambda e, c=c: e.tensor_tensor_scan(
                    nbc[:, c * 128:(c + 1) * 128], ones_f[:, :], spg[:, c * 128:(c + 1) * 128],
                    0.0, ALU.mult, ALU.add),
                   reads=["spg", "ones_f"], writes=[("nbc", c)], n=256)
            nbcs = [("nbc", c) for c in range(4)]
            op("act", lambda e: e.activation(eB[:, :], nbc[:, :], AF.Exp, scale=-1.0 / 16.0),
               reads=nbcs, writes=["eB"], n=512)
            op("act", lambda e: e.activation(eNB[:, :], nbc[:, :], AF.Exp, scale=1.0 / 16.0),
               reads=nbcs, writes=["eNB"], n=512)
            op("pool", lambda e: e.tensor_copy(
                eBLc[m % 3][:, :], eB[:, :].rearrange("p (c t) -> p c t", t=128)[:, :, 127]),
               reads=["eB"], writes=[("eBLc", m % 3)])
            b = fm_group(m, FM_QG, 128)
            op("dve", lambda e, b=b: e.scalar_tensor_tensor(
                qg_bf[s][:, :], PB[b][:, :], QSCALE, eB[:, :], ALU.mult, ALU.mult),
               reads=[("PB", b), "eB"], writes=[("qg_bf", s)], n=512)
            b = fm_group(m, FM_KG, 128)
            op("dve", lambda e, b=b: e.tensor_tensor(kg_bf[s][:, :], PB[b][:, :], eNB[:, :], ALU.mult),
               reads=[("PB", b), "eNB"], writes=[("kg_bf", s)], n=512)
            b = fm_group(m, FM_QM, 128)
            op("act", lambda e, b=b: e.activation(qkraw[s][:, 0, 3:515], PB[b][:, :], AF.Copy),
               reads=[("PB", b)], writes=[("qkraw", s, 0)], n=512)
            b = fm_group(m, FM_KM, 128)
            op("act", lambda e, b=b: e.activation(qkraw[s][:, 1, 3:515], PB[b][:, :], AF.Copy),
               reads=[("PB", b)], writes=[("qkraw", s, 1)], n=512)
            op("pool", lambda e: e.tensor_copy(qkraw[s][:, :, 0:3], qkraw[1 - s][:, :, 512:515]),
               reads=[("qkraw", 1 - s, 0), ("qkraw", 1 - s, 1)], writes=[("qkraw_h", s)])
            for w_ in range(2):
                op("dve", lambda e, w_=w_: e.tensor_scalar(
                    cv[s][:, w_, :], qkraw[s][:, w_, 0:512], smallc(SP_CONVW + 4 * w_),
                    smallc(SP_CONVB + w_), ALU.mult, ALU.add),
                   reads=[("qkraw", s, w_), ("qkraw_h", s), "small"], writes=[("cv", s, w_)], n=512)
                for j in range(1, 4):
                    op("dve", lambda e, w_=w_, j=j: e.scalar_tensor_tensor(
                        cv[s][:, w_, :], qkraw[s][:, w_, j:j + 512], smallc(SP_CONVW + 4 * w_ + j),
                        cv[s][:, w_, :], ALU.mult, ALU.add),
                       reads=[("qkraw", s, w_), ("qkraw_h", s), "small", ("cv", s, w_)],
                       writes=[("cv", s, w_)], n=512)

        def tm_tile(m, i):
            s = m % 2
            tsl = slice(i * 128, (i + 1) * 128)
            for (c0, n) in ((TM0, 512), (TM1, 512), (TM2, 258)):
                b = next_pb()
                S_.group_begin()
                for kt in range(8):
                    op("pe", lambda e, kt=kt, b=b, c0=c0, n=n: e.matmul(
                        PB[b][:, 0:n], xT[s][:, kt, tsl], W[:, kt, c0:c0 + n],
                        start=(kt == 0), stop=(kt == 7)),
                       reads=[("W", {TM0: 1, TM1: 2, TM2: 3}[c0]), ("xT", s, i)], writes=[("PB", b)], n=n)
                S_.group_end()
                if c0 == TM0:
                    op("dve", lambda e, b=b: e.tensor_copy(Vm[s][i][:, 0:256], PB[b][:, 0:256]),
                       reads=[("PB", b)], writes=[("Vm", s, i)], n=256)
                    op("act", lambda e, b=b: e.activation(obuf[s][:, i, :], PB[b][:, 256:512], AF.Copy),
                       reads=[("PB", b)], writes=[("obuf", s, i)], n=256)
                elif c0 == TM1:
                    op("act", lambda e, b=b: e.activation(zbuf[s][:, i, 0:256], PB[b][:, 0:256], AF.Copy),
                       reads=[("PB", b)], writes=[("zm", s, i)], n=256)
                    op("dve", lambda e, b=b: e.tensor_copy(Vg[s][i][:, :], PB[b][:, 256:512]),
                       reads=[("PB", b)], writes=[("Vg", s, i)], n=256)
                else:
                    op("act", lambda e, b=b: e.activation(zbuf[s][:, i, 256:512], PB[b][:, 0:256], AF.Copy),
                       reads=[("PB", b)], writes=[("zg", s, i)], n=256)
                    op("dve", lambda e, b=b: e.tensor_copy(iftok[s][i][:, :], PB[b][:, 256:258]),
                       reads=[("PB", b)], writes=[("iftok", s, i)])

        def act18(m):
            s = m % 2
            zs = [("zm", s, i) for i in range(4)] + [("zg", s, i) for i in range(4)]
            os_ = [("obuf", s, i) for i in range(4)]
            S_.group_begin()
            op("act", lambda e: e.activation(cv[s][:, :, :], cv[s][:, :, :], AF.Silu),
               reads=[("cv", s, 0), ("cv", s, 1)], writes=[("cv", s, 0), ("cv", s, 1)], n=2400)
            op("act", lambda e: e.activation(zbuf[s][:, :, :], zbuf[s][:, :, :], AF.Silu),
               reads=zs, writes=zs, n=2048)
            op("act", lambda e: e.activation(obuf[s][:, :, :], obuf[s][:, :, :], AF.Tanh, scale=0.5),
               reads=os_, writes=os_, n=2400)
            S_.group_end()
            op("pool", lambda e: e.tensor_scalar(q_bf[s][:, :], cv[s][:, 0, :], QSCALE, 1.0,
                                                 ALU.mult, ALU.mult),
               reads=[("cv", s, 0)], writes=[("q_bf", s)], n=512)
            for i in range(4):
                gprod(s, i)

        def gprod(s, i):
            op("pool", lambda e: e.tensor_tensor(zbuf[s][:, i, 0:256], zbuf[s][:, i, 0:256],
                                                 nwh[:, :], ALU.mult),
               reads=[("zm", s, i), "nwh"], writes=[("zm", s, i)], n=256)
            op("dve", lambda e: e.scalar_tensor_tensor(
                obuf[s][:, i, :], obuf[s][:, i, :], 1.0, zbuf[s][:, i, 0:256], ALU.add, ALU.mult),
               reads=[("obuf", s, i), ("zm", s, i)], writes=[("obuf", s, i)], n=256)
            op("pool", lambda e: e.tensor_tensor(zbuf[s][:, i, 256:512], zbuf[s][:, i, 256:512],
                                                 smallc(SP_GNW, 256), ALU.mult),
               reads=[("zg", s, i), "small"], writes=[("zg", s, i)], n=256)

        def stage_a_stream(m):
            for i in range(4):
                a_tile(m, i)

        def fmtm_stream(m):
            macro_fm(m)
            for i in range(4):
                tm_tile(m, i)
            act18(m)

        def cidx(m, i):
            return 4 * m + i

        def gates(m, i):
            s = m % 2
            cn = cidx(m, i)
            p = cn % 4
            G = gsc[p]
            mp, mn = mst[cn % 2], mst[(cn + 1) % 2]
            mpn, mnn = ("mst", cn % 2), ("mst", (cn + 1) % 2)
            if cn >= 4:
                S_.need(("M", cn - 4))
            op("act", lambda e: e.activation(G[:, 0:1], iftok[s][i][:, 1:2], AF.Exp,
                                             bias=nbf[:, :], scale=-1.0),
               reads=[("iftok", s, i), "nbf"], writes=[("g0", p)])
            op("act", lambda e: e.activation(G[:, 1:2], G[:, 0:1], AF.Ln, bias=ONE_AP[:, :], scale=1.0),
               reads=[("g0", p)], writes=[("g1", p)])
            op("dve", lambda e: e.tensor_scalar(G[:, 2:3], iftok[s][i][:, 0:1], smallc(SP_BI), None,
                                                ALU.add),
               reads=[("iftok", s, i), "small"], writes=[("g2", p)])
            op("pool", lambda e: e.tensor_scalar(igd[:, :], ident_f[:, :], G[:, 2:3], 1.0,
                                                 ALU.mult, ALU.mult),
               reads=["ident_f", ("g2", p)], writes=["igd"])
            op("dve", lambda e: e.scalar_tensor_tensor(rmat[:, :], tri_f[:, :], G[:, 1:2], igd[:, :],
                                                       ALU.mult, ALU.add),
               reads=["tri_f", ("g1", p), "igd"], writes=["rmat"])
            op("pe", lambda e: e.matmul(UREP, ones_f[:, :], rmat[:, :], start=True, stop=True),
               reads=["rmat", "ones_f"], writes=["urep"])
            op("pe", lambda e: e.matmul(GC0, tri_f[:, :], G[:, 1:3], start=True, stop=True),
               reads=["tri_f", ("g1", p), ("g2", p)], writes=["gc0"])
            op("pe", lambda e: e.matmul(GC1, ones_f[:, :], G[:, 1:3], start=True, stop=True),
               reads=["ones_f", ("g1", p), ("g2", p)], writes=["gc1"])
            op("dve", lambda e: e.tensor_reduce(G[:, 3:4], UREP, AX.X, ALU.max),
               reads=["urep"], writes=[("g3", p)])
            op("dve", lambda e: e.tensor_tensor(G[:, 4:5], G[:, 3:4], mp[:, :], ALU.max),
               reads=[("g3", p), mpn], writes=[("g4", p)])
            op("dve", lambda e: e.tensor_scalar(G[:, 5:6], G[:, 4:5], -1.0, None, ALU.mult),
               reads=[("g4", p)], writes=[("g5", p)])
            op("act", lambda e: e.activation(wrep[p][:, :], UREP, AF.Exp, bias=G[:, 5:6], scale=1.0),
               reads=["urep", ("g5", p)], writes=[("wrep", p)])
            op("act", lambda e: e.activation(G[:, 6:7], mp[:, :], AF.Exp, bias=G[:, 5:6], scale=1.0),
               reads=[mpn, ("g5", p)], writes=[("g6", p)])
            op("act", lambda e: e.activation(G[:, 7:8], MB2a[:, 260:261], AF.Exp, bias=G[:, 5:6], scale=1.0),
               reads=["gc0", ("g5", p)], writes=[("g7", p)])
            op("dve", lambda e: e.scalar_tensor_tensor(mn[:, :], MB2a[:, 262:263], -1.0, G[:, 4:5],
                                                       ALU.mult, ALU.add),
               reads=["gc1", ("g4", p)], writes=[mnn])
            S_.mark(("G", cn))

        def hsl(m, i):
            g = m // MPS
            return g % 2, (4 * m + i) * 128 - g * SEG

        def mlstm(m, i):
            s = m % 2
            p = cidx(m, i) % 4
            G = gsc[p]
            csl = slice(i * 128, (i + 1) * 128)
            hslot, tok0 = hsl(m, i)
            p2 = cidx(m, i) % 2
            S_.need(("G", cidx(m, i)))
            if cidx(m, i) >= 2:
                S_.need(("M", cidx(m, i) - 2))
            op("dve", lambda e: e.tensor_tensor(kh_bf[s][:, csl], cv[s][:, 1, csl], wrep[p][:, :], ALU.mult),
               reads=[("cv", s, 1), ("wrep", p)], writes=[("kh_bf", s, i)])
            op("pe", lambda e: e.matmul(ATM, kh_bf[s][:, csl], q_bf[s][:, csl], start=True, stop=True),
               reads=[("kh_bf", s, i), ("q_bf", s)], writes=["ATm"])
            op("pe", lambda e: e.transpose(KTM, kh_bf[s][:, csl], ident_b[:, :]),
               reads=[("kh_bf", s, i), "ident_b"], writes=["kTm"])
            op("dve", lambda e: e.tensor_tensor(ATm_bf[:, :], ATM, tri_f[:, :], ALU.mult),
               reads=["ATm", "tri_f"], writes=["ATm_bf"])
            op("act", lambda e: e.activation(km_tok[:, :], KTM, AF.Copy),
               reads=["kTm"], writes=["km_tok"])
            op("pool", lambda e: e.tensor_scalar(Ch_bf[:, :], Cst[:, :], G[:, 6:7], 1.0,
                                                 ALU.mult, ALU.mult),
               reads=["Cst", ("g6", p)], writes=["Ch_bf"], n=258)
            S_.group_begin()
            op("pe", lambda e: e.matmul(NUM, ATm_bf[:, :], Vm[s][i][:, :], start=True, stop=False),
               reads=["ATm_bf", ("Vm", s, i), ("Vm1", s, i)], writes=["num"], n=258)
            op("pe", lambda e: e.matmul(NUM, q_bf[s][:, csl], Ch_bf[:, :], start=False, stop=True),
               reads=[("q_bf", s), "Ch_bf"], writes=["num"], n=258)
            S_.group_end()
            op("act", lambda e: e.activation(numS[p2][:, :], NUM, AF.Copy),
               reads=["num"], writes=[("numS", p2)], n=258)
            op("pe", lambda e: e.matmul(DC, km_tok[:, :], Vm[s][i][:, :], start=True, stop=True),
               reads=["km_tok", ("Vm", s, i), ("Vm1", s, i)], writes=["dC"], n=258)
            op("dve", lambda e: e.scalar_tensor_tensor(Cst[:, :], Cst[:, :], G[:, 6:7], DC,
                                                       ALU.mult, ALU.add),
               reads=["Cst", ("g6", p), "dC", "Ch_bf"], writes=["Cst"], n=258)
            S_.mark(("A", cidx(m, i)))

        def mlstm_out(m, i):
            s = m % 2
            cn = cidx(m, i)
            p = cn % 4
            p2 = cn % 2
            G = gsc[p]
            hslot, tok0 = hsl(m, i)
            Gm_ = obuf[s][:, i, :]
            NS = numS[p2]
            nsn = ("numS", p2)
            S_.need(("A", cn))
            op("act", lambda e: e.activation(osc[:, 11:12], NS[:, 256:257], AF.Abs),
               reads=[nsn], writes=["o11"])
            op("dve", lambda e: e.tensor_tensor(osc[:, 0:1], osc[:, 11:12], G[:, 7:8], ALU.max),
               reads=["o11", ("g7", p)], writes=["o0"])
            op("dve", lambda e: e.reciprocal(osc[:, 1:2], osc[:, 0:1]), reads=["o0"], writes=["o1"])
            op("act", lambda e: e.activation(junk[:, 0:256], NS[:, 0:256], AF.Square,
                                             accum_out=osc[:, 2:3]),
               reads=[nsn], writes=["junk", "o2"], n=256)
            op("dve", lambda e: e.tensor_tensor(osc[:, 3:4], osc[:, 1:2], osc[:, 1:2], ALU.mult),
               reads=["o1"], writes=["o3"])
            op("dve", lambda e: e.tensor_tensor(osc[:, 4:5], osc[:, 2:3], osc[:, 3:4], ALU.mult),
               reads=["o2", "o3"], writes=["o4"])
            op("act", lambda e: e.activation(osc[:, 5:6], osc[:, 4:5], AF.Ln, bias=EPS_AP[:, :],
                                             scale=1.0 / 256.0),
               reads=["o4"], writes=["o5"])
            op("act", lambda e: e.activation(osc[:, 6:7], osc[:, 5:6], AF.Exp, scale=-0.5),
               reads=["o5"], writes=["o6"])
            op("dve", lambda e: e.tensor_tensor(osc[:, 7:8], osc[:, 1:2], osc[:, 6:7], ALU.mult),
               reads=["o1", "o6"], writes=["o7"])
            op("dve", lambda e: e.scalar_tensor_tensor(htile[:, 0:256], NS[:, 0:256], osc[:, 7:8],
                                                       Gm_, ALU.mult, ALU.mult),
               reads=[nsn, "o7", ("obuf", s, i)], writes=["htile_m"], n=256)
            S_.group_begin()
            for bk in range(2):
                op("pe", lambda e, bk=bk: e.transpose(HTPM[:, bk, :],
                                                      htile[:, bk * 128:(bk + 1) * 128], ident_b[:, :]),
                   reads=["htile_m", "ident_b"], writes=["hTm_ps"])
            S_.group_end()
            op("act", lambda e: e.activation(
                hTst[hslot][:, 0:2, tok0:tok0 + 128], HTPM[:, :, :], AF.Copy),
               reads=["hTm_ps"], writes=[("hTst", hslot, 0)], n=256)
            S_.mark(("M", cidx(m, i)))

        def gla(m, i):
            s = m % 2
            csl = slice(i * 128, (i + 1) * 128)
            hslot, tok0 = hsl(m, i)
            if i == 0:
                dec, decn = eBLc[(m + 2) % 3][:, 3:4], ("eBLc", (m + 2) % 3)
            else:
                dec, decn = eBLc[m % 3][:, i - 1:i], ("eBLc", m % 3)
            if cidx(m, i) >= 2:
                S_.need(("GM", cidx(m, i) - 2))
            op("pe", lambda e: e.matmul(ATG, kg_bf[s][:, csl], qg_bf[s][:, csl], start=True, stop=True),
               reads=[("kg_bf", s), ("qg_bf", s)], writes=["ATg"])
            op("pe", lambda e: e.transpose(KTG, kg_bf[s][:, csl], ident_b[:, :]),
               reads=[("kg_bf", s), "ident_b"], writes=["kTg"])
            op("dve", lambda e: e.tensor_tensor(ATg_bf[:, :], ATG, tri_f[:, :], ALU.mult),
               reads=["ATg", "tri_f"], writes=["ATg_bf"])
            op("act", lambda e: e.activation(kg_tok[:, :], KTG, AF.Copy),
               reads=["kTg"], writes=["kg_tok"])
            op("pool", lambda e: e.tensor_scalar(Sh_bf[:, :], Ust[:, :], dec, 1.0, ALU.mult, ALU.mult),
               reads=["Ust", decn], writes=["Sh_bf"], n=256)
            S_.group_begin()
            op("pe", lambda e: e.matmul(OG, ATg_bf[:, :], Vg[s][i][:, :], start=True, stop=False),
               reads=["ATg_bf", ("Vg", s, i)], writes=["og"], n=256)
            op("pe", lambda e: e.matmul(OG, qg_bf[s][:, csl], Sh_bf[:, :], start=False, stop=True),
               reads=[("qg_bf", s), "Sh_bf"], writes=["og"], n=256)
            S_.group_end()
            op("act", lambda e: e.activation(ogS[cidx(m, i) % 2][:, :], OG, AF.Copy),
               reads=["og"], writes=[("ogS", cidx(m, i) % 2)], n=256)
            op("pe", lambda e: e.matmul(DS, kg_tok[:, :], Vg[s][i][:, :], start=True, stop=True),
               reads=["kg_tok", ("Vg", s, i)], writes=["dS"], n=256)
            op("dve", lambda e: e.scalar_tensor_tensor(Ust[:, :], Ust[:, :], dec, DS, ALU.mult, ALU.add),
               reads=["Ust", decn, "dS", "Sh_bf"], writes=["Ust"], n=256)
            S_.mark(("GA", cidx(m, i)))

        def gla_out(m, i):
            s = m % 2
            cn = cidx(m, i)
            p2 = cn % 2
            hslot, tok0 = hsl(m, i)
            Gg_ = zbuf[s][:, i, 256:512]
            OS = ogS[p2]
            osn = ("ogS", p2)
            S_.need(("GA", cn))
            op("act", lambda e: e.activation(junk[:, 256:512], OS[:, :], AF.Square, accum_out=oscg[:, 0:1]),
               reads=[osn], writes=["junk2", "p0"], n=256)
            op("act", lambda e: e.activation(oscg[:, 1:2], oscg[:, 0:1], AF.Ln, bias=EPS_AP[:, :],
                                             scale=1.0 / 256.0),
               reads=["p0"], writes=["p1"])
            op("act", lambda e: e.activation(oscg[:, 2:3], oscg[:, 1:2], AF.Exp, scale=-0.5),
               reads=["p1"], writes=["p2"])
            op("dve", lambda e: e.scalar_tensor_tensor(htile[:, 256:512], OS[:, :], oscg[:, 2:3],
                                                       Gg_, ALU.mult, ALU.mult),
               reads=[osn, "p2", ("zg", s, i)], writes=["htile_g"], n=256)
            S_.group_begin()
            for bk in range(2):
                op("pe", lambda e, bk=bk: e.transpose(HTPG[:, bk, :],
                                                      htile[:, 256 + bk * 128:256 + (bk + 1) * 128],
                                                      ident_b[:, :]),
                   reads=["htile_g", "ident_b"], writes=["hTg_ps"])
            S_.group_end()
            op("act", lambda e: e.activation(
                hTst[hslot][:, 2:4, tok0:tok0 + 128], HTPG[:, :, :], AF.Copy),
               reads=["hTg_ps"], writes=[("hTst", hslot, 1)], n=256)
            S_.mark(("GM", cidx(m, i)))

        def gates_stream(m):
            for i in range(4):
                gates(m, i)

        def mlstm_stream(m):
            for i in range(4):
                mlstm(m, i)

        def gla_stream(m):
            for i in range(4):
                gla(m, i)

        def mlstm_out_stream(m):
            for i in range(4):
                mlstm_out(m, i)

        def gla_out_stream(m):
            for i in range(4):
                gla_out(m, i)

        def seg_send(g):
            hslot = g % 2
            dma("sp", ("hst", hslot), lambda e: e.dma_start(
                out=hsend[g].ap().rearrange("(b p) t -> p b t", p=128), in_=hTst[hslot][:, :, :]),
                reads=[("hTst", hslot, 0), ("hTst", hslot, 1)], writes=[("hsend", g)])
            dma("pool", "ccA%d" % (g % 2), lambda e: e.collective_compute(
                "AllGather", ALU.bypass, replica_groups=GROUPS,
                ins=[hsend[g].ap().opt()], outs=[hall[g].ap().opt()], dma_qos=CCQOS),
                reads=[("hsend", g)], writes=[("hall", g)], inc=cc_inc, n=15000)

        def outproj_loads(g):
            dma("sp", "hld", lambda e: e.dma_start(
                out=hTall[:, :, :], in_=hall[g].ap().rearrange("(k p) t -> p k t", p=128)),
                reads=[("hall", g)], writes=["hTall"])
            for tt in range(TS):
                xr_load(g, tt)

        def xr_load(g, tt):
            ti = g * TS + tt
            dma("sp", ("xrld", tt), lambda e: e.dma_start(
                out=xr[tt][:, :], in_=xres_d[ti * 128:(ti + 1) * 128, :]), writes=[("xr", tt)])

        def outproj_a(g):
            ys_ = g % 2
            for tt in range(TS):
                outproj_tile(g, tt)
            ssqs = [("ssq", ys_, tt) for tt in range(TS)]
            dma("sp", "sst", lambda e: e.dma_start(out=ssq_i[g].ap(), in_=ssq[ys_][:, :]),
                reads=ssqs, writes=[("ssq_i", g)])
            dma("pool", "ccR%d" % (g % 2), lambda e: e.collective_compute(
                "AllReduce", ALU.add, replica_groups=GROUPS,
                ins=[ssq_i[g].ap().opt()], outs=[ssq_o[g].ap().opt()], dma_qos=CCQOS),
                reads=[("ssq_i", g)], writes=[("ssq_o", g)], inc=cc_inc, n=12000)

        def outproj_tile(g, tt):
            ys_ = g % 2
            ti = g * TS + tt
            xs_ = tt
            for k in range(16):
                op("pe", lambda e, k=k: e.matmul(
                    PB[2][:, 0:256], hTall[:, k, tt * 128:(tt + 1) * 128], Wout[:, k, :],
                    start=(k == 0), stop=(k == 15)),
                   reads=["hTall", "Wout"], writes=[("PB", 2)], n=256)
            op("dve", lambda e: e.tensor_tensor(
                ypre[ys_][:, tt, :], PB[2][:, 0:256], xr[xs_][:, :], ALU.add),
               reads=[("PB", 2), ("xr", xs_)], writes=[("ypre", ys_, tt)], n=256)
            op("act", lambda e: e.activation(junk[:, 512:768], ypre[ys_][:, tt, :], AF.Square,
                                             accum_out=ssq[ys_][:, tt:tt + 1]),
               reads=[("ypre", ys_, tt)], writes=["junk3", ("ssq", ys_, tt)], n=256)

        def outproj_b(g):
            ys_ = g % 2
            dma("sp", "sld", lambda e: e.dma_start(out=ssr[:, :], in_=ssq_o[g].ap()),
                reads=[("ssq_o", g)], writes=["ssr"])
            op("act", lambda e: e.activation(rfin[:, :], ssr[:, :], AF.Ln, bias=EPS_AP[:, :], scale=1.0 / D),
               reads=["ssr"], writes=["rfin0"])
            op("act", lambda e: e.activation(rfin[:, :], rfin[:, :], AF.Exp, scale=-0.5),
               reads=["rfin0"], writes=["rfin"])
            for tt in range(TS):
                outproj_b_tile(g, tt)

        def outproj_b_tile(g, tt):
            ys_ = g % 2
            ti = g * TS + tt
            yb = tt
            op("dve", lambda e: e.scalar_tensor_tensor(
                yo[yb][:, :], ypre[ys_][:, tt, :], rfin[:, tt:tt + 1], smallc(SP_FNW, 256),
                ALU.mult, ALU.mult),
               reads=[("ypre", ys_, tt), "rfin", "small"], writes=[("yo", yb)], n=256)
            dma("sp", ("yst", yb), lambda e: e.dma_start(
                out=y_d[ti * 128:(ti + 1) * 128, :], in_=yo[yb][:, :]),
                reads=[("yo", yb)], writes=[("y", ti)])

        cap = S_.capture
        for it in cap(stage_a_stream, 0) + cap(fmtm_stream, 0):
            S_.emit(it)
        if NM > 1:
            for it in cap(stage_a_stream, 1):
                S_.emit(it)
        LAG_A, LAG_B = 2, 3
        for m in range(NM):
            g = m // MPS
            import os
            if os.environ.get("KSEQ") == "1":
                def seq_stream(m):
                    for i in range(4):
                        gates(m, i)
                        mlstm(m, i)
                        mlstm_out(m, i)
                        gla(m, i)
                        gla_out(m, i)
                streams = [cap(seq_stream, m)]
            else:
                streams = [cap(gates_stream, m), cap(mlstm_stream, m), cap(gla_stream, m),
                           cap(mlstm_out_stream, m), cap(gla_out_stream, m)]
            WG = float(os.environ.get("KWG", "0.5"))
            WP = float(os.environ.get("KWP", "1.0"))
            weights = [WG, 1.0, 1.0, 1.0, 1.0][:len(streams)] if len(streams) == 5 else [1.0]
            if m + 1 < NM:
                streams.append(cap(fmtm_stream, m + 1))
                weights.append(WP)
            last_of_seg = (m + 1) % MPS == 0
            if last_of_seg and g - LAG_A >= 0:
                streams.append(cap(outproj_a, g - LAG_A))
                weights.append(1.0)
            if last_of_seg and g - LAG_B >= 0:
                streams.append(cap(outproj_b, g - LAG_B))
                weights.append(1.0)
            if m + 2 < NM:
                streams.append(cap(stage_a_stream, m + 2))
                weights.append(1.0)
            S_.interleave(streams, weights)
            if last_of_seg:
                if 0 <= g + 1 - LAG_A and m + 1 < NM:
                    outproj_loads(g + 1 - LAG_A)
                seg_send(g)
        na, nb = max(0, NSEG - LAG_A), max(0, NSEG - LAG_B)
        while na < NSEG or nb < NSEG:
            if na < NSEG:
                outproj_loads(na)
                outproj_a(na)
                na += 1
            while nb < NSEG and (nb < na - 1 or na == NSEG):
                outproj_b(nb)
                nb += 1

        S_.wait_all("sp", list(S_.sem.keys()))
        with nc.Block() as block:
            block.tensor(lambda e: S_.replay("pe", e))
            block.vector(lambda e: S_.replay("dve", e))
            block.scalar(lambda e: S_.replay("act", e))
            block.gpsimd(lambda e: S_.replay("pool", e))
            block.sync(lambda e: S_.replay("sp", e))
        print("ninst", S_.ninst)
    return nc


def _prep_inputs(x, norm_w, w_in, conv_w, conv_b, b_igate, b_fgate, mlstm_norm_w,
                 w_gk_up, b_gk, gla_norm_w, w_out, final_norm_w):
    f = np.float32
    x = np.asarray(x, f)
    w_in = np.asarray(w_in, f)
    o = 0
    offs = {}
    for name, n in (("qm", 512), ("km", 512), ("vm", 1024), ("i", 4), ("f", 4), ("o", 1024),
                    ("zm", 1024), ("qg", 512), ("kg", 512), ("vg", 1024), ("gkl", 16), ("zg", 1024)):
        offs[name] = o
        o += n
    conv_w = np.asarray(conv_w, f)
    conv_b = np.asarray(conv_b, f)
    w_out = np.asarray(w_out, f)
    in_maps = []
    for c in range(8):
        b, j = c // 4, c % 4
        cols = np.concatenate([
            np.arange(offs["qm"] + 128 * j, offs["qm"] + 128 * (j + 1)),
            np.arange(offs["km"] + 128 * j, offs["km"] + 128 * (j + 1)),
            np.arange(offs["qg"] + 128 * j, offs["qg"] + 128 * (j + 1)),
            np.arange(offs["kg"] + 128 * j, offs["kg"] + 128 * (j + 1)),
            np.arange(offs["gkl"], offs["gkl"] + 16),
            np.arange(offs["vm"] + 256 * j, offs["vm"] + 256 * (j + 1)),
            np.arange(offs["o"] + 256 * j, offs["o"] + 256 * (j + 1)),
            np.arange(offs["zm"] + 256 * j, offs["zm"] + 256 * (j + 1)),
            np.arange(offs["vg"] + 256 * j, offs["vg"] + 256 * (j + 1)),
            np.arange(offs["zg"] + 256 * j, offs["zg"] + 256 * (j + 1)),
            np.array([offs["i"] + j, offs["f"] + j]),
        ])
        assert cols.size == NCOL
        w_in_c = np.ascontiguousarray(w_in[:, cols])
        wup_c = np.ascontiguousarray(np.asarray(w_gk_up, f)[:, 128 * j:128 * (j + 1)])
        rows = np.concatenate([np.concatenate([np.arange(256 * r, 256 * (r + 1)),
                                               np.arange(1024 + 256 * r, 1024 + 256 * (r + 1))])
                               for r in range(4)])
        wout_c = np.ascontiguousarray(w_out[rows][:, 256 * j:256 * (j + 1)])
        xres_c = np.ascontiguousarray(x[b][:, 256 * j:256 * (j + 1)])
        small = np.zeros((128, NSMALL), f)
        small[:, SP_NORMW:SP_NORMW + 8] = np.asarray(norm_w, f).reshape(8, 128).T
        small[:, SP_CONVW:SP_CONVW + 4] = conv_w[:, 128 * j:128 * (j + 1)].T
        small[:, SP_CONVW + 4:SP_CONVW + 8] = conv_w[:, 512 + 128 * j:512 + 128 * (j + 1)].T
        small[:, SP_CONVB] = conv_b[128 * j:128 * (j + 1)]
        small[:, SP_CONVB + 1] = conv_b[512 + 128 * j:512 + 128 * (j + 1)]
        small[:, SP_BGK] = np.asarray(b_gk, f)[128 * j:128 * (j + 1)]
        small[:, SP_BI] = np.asarray(b_igate, f)[j]
        small[:, SP_BF] = np.asarray(b_fgate, f)[j]
        small[:, SP_MNW:SP_MNW + 256] = np.asarray(mlstm_norm_w, f)[256 * j:256 * (j + 1)][None, :]
        small[:, SP_GNW:SP_GNW + 256] = np.asarray(gla_norm_w, f)[256 * j:256 * (j + 1)][None, :]
        small[:, SP_FNW:SP_FNW + 256] = np.asarray(final_norm_w, f)[256 * j:256 * (j + 1)][None, :]
        small[:, SP_NWROW:SP_NWROW + 1024] = np.asarray(norm_w, f)[None, :]
        in_maps.append({"x_b": np.ascontiguousarray(x[b]), "w_in_c": w_in_c, "wup_c": wup_c,
                        "wout_c": wout_c, "xres_c": xres_c, "small_c": small})
    return in_maps


def run(inputs, NSEG=None, trace=False):
    x = np.asarray(inputs["x"])
    B, S, _ = x.shape
    assert B == 2
    if NSEG is None:
        NSEG = S // 512
    nc = build_nc(S, NSEG)
    in_maps = _prep_inputs(**inputs)
    res = run_bass_kernel_spmd(nc, in_maps, core_ids=list(range(8)), trace=trace)
    out = np.empty((B, S, D), np.float32)
    for c in range(8):
        b, j = c // 4, c % 4
        out[b, :, 256 * j:256 * (j + 1)] = np.asarray(res.results[c]["y_c"], np.float32)
    return out, res


def kernel(**inputs):
    out, _ = run(inputs)
    return out
```

```python
import contextlib
import numpy as np
import ml_dtypes
import concourse.bass as bass
import concourse.mybir as mybir
from concourse.bass_utils import run_bass_kernel_spmd

F32 = mybir.dt.float32
BF16 = mybir.dt.bfloat16
AF = mybir.ActivationFunctionType
ALU = mybir.AluOpType
AX = mybir.AxisListType

D = 1024
EPS = 1e-6
NEG_INIT = -1e30
QSCALE = 128 ** -0.5
NCOL = 1810
FM_QM, FM_KM, FM_QG, FM_KG, FM_GKL = 0, 128, 256, 384, 512
TM0, TM1, TM2 = 528, 1040, 1552
SP_NORMW, SP_CONVW, SP_CONVB, SP_BGK, SP_BI, SP_BF = 0, 8, 16, 18, 19, 20
SP_MNW, SP_GNW, SP_FNW = 21, 277, 533
SP_NWROW = 789
NSMALL = 789 + 1024


class Sched:
    def __init__(self, nc, stack):
        self.nc = nc
        self.stack = stack
        self.eng = {"pe": nc.tensor, "act": nc.scalar, "dve": nc.vector,
                    "pool": nc.gpsimd, "sp": nc.sync}
        self.sem, self.cnt = {}, {}
        self.known = {e: {} for e in self.eng}
        self.bufs = {}
        self.ninst = 0
        self.prog = {e: [] for e in self.eng}
        self.cap = None
        self.marks = set()
        self.bank_of = {"xT_ps": 0, ("PB", 0): 1, ("PB", 1): 2, ("PB", 2): 3,
                        "urep": 4, "gc0": 6, "gc1": 6, "hTm_ps": 4, "hTg_ps": 4, "ATg": 4,
                        "ATm": 5, "num": 5, "kTm": 5, "dC": 6, "kTg": 6, "og": 7, "dS": 7}
        self.bank_last = {i: {} for i in range(8)}
        for e in self.eng:
            self._mk(e)

    def _mk(self, key):
        self.sem[key] = self.stack.enter_context(self.nc.semaphore("s_" + str(key)))
        self.cnt[key] = 0

    def _deps(self, eng, reads, writes, is_dma):
        deps = {}

        def add(k, v):
            if v > deps.get(k, 0):
                deps[k] = v
        for b in reads:
            st = self.bufs.get(b)
            if st and st["w"]:
                k, v = st["w"]
                if not (k == eng and eng == "pe" and not is_dma):
                    add(k, v)
        for b in writes:
            st = self.bufs.get(b)
            if st:
                if st["w"]:
                    k, v = st["w"]
                    if is_dma or k != eng or eng != "pe":
                        add(k, v)
                for k, v in st["r"].items():
                    if is_dma or k != eng or eng != "pe":
                        add(k, v)
        for b in list(reads) + list(writes):
            bank = self.bank_of.get(b)
            if bank is not None:
                for k, v in self.bank_last[bank].items():
                    if k != eng:
                        add(k, v)
        return deps

    def _wait(self, eng, deps):
        for k, v in deps.items():
            if self.known[eng].get(k, 0) >= v:
                continue
            self.prog[eng].append(("w", k, v))
            self.known[eng][k] = v

    def _record(self, key, val, reads, writes):
        for b in reads:
            st = self.bufs.setdefault(b, {"w": None, "r": {}})
            if st["r"].get(key, 0) < val:
                st["r"][key] = val
        for b in writes:
            self.bufs[b] = {"w": (key, val), "r": {}}

    def op(self, eng, fn, reads=(), writes=(), n=128):
        if self.cap is not None:
            self.cap.append(("op", eng, fn, tuple(reads), tuple(writes), n))
            return
        self.sim_commit(("op", eng, fn, tuple(reads), tuple(writes), n))
        self._wait(eng, self._deps(eng, reads, writes, False))
        self.cnt[eng] += 1
        self.prog[eng].append(("i", fn, eng, 1))
        self._record(eng, self.cnt[eng], reads, writes)
        for b in list(reads) + list(writes):
            bank = self.bank_of.get(b)
            if bank is not None:
                self.bank_last[bank][eng] = self.cnt[eng]
        self.ninst += 1

    def dma(self, q, key, fn, reads=(), writes=(), inc=16, n=128):
        if self.cap is not None:
            self.cap.append(("dma", q, key, fn, tuple(reads), tuple(writes), inc, n))
            return
        self.sim_commit(("dma", q, key, fn, tuple(reads), tuple(writes), inc, n))
        if key not in self.sem:
            self._mk(key)
        self._wait(q, self._deps(q, reads, writes, True))
        self.cnt[key] += inc
        self.prog[q].append(("i", fn, key, inc))
        self._record(key, self.cnt[key], reads, writes)
        self.ninst += 1

    def mark(self, key):
        if self.cap is not None:
            self.cap.append(("mark", key))

    def need(self, key):
        if self.cap is not None:
            self.cap.append(("need", key))

    def group_begin(self):
        if self.cap is not None:
            self._gstack = self.cap
            self.cap = []

    def group_end(self):
        if self.cap is not None:
            items = self.cap
            self.cap = self._gstack
            self.cap.append(("group", items))

    def capture(self, f, *a):
        assert self.cap is None
        self.cap = []
        f(*a)
        out, self.cap = self.cap, None
        return out

    def emit(self, it):
        if it[0] == "mark":
            self.marks.add(it[1])
        elif it[0] == "need":
            assert it[1] in self.marks, it
        elif it[0] == "group":
            for x in it[1]:
                self.emit(x)
        elif it[0] == "op":
            self.op(it[1], it[2], it[3], it[4], it[5])
        else:
            self.dma(it[1], it[2], it[3], it[4], it[5], it[6], it[7])

    FIX = {"pe": 0.06, "act": 0.2, "dve": 0.12, "pool": 0.3, "sp": 0.05}
    PER = {"pe": 0.00042, "act": 0.0009, "dve": 0.0012, "pool": 0.0008, "sp": 0.0}
    LAT = 0.45

    def _sim_init(self):
        if not hasattr(self, "ef"):
            self.ef = {e: 0.0 for e in self.eng}
            self.tw, self.trd = {}, {}
            self.bank_t = {i: {} for i in range(8)}

    def _first(self, it):
        while it[0] == "group":
            it = it[1][0]
        return it

    def est_start(self, it):
        self._sim_init()
        it = self._first(it)
        eng = it[1]
        reads, writes = (it[3], it[4]) if it[0] == "op" else (it[4], it[5])
        t = self.ef[eng]

        def rdy(tt_e):
            tt, e2 = tt_e
            return tt + (self.LAT if e2 != eng else 0.03)
        for b_ in reads:
            if b_ in self.tw:
                t = max(t, rdy(self.tw[b_]))
        for b_ in writes:
            if b_ in self.tw:
                t = max(t, rdy(self.tw[b_]))
            for e2, tt in self.trd.get(b_, {}).items():
                t = max(t, rdy((tt, e2)))
        for b_ in list(reads) + list(writes):
            bank = self.bank_of.get(b_)
            if bank is not None:
                for e2, tt in self.bank_t[bank].items():
                    if e2 != eng:
                        t = max(t, tt + self.LAT)
        return t

    def sim_commit(self, it):
        self._sim_init()
        eng = it[1]
        if it[0] == "op":
            reads, writes, n = it[3], it[4], it[5]
            start = self.est_start(it)
            end = start + self.FIX[eng] + self.PER[eng] * n
            self.ef[eng] = end
            who = eng
        else:
            reads, writes, n = it[4], it[5], it[7]
            start = self.est_start(it)
            self.ef[eng] = start + 0.06
            end = start + 2.0 + 0.002 * n
            who = "dma"
        for b_ in reads:
            d = self.trd.setdefault(b_, {})
            d[who] = max(d.get(who, 0.0), end)
        for b_ in writes:
            self.tw[b_] = (end, who)
            self.trd[b_] = {}
        for b_ in list(reads) + list(writes):
            bank = self.bank_of.get(b_)
            if bank is not None:
                self.bank_t[bank][eng] = end

    def interleave(self, streams, weights=None):
        pos = [0] * len(streams)
        marks = self.marks
        while True:
            best, bt = -1, 1e30
            for k, s in enumerate(streams):
                while pos[k] < len(s) and (s[pos[k]][0] == "mark" or
                                           (s[pos[k]][0] == "need" and s[pos[k]][1] in marks)):
                    if s[pos[k]][0] == "mark":
                        marks.add(s[pos[k]][1])
                    pos[k] += 1
                if pos[k] < len(s) and s[pos[k]][0] != "need":
                    t = self.est_start(s[pos[k]])
                    if t < bt:
                        best, bt = k, t
            if best < 0:
                assert all(pos[k] >= len(s) for k, s in enumerate(streams)), "interleave deadlock"
                return
            self.emit(streams[best][pos[best]])
            pos[best] += 1

    def wait_all(self, eng, keys):
        for k in keys:
            if self.cnt[k] > 0:
                self.prog[eng].append(("w", k, self.cnt[k]))

    def replay(self, eng, e):
        for it in self.prog[eng]:
            if it[0] == "w":
                e.wait_ge(self.sem[it[1]], it[2])
            else:
                inst = it[1](e)
                inst.then_inc(self.sem[it[2]], it[3])


def build_nc(S, NSEG, cc_inc=1):
    NT = S // 128
    NM = S // 512
    TS = NT // NSEG
    SEG = S // NSEG
    MPS = NM // NSEG
    assert NM * 512 == S and MPS * NSEG == NM

    nc = bass.Bass("TRN2", target_bir_lowering=False)
    x_d = nc.dram_tensor("x_b", [S, D], F32, kind="ExternalInput").ap()
    win_d = nc.dram_tensor("w_in_c", [D, NCOL], F32, kind="ExternalInput").ap()
    wup_d = nc.dram_tensor("wup_c", [16, 128], F32, kind="ExternalInput").ap()
    wout_d = nc.dram_tensor("wout_c", [2048, 256], F32, kind="ExternalInput").ap()
    xres_d = nc.dram_tensor("xres_c", [S, 256], F32, kind="ExternalInput").ap()
    small_d = nc.dram_tensor("small_c", [128, NSMALL], F32, kind="ExternalInput").ap()
    y_d = nc.dram_tensor("y_c", [S, 256], F32, kind="ExternalOutput").ap()
    hsend = [nc.dram_tensor(f"hsend{g}", [512, SEG], BF16) for g in range(NSEG)]
    hall = [nc.dram_tensor(f"hall{g}", [2048, SEG], BF16) for g in range(NSEG)]
    ssq_i = [nc.dram_tensor(f"ssqi{g}", [128, TS], F32) for g in range(NSEG)]
    ssq_o = [nc.dram_tensor(f"ssqo{g}", [128, TS], F32) for g in range(NSEG)]
    GROUPS = [[0, 1, 2, 3], [4, 5, 6, 7]]
    import os
    CCQOS = os.environ.get('KQOS', 'P2')
    CCQOS = None if CCQOS == 'none' else CCQOS

    with contextlib.ExitStack() as st:
        def sb(name, shape, dt):
            return st.enter_context(nc.sbuf_tensor(name, shape, dt))

        def ps(name, shape, dt):
            return st.enter_context(nc.psum_tensor(name, shape, dt))

        xT_ps = ps("xT_ps", [128, 8, 128], BF16)
        PB = [ps(f"PB{i}", [128, 512], F32) for i in range(3)]
        MB0 = ps("MB0", [128, 512], F32)
        MB1a = ps("MB1a", [128, 512], F32)
        MB2a = ps("MB2a", [128, 512], F32)
        MB3 = ps("MB3", [128, 512], F32)
        UREP, ATG = MB0[:, 0:128], MB0[:, 384:512]
        GC0, GC1 = MB2a[:, 260:262], MB2a[:, 262:264]
        HTPM = MB0[:, 128:256].bitcast(BF16).rearrange("p (b t) -> p b t", b=2)
        HTPG = MB0[:, 256:384].bitcast(BF16).rearrange("p (b t) -> p b t", b=2)
        ATM, NUM = MB1a[:, 0:128], MB1a[:, 128:386]
        KTM = MB1a[:, 448:512].bitcast(BF16)
        DC = MB2a[:, 0:258]
        KTG = MB2a[:, 448:512].bitcast(BF16)
        OG, DS = MB3[:, 0:256], MB3[:, 256:512]

        W = sb("W", [128, 8, NCOL], BF16)
        Wout = sb("Wout", [128, 16, 256], BF16)
        wup = sb("wup", [16, 128], BF16)
        small = sb("small", [128, NSMALL], F32)
        ident_b = sb("ident_b", [128, 128], BF16)
        ident_f = sb("ident_f", [128, 128], F32)
        tri_f = sb("tri_f", [128, 128], F32)
        ones_f = sb("ones_f", [128, 128], F32)
        nbgk = sb("nbgk", [128, 1], F32)
        nbf = sb("nbf", [128, 1], F32)
        nwq = sb("nwq", [128, 256], F32)
        gnwh = sb("gnwh", [128, 256], F32)
        NEGH = sb("negh", [128, 8], F32)
        EPS_AP = sb("eps_ap", [128, 1], F32)
        ONE_AP = sb("one_ap", [128, 1], F32)

        NXT = 4
        xt = [sb(f"xt{i}", [128, D], F32) for i in range(NXT)]
        junk = sb("junk", [128, D], BF16)
        xs = [sb(f"xs{i}", [128, D], BF16) for i in range(2)]
        xT = [sb(f"xT{i}", [128, 8, 512], BF16) for i in range(2)]
        st_small = sb("st_small", [128, 8], F32)
        qkraw = [sb(f"qkraw{s}", [128, 2, 515], F32) for s in range(2)]
        cv = [sb(f"cv{s}", [128, 2, 512], F32) for s in range(2)]
        q_bf = [sb(f"q_bf{s}", [128, 512], BF16) for s in range(2)]
        kh_bf = [sb(f"kh_bf{s}", [128, 512], BF16) for s in range(2)]
        qg_bf = [sb(f"qg_bf{s}", [128, 512], BF16) for s in range(2)]
        kg_bf = [sb(f"kg_bf{s}", [128, 512], BF16) for s in range(2)]
        eBLc = [sb(f"eBLc{s}", [128, 4], F32) for s in range(3)]
        gkl_bf = sb("gkl_bf", [16, 512], BF16)
        spg = sb("spg", [128, 512], F32)
        nbc = sb("nbc", [128, 512], F32)
        eB = sb("eB", [128, 512], F32)
        eNB = sb("eNB", [128, 512], F32)
        Vm = [[sb(f"Vm{s}_{i}", [128, 258], BF16) for i in range(4)] for s in range(2)]
        Vg = [[sb(f"Vg{s}_{i}", [128, 256], BF16) for i in range(4)] for s in range(2)]
        zbuf = [sb(f"zbuf{s}", [128, 4, 512], F32) for s in range(2)]
        obuf = [sb(f"obuf{s}", [128, 4, 256], F32) for s in range(2)]
        iftok = [sb(f"iftok{s}", [128, 4, 2], F32) for s in range(2)]
        spf4 = [sb(f"spf4_{s}", [128, 4], F32) for s in range(2)]
        ef4 = sb("ef4", [128, 4], F32)
        tcv = sb("tcv", [128, 2, 512], F32)
        tz = [tcv[:, j, :] for j in range(2)]
        gsc = [sb(f"gsc{i}", [128, 16], F32) for i in range(4)]
        igd = sb("igd", [128, 128], F32)
        rmat = sb("rmat", [128, 128], F32)
        iot = rmat
        wrep = [sb(f"wrep{i}", [128, 128], F32) for i in range(4)]
        oscg = sb("oscg", [128, 16], F32)
        numS = [sb(f"numS{i}", [128, 258], F32) for i in range(2)]
        ogS = [sb(f"ogS{i}", [128, 256], F32) for i in range(2)]
        mst = [sb(f"mst{i}", [128, 1], F32) for i in range(2)]
        Cst = sb("Cst", [128, 258], F32)
        Ch_bf = sb("Ch_bf", [128, 258], BF16)
        Ust = sb("Ust", [128, 256], F32)
        Sh_bf = sb("Sh_bf", [128, 256], BF16)
        ATm_bf = sb("ATm_bf", [128, 128], BF16)
        ATg_bf = sb("ATg_bf", [128, 128], BF16)
        km_tok = sb("km_tok", [128, 128], BF16)
        kg_tok = sb("kg_tok", [128, 128], BF16)
        htile = sb("htile", [128, 512], BF16)
        osc = sb("osc", [128, 16], F32)
        hTst = [sb(f"hTst{i}", [128, 4, SEG], BF16) for i in range(2)]
        hTall = sb("hTall", [128, 16, SEG], BF16)
        xr = [sb(f"xr{i}", [128, 256], F32) for i in range(TS)]
        ypre = [sb(f"ypre{i}", [128, TS, 256], F32) for i in range(2)]
        ssq = [sb(f"ssq{i}", [128, TS], F32) for i in range(2)]
        ssr = sb("ssr", [128, TS], F32)
        rfin = sb("rfin", [128, TS], F32)
        yo = [sb(f"yo{i}", [128, 256], F32) for i in range(TS)]

        S_ = Sched(nc, st)
        op, dma = S_.op, S_.dma

        def smallc(c0, n=1):
            return small[:, c0:c0 + n]

        dma("sp", "wld", lambda e: e.dma_start(out=small[:, :], in_=small_d[:, :]),
            writes=["small"])
        dma("pool", "wout_ld", lambda e: e.dma_start(out=Wout[:, :, :],
                                                 in_=wout_d.rearrange("(k p) n -> p k n", p=128)),
            writes=["Wout"])
        dma("pool", "wup_ld", lambda e: e.dma_start(out=wup[:, :], in_=wup_d[:, :]), writes=["wup"])
        op("pool", lambda e: e.iota(iot[:, :], [[1, 128]], base=0, channel_multiplier=-1,
                                    allow_small_or_imprecise_dtypes=True), writes=["iot"])
        op("dve", lambda e: e.tensor_single_scalar(tri_f[:, :], iot[:, :], 0.0, ALU.is_ge),
           reads=["iot"], writes=["tri_f"])
        op("dve", lambda e: e.tensor_single_scalar(ident_f[:, :], iot[:, :], 0.0, ALU.is_equal),
           reads=["iot"], writes=["ident_f"])
        op("dve", lambda e: e.tensor_copy(ident_b[:, :], ident_f[:, :]),
           reads=["ident_f"], writes=["ident_b"])
        op("pool", lambda e: e.memset(ones_f[:, :], 1.0), writes=["ones_f"])
        op("pool", lambda e: e.memset(qkraw[1][:, :, 512:515], 0.0), writes=[("qkraw", 1, 0), ("qkraw", 1, 1)])
        op("pool", lambda e: e.memset(Cst[:, :], 0.0), writes=["Cst"])
        op("pool", lambda e: e.memset(Ust[:, :], 0.0), writes=["Ust"])
        op("pool", lambda e: e.memset(mst[0][:, :], NEG_INIT), writes=[("mst", 0)])
        op("pool", lambda e: e.memset(eBLc[2][:, :], 1.0), writes=[("eBLc", 2)])
        op("pool", lambda e: e.memset(EPS_AP[:, :], EPS), writes=["eps_ap"])
        op("pool", lambda e: e.memset(ONE_AP[:, :], 1.0), writes=["one_ap"])
        for s_ in range(2):
            for i in range(4):
                op("pool", lambda e, s_=s_, i=i: e.memset(Vm[s_][i][:, 256:258], 1.0),
                   writes=[("Vm1", s_, i)])
        op("dve", lambda e: e.tensor_scalar(nbgk[:, :], smallc(SP_BGK), -1.0, None, ALU.mult),
           reads=["small"], writes=["nbgk"])
        op("dve", lambda e: e.tensor_scalar(nbf[:, :], smallc(SP_BF), -1.0, None, ALU.mult),
           reads=["small"], writes=["nbf"])
        op("dve", lambda e: e.tensor_scalar(nwq[:, :], smallc(SP_MNW, 256), 0.25, None, ALU.mult),
           reads=["small"], writes=["nwq"])
        op("dve", lambda e: e.tensor_scalar(gnwh[:, :], smallc(SP_GNW, 256), 0.5, None, ALU.mult),
           reads=["small"], writes=["gnwh"])
        op("pool", lambda e: e.memset(NEGH[:, :], -0.5), writes=["negh"])
        win_v = win_d.rearrange("(k p) n -> p k n", p=128)
        WG = [(0, 528), (528, 1040), (1040, 1552), (1552, NCOL)]
        for gi, (c0, c1) in enumerate(WG):
            dma("pool", "wg%d" % gi, lambda e, c0=c0, c1=c1: e.dma_start(
                out=W[:, :, c0:c1], in_=win_v[:, :, c0:c1]), writes=[("W", gi)], n=4000)
        Wall = [("W", gi) for gi in range(4)]
        for e_ in ("act", "dve", "pe", "pool"):
            S_._wait(e_, {"pool": S_.cnt["pool"], "dve": S_.cnt["dve"]})

        pb_rr = [0]

        def next_pb():
            b = pb_rr[0]
            pb_rr[0] = (b + 1) % 2
            return b

        def a_tile(m, i):
            slot = m % 2
            ti = 4 * m + i
            xs_ = ti % 2
            xq = ti % NXT
            op("act", lambda e: e.activation(
                junk[:, :], xt[xq][:, :], AF.Square, accum_out=st_small[:, 0:1]),
               reads=[("xt", xq)], writes=["junk", "junk2", "junk3", "sa0"], n=1024)
            op("dve", lambda e: e.tensor_scalar(
                st_small[:, 1:2], st_small[:, 0:1], 1.0 / D, EPS, ALU.mult, ALU.add),
               reads=["sa0"], writes=["sa1"])
            op("pool", lambda e: e.tensor_tensor(
                st_small[:, 2:3], st_small[:, 1:2], NEGH[:, 0:1], ALU.pow),
               reads=["sa1", "negh"], writes=["sa2"])
            op("dve", lambda e: e.scalar_tensor_tensor(
                xs[xs_][:, :], xt[xq][:, :], st_small[:, 2:3], smallc(SP_NWROW, 1024),
                ALU.mult, ALU.mult),
               reads=[("xt", xq), "sa2", "small"], writes=[("xs", xs_)], n=1024)
            for kt in range(8):
                op("pe", lambda e, kt=kt: e.transpose(
                    xT_ps[:, kt, :], xs[xs_][:, kt * 128:(kt + 1) * 128], ident_b[:, :]),
                   reads=[("xs", xs_), "ident_b"], writes=["xT_ps"])
            op("act", lambda e: e.activation(
                xT[slot][:, :, i * 128:(i + 1) * 128], xT_ps[:, :, :], AF.Copy),
               reads=["xT_ps"], writes=[("xT", slot, i)], n=1024)

        def fm_group(m, col0, ncols):
            slot = m % 2
            b = next_pb()
            S_.group_begin()
            for kt in range(8):
                op("pe", lambda e, kt=kt, b=b: e.matmul(
                    PB[b][0:ncols, :], W[:, kt, col0:col0 + ncols], xT[slot][:, kt, :],
                    start=(kt == 0), stop=(kt == 7)),
                   reads=[("W", 0)] + [("xT", slot, i) for i in range(4)], writes=[("PB", b)], n=512)
            S_.group_end()
            return b

        def macro_fm(m):
            s = m % 2
            b = fm_group(m, FM_QM, 128)
            op("act", lambda e, b=b: e.activation(qkraw[s][:, 0, 3:515], PB[b][:, :], AF.Copy),
               reads=[("PB", b)], writes=[("qkraw", s, 0)], n=512)
            b = fm_group(m, FM_KM, 128)
            op("act", lambda e, b=b: e.activation(qkraw[s][:, 1, 3:515], PB[b][:, :], AF.Copy),
               reads=[("PB", b)], writes=[("qkraw", s, 1)], n=512)
            op("pool", lambda e: e.tensor_copy(qkraw[s][:, :, 0:3], qkraw[1 - s][:, :, 512:515]),
               reads=[("qkraw", 1 - s, 0), ("qkraw", 1 - s, 1)], writes=[("qkraw_h", s)])
            for w_ in range(2):
                op("dve", lambda e, w_=w_: e.tensor_scalar(
                    cv[s][:, w_, :], qkraw[s][:, w_, 0:512], smallc(SP_CONVW + 4 * w_),
                    smallc(SP_CONVB + w_), ALU.mult, ALU.add),
                   reads=[("qkraw", s, w_), ("qkraw_h", s), "small"], writes=[("cv", s, w_)], n=512)
                for j in range(1, 4):
                    op("dve", lambda e, w_=w_, j=j: e.scalar_tensor_tensor(
                        cv[s][:, w_, :], qkraw[s][:, w_, j:j + 512], smallc(SP_CONVW + 4 * w_ + j),
                        cv[s][:, w_, :], ALU.mult, ALU.add),
                       reads=[("qkraw", s, w_), ("qkraw_h", s), "small", ("cv", s, w_)],
                       writes=[("cv", s, w_)], n=512)
            cvn = [("cv", s, 0), ("cv", s, 1)]
            op("act", lambda e: e.activation(tcv[:, :, :], cv[s][:, :, :], AF.Tanh, scale=0.5),
               reads=cvn, writes=["tcv", ("tz", 0, 0), ("tz", 0, 1), ("tz", 1, 0), ("tz", 1, 1)], n=1024)
            op("dve", lambda e: e.scalar_tensor_tensor(
                cv[s][:, :, :], tcv[:, :, :], 1.0, cv[s][:, :, :], ALU.add, ALU.mult),
               reads=cvn + ["tcv", ("tz", 0, 0), ("tz", 0, 1), ("tz", 1, 0), ("tz", 1, 1)], writes=cvn, n=1024)
            op("pool", lambda e: e.tensor_scalar(q_bf[s][:, :], cv[s][:, 0, :], 0.5 * QSCALE, 1.0,
                                                 ALU.mult, ALU.mult),
               reads=[("cv", s, 0)], writes=[("q_bf", s)], n=512)

        def macro_gla(m):
            s = m % 2
            b = fm_group(m, FM_GKL, 16)
            op("act", lambda e, b=b: e.activation(gkl_bf[:, :], PB[b][0:16, :], AF.Copy),
               reads=[("PB", b)], writes=["gkl_bf"])
            b = next_pb()
            op("pe", lambda e, b=b: e.matmul(PB[b][:, :], wup[:, :], gkl_bf[:, :],
                                             start=True, stop=True),
               reads=["wup", "gkl_bf"], writes=[("PB", b)], n=512)
            op("act", lambda e, b=b: e.activation(spg[:, :], PB[b][:, :], AF.Exp,
                                                  bias=nbgk[:, :], scale=-1.0),
               reads=[("PB", b), "nbgk"], writes=["spg"], n=512)
            ifn = [("iftok", s, i) for i in range(4)]
            op("act", lambda e: e.activation(ef4[:, :], iftok[s][:, :, 1], AF.Exp,
                                             bias=nbf[:, :], scale=-1.0),
               reads=ifn + ["nbf"], writes=["ef4"])
            S_.group_begin()
            op("act", lambda e: e.activation(spg[:, :], spg[:, :], AF.Ln, bias=ONE_AP[:, :], scale=1.0),
               reads=["spg", "one_ap"], writes=["spg"], n=1800)
            op("act", lambda e: e.activation(spf4[s][:, :], ef4[:, :], AF.Ln, bias=ONE_AP[:, :], scale=1.0),
               reads=["ef4", "one_ap"], writes=[("spf4", s)], n=1500)
            S_.group_end()
            for c in range(4):
                op("dve", lambda e, c=c: e.tensor_tensor_scan(
                    nbc[:, c * 128:(c + 1) * 128], ones_f[:, :], spg[:, c * 128:(c + 1) * 128],
                    0.0, ALU.mult, ALU.add),
                   reads=["spg", "ones_f"], writes=[("nbc", c)], n=256)
            nbcs = [("nbc", c) for c in range(4)]
            op("act", lambda e: e.activation(eB[:, :], nbc[:, :], AF.Exp, scale=-1.0 / 16.0),
               reads=nbcs, writes=["eB"], n=512)
            op("act", lambda e: e.activation(eNB[:, :], nbc[:, :], AF.Exp, scale=1.0 / 16.0),
               reads=nbcs, writes=["eNB"], n=512)
            op("pool", lambda e: e.tensor_copy(
                eBLc[m % 3][:, :], eB[:, :].rearrange("p (c t) -> p c t", t=128)[:, :, 127]),
               reads=["eB"], writes=[("eBLc", m % 3)])
            b = fm_group(m, FM_QG, 128)
            op("dve", lambda e, b=b: e.scalar_tensor_tensor(
                qg_bf[s][:, :], PB[b][:, :], QSCALE, eB[:, :], ALU.mult, ALU.mult),
               reads=[("PB", b), "eB"], writes=[("qg_bf", s)], n=512)
            b = fm_group(m, FM_KG, 128)
            op("dve", lambda e, b=b: e.tensor_tensor(kg_bf[s][:, :], PB[b][:, :], eNB[:, :], ALU.mult),
               reads=[("PB", b), "eNB"], writes=[("kg_bf", s)], n=512)

        def tm_tile(m, i):
            s = m % 2
            tsl = slice(i * 128, (i + 1) * 128)
            for (c0, n) in ((TM0, 512), (TM1, 512), (TM2, 258)):
                b = next_pb()
                S_.group_begin()
                for kt in range(8):
                    op("pe", lambda e, kt=kt, b=b, c0=c0, n=n: e.matmul(
                        PB[b][:, 0:n], xT[s][:, kt, tsl], W[:, kt, c0:c0 + n],
                        start=(kt == 0), stop=(kt == 7)),
                       reads=[("W", {TM0: 1, TM1: 2, TM2: 3}[c0]), ("xT", s, i)], writes=[("PB", b)], n=n)
                S_.group_end()
                j = i % 2
                if c0 == TM0:
                    op("dve", lambda e, b=b: e.tensor_copy(Vm[s][i][:, 0:256], PB[b][:, 0:256]),
                       reads=[("PB", b)], writes=[("Vm", s, i)], n=256)
                    op("act", lambda e, b=b: e.activation(obuf[s][:, i, :], PB[b][:, 256:512],
                                                          AF.Tanh, scale=0.5),
                       reads=[("PB", b)], writes=[("obuf", s, i)], n=256)
                elif c0 == TM1:
                    op("act", lambda e, b=b: e.activation(tz[j][:, 0:256], PB[b][:, 0:256],
                                                          AF.Tanh, scale=0.5),
                       reads=[("PB", b)], writes=[("tz", j, 0)], n=256)
                    op("dve", lambda e, b=b: e.scalar_tensor_tensor(
                        zbuf[s][:, i, 0:256], tz[j][:, 0:256], 1.0, PB[b][:, 0:256], ALU.add, ALU.mult),
                       reads=[("PB", b), ("tz", j, 0)], writes=[("zm", s, i)], n=256)
                    op("dve", lambda e, b=b: e.tensor_copy(Vg[s][i][:, :], PB[b][:, 256:512]),
                       reads=[("PB", b)], writes=[("Vg", s, i)], n=256)
                else:
                    op("act", lambda e, b=b: e.activation(tz[j][:, 256:512], PB[b][:, 0:256],
                                                          AF.Tanh, scale=0.5),
                       reads=[("PB", b)], writes=[("tz", j, 1)], n=256)
                    op("dve", lambda e, b=b: e.scalar_tensor_tensor(
                        zbuf[s][:, i, 256:512], tz[j][:, 256:512], 1.0, PB[b][:, 0:256], ALU.add, ALU.mult),
                       reads=[("PB", b), ("tz", j, 1)], writes=[("zg", s, i)], n=256)
                    op("dve", lambda e, b=b: e.tensor_copy(iftok[s][:, i, :], PB[b][:, 256:258]),
                       reads=[("PB", b)], writes=[("iftok", s, i)])
            gprod(s, i)

        def gprod(s, i):
            op("pool", lambda e: e.tensor_tensor(zbuf[s][:, i, 0:256], zbuf[s][:, i, 0:256],
                                                 nwq[:, :], ALU.mult),
               reads=[("zm", s, i), "nwq"], writes=[("zm", s, i)], n=256)
            op("pool", lambda e: e.tensor_scalar(obuf[s][:, i, :], obuf[s][:, i, :], 1.0, 1.0,
                                                 ALU.mult, ALU.add),
               reads=[("obuf", s, i)], writes=[("obuf", s, i)], n=256)
            op("pool", lambda e: e.tensor_tensor(obuf[s][:, i, :], obuf[s][:, i, :],
                                                 zbuf[s][:, i, 0:256], ALU.mult),
               reads=[("obuf", s, i), ("zm", s, i)], writes=[("obuf", s, i)], n=256)
            op("pool", lambda e: e.tensor_tensor(zbuf[s][:, i, 256:512], zbuf[s][:, i, 256:512],
                                                 gnwh[:, :], ALU.mult),
               reads=[("zg", s, i), "gnwh"], writes=[("zg", s, i)], n=256)

        def x_loads(m):
            for i in range(4):
                x_load(4 * m + i)

        def x_load(ti):
            xq = ti % NXT
            dma("sp", ("xld", xq), lambda e: e.dma_start(
                out=xt[xq][:, :], in_=x_d[ti * 128:(ti + 1) * 128, :]), writes=[("xt", xq)], n=1024)

        def stage_a_stream(m):
            for i in range(4):
                a_tile(m, i)

        def fmtm_stream(m):
            macro_fm(m)
            for i in range(4):
                tm_tile(m, i)
            macro_gla(m)

        def cidx(m, i):
            return 4 * m + i

        def gates(m, i):
            s = m % 2
            cn = cidx(m, i)
            p = cn % 4
            G = gsc[p]
            mp, mn = mst[cn % 2], mst[(cn + 1) % 2]
            mpn, mnn = ("mst", cn % 2), ("mst", (cn + 1) % 2)
            if cn >= 4:
                S_.need(("M", cn - 4))
            op("dve", lambda e: e.tensor_copy(G[:, 1:2], spf4[s][:, i:i + 1]),
               reads=[("spf4", s)], writes=[("g1", p)])
            op("dve", lambda e: e.tensor_scalar(G[:, 2:3], iftok[s][:, i, 0:1], smallc(SP_BI), None,
                                                ALU.add),
               reads=[("iftok", s, i), "small"], writes=[("g2", p)])
            op("pool", lambda e: e.tensor_scalar(igd[:, :], ident_f[:, :], G[:, 2:3], 1.0,
                                                 ALU.mult, ALU.mult),
               reads=["ident_f", ("g2", p)], writes=["igd"])
            op("dve", lambda e: e.scalar_tensor_tensor(rmat[:, :], tri_f[:, :], G[:, 1:2], igd[:, :],
                                                       ALU.mult, ALU.add),
               reads=["tri_f", ("g1", p), "igd"], writes=["rmat"])
            op("pe", lambda e: e.matmul(UREP, ones_f[:, :], rmat[:, :], start=True, stop=True),
               reads=["rmat", "ones_f"], writes=["urep"])
            op("pe", lambda e: e.matmul(GC0, tri_f[:, :], G[:, 1:3], start=True, stop=True),
               reads=["tri_f", ("g1", p), ("g2", p)], writes=["gc0"])
            op("pe", lambda e: e.matmul(GC1, ones_f[:, :], G[:, 1:3], start=True, stop=True),
               reads=["ones_f", ("g1", p), ("g2", p)], writes=["gc1"])
            op("dve", lambda e: e.tensor_reduce(G[:, 3:4], UREP, AX.X, ALU.max),
               reads=["urep"], writes=[("g3", p)])
            op("dve", lambda e: e.tensor_tensor(G[:, 4:5], G[:, 3:4], mp[:, :], ALU.max),
               reads=[("g3", p), mpn], writes=[("g4", p)])
            op("dve", lambda e: e.tensor_scalar(G[:, 5:6], G[:, 4:5], -1.0, None, ALU.mult),
               reads=[("g4", p)], writes=[("g5", p)])
            op("act", lambda e: e.activation(wrep[p][:, :], UREP, AF.Exp, bias=G[:, 5:6], scale=1.0),
               reads=["urep", ("g5", p)], writes=[("wrep", p)])
            op("act", lambda e: e.activation(G[:, 6:7], mp[:, :], AF.Exp, bias=G[:, 5:6], scale=1.0),
               reads=[mpn, ("g5", p)], writes=[("g6", p)])
            op("act", lambda e: e.activation(G[:, 7:8], MB2a[:, 260:261], AF.Exp, bias=G[:, 5:6], scale=1.0),
               reads=["gc0", ("g5", p)], writes=[("g7", p)])
            op("dve", lambda e: e.scalar_tensor_tensor(mn[:, :], MB2a[:, 262:263], -1.0, G[:, 4:5],
                                                       ALU.mult, ALU.add),
               reads=["gc1", ("g4", p)], writes=[mnn])
            S_.mark(("G", cn))

        def hsl(m, i):
            g = m // MPS
            return g % 2, (4 * m + i) * 128 - g * SEG

        def mlstm(m, i):
            s = m % 2
            p = cidx(m, i) % 4
            G = gsc[p]
            csl = slice(i * 128, (i + 1) * 128)
            hslot, tok0 = hsl(m, i)
            p2 = cidx(m, i) % 2
            S_.need(("G", cidx(m, i)))
            if cidx(m, i) >= 2:
                S_.need(("M", cidx(m, i) - 2))
            op("dve", lambda e: e.scalar_tensor_tensor(kh_bf[s][:, csl], cv[s][:, 1, csl], 0.5,
                                                       wrep[p][:, :], ALU.mult, ALU.mult),
               reads=[("cv", s, 1), ("wrep", p)], writes=[("kh_bf", s, i)])
            op("pe", lambda e: e.matmul(ATM, kh_bf[s][:, csl], q_bf[s][:, csl], start=True, stop=True),
               reads=[("kh_bf", s, i), ("q_bf", s)], writes=["ATm"])
            op("pe", lambda e: e.transpose(KTM, kh_bf[s][:, csl], ident_b[:, :]),
               reads=[("kh_bf", s, i), "ident_b"], writes=["kTm"])
            op("dve", lambda e: e.tensor_tensor(ATm_bf[:, :], ATM, tri_f[:, :], ALU.mult),
               reads=["ATm", "tri_f"], writes=["ATm_bf"])
            op("act", lambda e: e.activation(km_tok[:, :], KTM, AF.Copy),
               reads=["kTm"], writes=["km_tok"])
            op("pool", lambda e: e.tensor_scalar(Ch_bf[:, :], Cst[:, :], G[:, 6:7], 1.0,
                                                 ALU.mult, ALU.mult),
               reads=["Cst", ("g6", p)], writes=["Ch_bf"], n=258)
            S_.group_begin()
            op("pe", lambda e: e.matmul(NUM, ATm_bf[:, :], Vm[s][i][:, :], start=True, stop=False),
               reads=["ATm_bf", ("Vm", s, i), ("Vm1", s, i)], writes=["num"], n=258)
            op("pe", lambda e: e.matmul(NUM, q_bf[s][:, csl], Ch_bf[:, :], start=False, stop=True),
               reads=[("q_bf", s), "Ch_bf"], writes=["num"], n=258)
            S_.group_end()
            op("act", lambda e: e.activation(numS[p2][:, :], NUM, AF.Copy),
               reads=["num"], writes=[("numS", p2)], n=258)
            op("pe", lambda e: e.matmul(DC, km_tok[:, :], Vm[s][i][:, :], start=True, stop=True),
               reads=["km_tok", ("Vm", s, i), ("Vm1", s, i)], writes=["dC"], n=258)
            op("dve", lambda e: e.scalar_tensor_tensor(Cst[:, :], Cst[:, :], G[:, 6:7], DC,
                                                       ALU.mult, ALU.add),
               reads=["Cst", ("g6", p), "dC", "Ch_bf"], writes=["Cst"], n=258)
            S_.mark(("A", cidx(m, i)))

        def mlstm_out(m, i):
            s = m % 2
            cn = cidx(m, i)
            p = cn % 4
            p2 = cn % 2
            G = gsc[p]
            hslot, tok0 = hsl(m, i)
            Gm_ = obuf[s][:, i, :]
            NS = numS[p2]
            nsn = ("numS", p2)
            S_.need(("A", cn))
            op("act", lambda e: e.activation(osc[:, 11:12], NS[:, 256:257], AF.Abs),
               reads=[nsn], writes=["o11"])
            op("act", lambda e: e.activation(junk[:, 0:256], NS[:, 0:256], AF.Square, scale=1.0 / 16.0,
                                             accum_out=osc[:, 2:3]),
               reads=[nsn], writes=["junk", "o2"], n=256)
            op("dve", lambda e: e.tensor_tensor(osc[:, 0:1], osc[:, 11:12], G[:, 7:8], ALU.max),
               reads=["o11", ("g7", p)], writes=["o0"])
            op("dve", lambda e: e.tensor_tensor(osc[:, 3:4], osc[:, 0:1], osc[:, 0:1], ALU.mult),
               reads=["o0"], writes=["o3"])
            op("dve", lambda e: e.scalar_tensor_tensor(osc[:, 4:5], osc[:, 3:4], EPS, osc[:, 2:3],
                                                       ALU.mult, ALU.add),
               reads=["o3", "o2"], writes=["o4"])
            op("pool", lambda e: e.tensor_tensor(osc[:, 7:8], osc[:, 4:5], NEGH[:, 1:2], ALU.pow),
               reads=["o4", "negh"], writes=["o7"])
            op("dve", lambda e: e.scalar_tensor_tensor(htile[:, 0:256], NS[:, 0:256], osc[:, 7:8],
                                                       Gm_, ALU.mult, ALU.mult),
               reads=[nsn, "o7", ("obuf", s, i)], writes=["htile_m"], n=256)
            S_.group_begin()
            for bk in range(2):
                op("pe", lambda e, bk=bk: e.transpose(HTPM[:, bk, :],
                                                      htile[:, bk * 128:(bk + 1) * 128], ident_b[:, :]),
                   reads=["htile_m", "ident_b"], writes=["hTm_ps"])
            S_.group_end()
            op("act", lambda e: e.activation(
                hTst[hslot][:, 0:2, tok0:tok0 + 128], HTPM[:, :, :], AF.Copy),
               reads=["hTm_ps"], writes=[("hTst", hslot, 0)], n=256)
            S_.mark(("M", cidx(m, i)))

        def gla(m, i):
            s = m % 2
            csl = slice(i * 128, (i + 1) * 128)
            hslot, tok0 = hsl(m, i)
            if i == 0:
                dec, decn = eBLc[(m + 2) % 3][:, 3:4], ("eBLc", (m + 2) % 3)
            else:
                dec, decn = eBLc[m % 3][:, i - 1:i], ("eBLc", m % 3)
            if cidx(m, i) >= 2:
                S_.need(("GM", cidx(m, i) - 2))
            op("pe", lambda e: e.matmul(ATG, kg_bf[s][:, csl], qg_bf[s][:, csl], start=True, stop=True),
               reads=[("kg_bf", s), ("qg_bf", s)], writes=["ATg"])
            op("pe", lambda e: e.transpose(KTG, kg_bf[s][:, csl], ident_b[:, :]),
               reads=[("kg_bf", s), "ident_b"], writes=["kTg"])
            op("dve", lambda e: e.tensor_tensor(ATg_bf[:, :], ATG, tri_f[:, :], ALU.mult),
               reads=["ATg", "tri_f"], writes=["ATg_bf"])
            op("act", lambda e: e.activation(kg_tok[:, :], KTG, AF.Copy),
               reads=["kTg"], writes=["kg_tok"])
            op("pool", lambda e: e.tensor_scalar(Sh_bf[:, :], Ust[:, :], dec, 1.0, ALU.mult, ALU.mult),
               reads=["Ust", decn], writes=["Sh_bf"], n=256)
            S_.group_begin()
            op("pe", lambda e: e.matmul(OG, ATg_bf[:, :], Vg[s][i][:, :], start=True, stop=False),
               reads=["ATg_bf", ("Vg", s, i)], writes=["og"], n=256)
            op("pe", lambda e: e.matmul(OG, qg_bf[s][:, csl], Sh_bf[:, :], start=False, stop=True),
               reads=[("qg_bf", s), "Sh_bf"], writes=["og"], n=256)
            S_.group_end()
            op("act", lambda e: e.activation(ogS[cidx(m, i) % 2][:, :], OG, AF.Copy),
               reads=["og"], writes=[("ogS", cidx(m, i) % 2)], n=256)
            op("pe", lambda e: e.matmul(DS, kg_tok[:, :], Vg[s][i][:, :], start=True, stop=True),
               reads=["kg_tok", ("Vg", s, i)], writes=["dS"], n=256)
            op("dve", lambda e: e.scalar_tensor_tensor(Ust[:, :], Ust[:, :], dec, DS, ALU.mult, ALU.add),
               reads=["Ust", decn, "dS", "Sh_bf"], writes=["Ust"], n=256)
            S_.mark(("GA", cidx(m, i)))

        def gla_out(m, i):
            s = m % 2
            cn = cidx(m, i)
            p2 = cn % 2
            hslot, tok0 = hsl(m, i)
            Gg_ = zbuf[s][:, i, 256:512]
            OS = ogS[p2]
            osn = ("ogS", p2)
            S_.need(("GA", cn))
            op("act", lambda e: e.activation(junk[:, 256:512], OS[:, :], AF.Square, scale=1.0 / 16.0,
                                             accum_out=oscg[:, 0:1]),
               reads=[osn], writes=["junk2", "p0"], n=256)
            op("dve", lambda e: e.tensor_scalar(oscg[:, 1:2], oscg[:, 0:1], EPS, None, ALU.add),
               reads=["p0"], writes=["p1"])
            op("pool", lambda e: e.tensor_tensor(oscg[:, 2:3], oscg[:, 1:2], NEGH[:, 2:3], ALU.pow),
               reads=["p1", "negh"], writes=["p2"])
            op("dve", lambda e: e.scalar_tensor_tensor(htile[:, 256:512], OS[:, :], oscg[:, 2:3],
                                                       Gg_, ALU.mult, ALU.mult),
               reads=[osn, "p2", ("zg", s, i)], writes=["htile_g"], n=256)
            S_.group_begin()
            for bk in range(2):
                op("pe", lambda e, bk=bk: e.transpose(HTPG[:, bk, :],
                                                      htile[:, 256 + bk * 128:256 + (bk + 1) * 128],
                                                      ident_b[:, :]),
                   reads=["htile_g", "ident_b"], writes=["hTg_ps"])
            S_.group_end()
            op("act", lambda e: e.activation(
                hTst[hslot][:, 2:4, tok0:tok0 + 128], HTPG[:, :, :], AF.Copy),
               reads=["hTg_ps"], writes=[("hTst", hslot, 1)], n=256)
            S_.mark(("GM", cidx(m, i)))

        def gates_stream(m):
            for i in range(4):
                gates(m, i)

        def mlstm_stream(m):
            for i in range(4):
                mlstm(m, i)

        def gla_stream(m):
            for i in range(4):
                gla(m, i)

        def mlstm_out_stream(m):
            for i in range(4):
                mlstm_out(m, i)

        def gla_out_stream(m):
            for i in range(4):
                gla_out(m, i)

        def seg_send(g):
            hslot = g % 2
            dma("sp", ("hst", hslot), lambda e: e.dma_start(
                out=hsend[g].ap().rearrange("(b p) t -> p b t", p=128), in_=hTst[hslot][:, :, :]),
                reads=[("hTst", hslot, 0), ("hTst", hslot, 1)], writes=[("hsend", g)])
            dma("pool", "ccA%d" % (g % 2), lambda e: e.collective_compute(
                "AllGather", ALU.bypass, replica_groups=GROUPS,
                ins=[hsend[g].ap().opt()], outs=[hall[g].ap().opt()], dma_qos=CCQOS),
                reads=[("hsend", g)], writes=[("hall", g)], inc=cc_inc, n=15000)

        def outproj_loads(g):
            dma("sp", "hld", lambda e: e.dma_start(
                out=hTall[:, :, :], in_=hall[g].ap().rearrange("(k p) t -> p k t", p=128)),
                reads=[("hall", g)], writes=["hTall"])
            for tt in range(TS):
                xr_load(g, tt)

        def xr_load(g, tt):
            ti = g * TS + tt
            dma("sp", ("xrld", tt), lambda e: e.dma_start(
                out=xr[tt][:, :], in_=xres_d[ti * 128:(ti + 1) * 128, :]), writes=[("xr", tt)])

        def outproj_a(g):
            ys_ = g % 2
            for tt in range(TS):
                outproj_tile(g, tt)
            ssqs = [("ssq", ys_, tt) for tt in range(TS)]
            dma("sp", "sst", lambda e: e.dma_start(out=ssq_i[g].ap(), in_=ssq[ys_][:, :]),
                reads=ssqs, writes=[("ssq_i", g)])
            dma("pool", "ccR%d" % (g % 2), lambda e: e.collective_compute(
                "AllReduce", ALU.add, replica_groups=GROUPS,
                ins=[ssq_i[g].ap().opt()], outs=[ssq_o[g].ap().opt()], dma_qos=CCQOS),
                reads=[("ssq_i", g)], writes=[("ssq_o", g)], inc=cc_inc, n=12000)

        def outproj_tile(g, tt):
            ys_ = g % 2
            ti = g * TS + tt
            xs_ = tt
            for k in range(16):
                op("pe", lambda e, k=k: e.matmul(
                    PB[2][:, 0:256], hTall[:, k, tt * 128:(tt + 1) * 128], Wout[:, k, :],
                    start=(k == 0), stop=(k == 15)),
                   reads=["hTall", "Wout"], writes=[("PB", 2)], n=256)
            op("dve", lambda e: e.tensor_tensor(
                ypre[ys_][:, tt, :], PB[2][:, 0:256], xr[xs_][:, :], ALU.add),
               reads=[("PB", 2), ("xr", xs_)], writes=[("ypre", ys_, tt)], n=256)
            op("act", lambda e: e.activation(junk[:, 512:768], ypre[ys_][:, tt, :], AF.Square,
                                             accum_out=ssq[ys_][:, tt:tt + 1]),
               reads=[("ypre", ys_, tt)], writes=["junk3", ("ssq", ys_, tt)], n=256)

        def outproj_b(g):
            ys_ = g % 2
            dma("sp", "sld", lambda e: e.dma_start(out=ssr[:, :], in_=ssq_o[g].ap()),
                reads=[("ssq_o", g)], writes=["ssr"])
            op("dve", lambda e: e.tensor_scalar(rfin[:, :], ssr[:, :], 1.0 / D, EPS, ALU.mult, ALU.add),
               reads=["ssr"], writes=["rfin0"])
            op("pool", lambda e: e.tensor_tensor(rfin[:, :], rfin[:, :], NEGH[:, 0:TS], ALU.pow),
               reads=["rfin0", "negh"], writes=["rfin"])
            for tt in range(TS):
                outproj_b_tile(g, tt)

        def outproj_b_tile(g, tt):
            ys_ = g % 2
            ti = g * TS + tt
            yb = tt
            op("dve", lambda e: e.scalar_tensor_tensor(
                yo[yb][:, :], ypre[ys_][:, tt, :], rfin[:, tt:tt + 1], smallc(SP_FNW, 256),
                ALU.mult, ALU.mult),
               reads=[("ypre", ys_, tt), "rfin", "small"], writes=[("yo", yb)], n=256)
            dma("sp", ("yst", yb), lambda e: e.dma_start(
                out=y_d[ti * 128:(ti + 1) * 128, :], in_=yo[yb][:, :]),
                reads=[("yo", yb)], writes=[("y", ti)])

        cap = S_.capture
        x_loads(0)
        for it in cap(stage_a_stream, 0):
            S_.emit(it)
        if NM > 1:
            x_loads(1)
        for it in cap(fmtm_stream, 0):
            S_.emit(it)
        if NM > 1:
            for it in cap(stage_a_stream, 1):
                S_.emit(it)
        if NM > 2:
            x_loads(2)
        LAG_A, LAG_B = int(os.environ.get("KLA", "2")), int(os.environ.get("KLB", "3"))
        for m in range(NM):
            g = m // MPS
            import os
            if os.environ.get("KSEQ") == "1":
                def seq_stream(m):
                    for i in range(4):
                        gates(m, i)
                        mlstm(m, i)
                        mlstm_out(m, i)
                        gla(m, i)
                        gla_out(m, i)
                streams = [cap(seq_stream, m)]
            else:
                streams = [cap(gates_stream, m), cap(mlstm_stream, m), cap(gla_stream, m),
                           cap(mlstm_out_stream, m), cap(gla_out_stream, m)]
            WG = float(os.environ.get("KWG", "0.5"))
            WP = float(os.environ.get("KWP", "1.0"))
            weights = [WG, 1.0, 1.0, 1.0, 1.0][:len(streams)] if len(streams) == 5 else [1.0]
            if m + 1 < NM:
                if os.environ.get("KPF", "0") == "1":
                    streams.insert(0, cap(fmtm_stream, m + 1))
                else:
                    streams.append(cap(fmtm_stream, m + 1))
                weights.append(WP)
            last_of_seg = (m + 1) % MPS == 0
            if last_of_seg and g - LAG_A >= 0:
                streams.append(cap(outproj_a, g - LAG_A))
                weights.append(1.0)
            if last_of_seg and g - LAG_B >= 0:
                streams.append(cap(outproj_b, g - LAG_B))
                weights.append(1.0)
            if m + 2 < NM:
                streams.append(cap(stage_a_stream, m + 2))
                weights.append(1.0)
            S_.interleave(streams, weights)
            if m + 3 < NM:
                x_loads(m + 3)
            if last_of_seg:
                if 0 <= g + 1 - LAG_A and m + 1 < NM:
                    outproj_loads(g + 1 - LAG_A)
                seg_send(g)
        na, nb = max(0, NSEG - LAG_A), max(0, NSEG - LAG_B)
        while na < NSEG or nb < NSEG:
            if na < NSEG:
                outproj_loads(na)
                outproj_a(na)
                na += 1
            while nb < NSEG and (nb < na - 1 or na == NSEG):
                outproj_b(nb)
                nb += 1

        S_.wait_all("sp", list(S_.sem.keys()))
        with nc.Block() as block:
            block.tensor(lambda e: S_.replay("pe", e))
            block.vector(lambda e: S_.replay("dve", e))
            block.scalar(lambda e: S_.replay("act", e))
            block.gpsimd(lambda e: S_.replay("pool", e))
            block.sync(lambda e: S_.replay("sp", e))
        print("ninst", S_.ninst)
    return nc


def _prep_inputs(x, norm_w, w_in, conv_w, conv_b, b_igate, b_fgate, mlstm_norm_w,
                 w_gk_up, b_gk, gla_norm_w, w_out, final_norm_w):
    f = np.float32
    x = np.asarray(x, f)
    w_in = np.asarray(w_in, f)
    o = 0
    offs = {}
    for name, n in (("qm", 512), ("km", 512), ("vm", 1024), ("i", 4), ("f", 4), ("o", 1024),
                    ("zm", 1024), ("qg", 512), ("kg", 512), ("vg", 1024), ("gkl", 16), ("zg", 1024)):
        offs[name] = o
        o += n
    conv_w = np.asarray(conv_w, f)
    conv_b = np.asarray(conv_b, f)
    w_out = np.asarray(w_out, f)
    in_maps = []
    for c in range(8):
        b, j = c // 4, c % 4
        cols = np.concatenate([
            np.arange(offs["qm"] + 128 * j, offs["qm"] + 128 * (j + 1)),
            np.arange(offs["km"] + 128 * j, offs["km"] + 128 * (j + 1)),
            np.arange(offs["qg"] + 128 * j, offs["qg"] + 128 * (j + 1)),
            np.arange(offs["kg"] + 128 * j, offs["kg"] + 128 * (j + 1)),
            np.arange(offs["gkl"], offs["gkl"] + 16),
            np.arange(offs["vm"] + 256 * j, offs["vm"] + 256 * (j + 1)),
            np.arange(offs["o"] + 256 * j, offs["o"] + 256 * (j + 1)),
            np.arange(offs["zm"] + 256 * j, offs["zm"] + 256 * (j + 1)),
            np.arange(offs["vg"] + 256 * j, offs["vg"] + 256 * (j + 1)),
            np.arange(offs["zg"] + 256 * j, offs["zg"] + 256 * (j + 1)),
            np.array([offs["i"] + j, offs["f"] + j]),
        ])
        assert cols.size == NCOL
        w_in_c = np.ascontiguousarray(w_in[:, cols])
        wup_c = np.ascontiguousarray(np.asarray(w_gk_up, f)[:, 128 * j:128 * (j + 1)])
        rows = np.concatenate([np.concatenate([np.arange(256 * r, 256 * (r + 1)),
                                               np.arange(1024 + 256 * r, 1024 + 256 * (r + 1))])
                               for r in range(4)])
        wout_c = np.ascontiguousarray(w_out[rows][:, 256 * j:256 * (j + 1)])
        xres_c = np.ascontiguousarray(x[b][:, 256 * j:256 * (j + 1)])
        small = np.zeros((128, NSMALL), f)
        small[:, SP_NORMW:SP_NORMW + 8] = np.asarray(norm_w, f).reshape(8, 128).T
        small[:, SP_CONVW:SP_CONVW + 4] = conv_w[:, 128 * j:128 * (j + 1)].T
        small[:, SP_CONVW + 4:SP_CONVW + 8] = conv_w[:, 512 + 128 * j:512 + 128 * (j + 1)].T
        small[:, SP_CONVB] = conv_b[128 * j:128 * (j + 1)]
        small[:, SP_CONVB + 1] = conv_b[512 + 128 * j:512 + 128 * (j + 1)]
        small[:, SP_BGK] = np.asarray(b_gk, f)[128 * j:128 * (j + 1)]
        small[:, SP_BI] = np.asarray(b_igate, f)[j]
        small[:, SP_BF] = np.asarray(b_fgate, f)[j]
        small[:, SP_MNW:SP_MNW + 256] = np.asarray(mlstm_norm_w, f)[256 * j:256 * (j + 1)][None, :]
        small[:, SP_GNW:SP_GNW + 256] = np.asarray(gla_norm_w, f)[256 * j:256 * (j + 1)][None, :]
        small[:, SP_FNW:SP_FNW + 256] = np.asarray(final_norm_w, f)[256 * j:256 * (j + 1)][None, :]
        small[:, SP_NWROW:SP_NWROW + 1024] = np.asarray(norm_w, f)[None, :]
        in_maps.append({"x_b": np.ascontiguousarray(x[b]), "w_in_c": w_in_c, "wup_c": wup_c,
                        "wout_c": wout_c, "xres_c": xres_c, "small_c": small})
    return in_maps


def run(inputs, NSEG=None, trace=False):
    x = np.asarray(inputs["x"])
    B, S, _ = x.shape
    assert B == 2
    if NSEG is None:
        NSEG = S // 512
    nc = build_nc(S, NSEG)
    in_maps = _prep_inputs(**inputs)
    res = run_bass_kernel_spmd(nc, in_maps, core_ids=list(range(8)), trace=trace)
    out = np.empty((B, S, D), np.float32)
    for c in range(8):
        b, j = c // 4, c % 4
        out[b, :, 256 * j:256 * (j + 1)] = np.asarray(res.results[c]["y_c"], np.float32)
    return out, res


def kernel(**inputs):
    out, _ = run(inputs)
    return out
```

```python
import contextlib
import numpy as np
import ml_dtypes
import concourse.bass as bass
import concourse.mybir as mybir
from concourse.bass_utils import run_bass_kernel_spmd

F32 = mybir.dt.float32
BF16 = mybir.dt.bfloat16
AF = mybir.ActivationFunctionType
ALU = mybir.AluOpType
AX = mybir.AxisListType

D = 1024
EPS = 1e-6
NEG_INIT = -1e30
QSCALE = 128 ** -0.5
NCOL = 1810
FM_QM, FM_KM, FM_QG, FM_KG, FM_GKL = 0, 128, 256, 384, 512
TM0, TM1, TM2 = 528, 1040, 1552
SP_NORMW, SP_CONVW, SP_CONVB, SP_BGK, SP_BI, SP_BF = 0, 8, 16, 18, 19, 20
SP_MNW, SP_GNW, SP_FNW = 21, 277, 533
SP_NWROW = 789
NSMALL = 789 + 1024


class Sched:
    def __init__(self, nc, stack):
        self.nc = nc
        self.stack = stack
        self.eng = {"pe": nc.tensor, "act": nc.scalar, "dve": nc.vector,
                    "pool": nc.gpsimd, "sp": nc.sync}
        self.sem, self.cnt = {}, {}
        self.known = {e: {} for e in self.eng}
        self.bufs = {}
        self.ninst = 0
        self.prog = {e: [] for e in self.eng}
        self.cap = None
        self.marks = set()
        self.bank_of = {"xT_ps": 0, ("PB", 0): 1, ("PB", 1): 2, ("PB", 2): 3,
                        "urep": 4, "gc0": 6, "gc1": 6, "hTm_ps": 4, "hTg_ps": 4, "ATg": 4,
                        "ATm": 5, "num": 5, "kTm": 5, "dC": 6, "kTg": 6, "og": 7, "dS": 7}
        self.bank_last = {i: {} for i in range(8)}
        for e in self.eng:
            self._mk(e)

    def _mk(self, key):
        self.sem[key] = self.stack.enter_context(self.nc.semaphore("s_" + str(key)))
        self.cnt[key] = 0

    def _deps(self, eng, reads, writes, is_dma):
        deps = {}

        def add(k, v):
            if v > deps.get(k, 0):
                deps[k] = v
        for b in reads:
            st = self.bufs.get(b)
            if st and st["w"]:
                k, v = st["w"]
                if not (k == eng and eng == "pe" and not is_dma):
                    add(k, v)
        for b in writes:
            st = self.bufs.get(b)
            if st:
                if st["w"]:
                    k, v = st["w"]
                    if is_dma or k != eng or eng != "pe":
                        add(k, v)
                for k, v in st["r"].items():
                    if is_dma or k != eng or eng != "pe":
                        add(k, v)
        for b in list(reads) + list(writes):
            bank = self.bank_of.get(b)
            if bank is not None:
                for k, v in self.bank_last[bank].items():
                    if k != eng:
                        add(k, v)
        return deps

    def _wait(self, eng, deps):
        for k, v in deps.items():
            if self.known[eng].get(k, 0) >= v:
                continue
            self.prog[eng].append(("w", k, v))
            self.known[eng][k] = v

    def _record(self, key, val, reads, writes):
        for b in reads:
            st = self.bufs.setdefault(b, {"w": None, "r": {}})
            if st["r"].get(key, 0) < val:
                st["r"][key] = val
        for b in writes:
            self.bufs[b] = {"w": (key, val), "r": {}}

    def op(self, eng, fn, reads=(), writes=(), n=128):
        if self.cap is not None:
            self.cap.append(("op", eng, fn, tuple(reads), tuple(writes), n))
            return
        self.sim_commit(("op", eng, fn, tuple(reads), tuple(writes), n))
        self._wait(eng, self._deps(eng, reads, writes, False))
        self.cnt[eng] += 1
        self.prog[eng].append(("i", fn, eng, 1))
        self._record(eng, self.cnt[eng], reads, writes)
        for b in list(reads) + list(writes):
            bank = self.bank_of.get(b)
            if bank is not None:
                self.bank_last[bank][eng] = self.cnt[eng]
        self.ninst += 1

    def dma(self, q, key, fn, reads=(), writes=(), inc=16, n=128):
        if self.cap is not None:
            self.cap.append(("dma", q, key, fn, tuple(reads), tuple(writes), inc, n))
            return
        self.sim_commit(("dma", q, key, fn, tuple(reads), tuple(writes), inc, n))
        if key not in self.sem:
            self._mk(key)
        self._wait(q, self._deps(q, reads, writes, True))
        self.cnt[key] += inc
        self.prog[q].append(("i", fn, key, inc))
        self._record(key, self.cnt[key], reads, writes)
        self.ninst += 1

    def mark(self, key):
        if self.cap is not None:
            self.cap.append(("mark", key))

    def need(self, key):
        if self.cap is not None:
            self.cap.append(("need", key))

    def group_begin(self):
        if self.cap is not None:
            self._gstack = self.cap
            self.cap = []

    def group_end(self):
        if self.cap is not None:
            items = self.cap
            self.cap = self._gstack
            self.cap.append(("group", items))

    def capture(self, f, *a):
        assert self.cap is None
        self.cap = []
        f(*a)
        out, self.cap = self.cap, None
        return out

    def emit(self, it):
        if it[0] == "mark":
            self.marks.add(it[1])
        elif it[0] == "need":
            assert it[1] in self.marks, it
        elif it[0] == "group":
            for x in it[1]:
                self.emit(x)
        elif it[0] == "op":
            self.op(it[1], it[2], it[3], it[4], it[5])
        else:
            self.dma(it[1], it[2], it[3], it[4], it[5], it[6], it[7])

    FIX = {"pe": 0.06, "act": 0.2, "dve": 0.12, "pool": 0.3, "sp": 0.05}
    PER = {"pe": 0.00042, "act": 0.0009, "dve": 0.0012, "pool": 0.0008, "sp": 0.0}
    LAT = 0.45

    def _sim_init(self):
        if not hasattr(self, "ef"):
            self.ef = {e: 0.0 for e in self.eng}
            self.tw, self.trd = {}, {}
            self.bank_t = {i: {} for i in range(8)}

    def _first(self, it):
        while it[0] == "group":
            it = it[1][0]
        return it

    def est_start(self, it):
        self._sim_init()
        it = self._first(it)
        eng = it[1]
        reads, writes = (it[3], it[4]) if it[0] == "op" else (it[4], it[5])
        t = self.ef[eng]

        def rdy(tt_e):
            tt, e2 = tt_e
            return tt + (self.LAT if e2 != eng else 0.03)
        for b_ in reads:
            if b_ in self.tw:
                t = max(t, rdy(self.tw[b_]))
        for b_ in writes:
            if b_ in self.tw:
                t = max(t, rdy(self.tw[b_]))
            for e2, tt in self.trd.get(b_, {}).items():
                t = max(t, rdy((tt, e2)))
        for b_ in list(reads) + list(writes):
            bank = self.bank_of.get(b_)
            if bank is not None:
                for e2, tt in self.bank_t[bank].items():
                    if e2 != eng:
                        t = max(t, tt + self.LAT)
        return t

    def sim_commit(self, it):
        self._sim_init()
        eng = it[1]
        if it[0] == "op":
            reads, writes, n = it[3], it[4], it[5]
            start = self.est_start(it)
            end = start + self.FIX[eng] + self.PER[eng] * n
            self.ef[eng] = end
            who = eng
        else:
            reads, writes, n = it[4], it[5], it[7]
            start = self.est_start(it)
            self.ef[eng] = start + 0.06
            end = start + 2.0 + 0.002 * n
            who = "dma"
        for b_ in reads:
            d = self.trd.setdefault(b_, {})
            d[who] = max(d.get(who, 0.0), end)
        for b_ in writes:
            self.tw[b_] = (end, who)
            self.trd[b_] = {}
        for b_ in list(reads) + list(writes):
            bank = self.bank_of.get(b_)
            if bank is not None:
                self.bank_t[bank][eng] = end

    def interleave(self, streams, weights=None):
        pos = [0] * len(streams)
        marks = self.marks
        while True:
            best, bt = -1, 1e30
            for k, s in enumerate(streams):
                while pos[k] < len(s) and (s[pos[k]][0] == "mark" or
                                           (s[pos[k]][0] == "need" and s[pos[k]][1] in marks)):
                    if s[pos[k]][0] == "mark":
                        marks.add(s[pos[k]][1])
                    pos[k] += 1
                if pos[k] < len(s) and s[pos[k]][0] != "need":
                    t = self.est_start(s[pos[k]])
                    if t < bt:
                        best, bt = k, t
            if best < 0:
                assert all(pos[k] >= len(s) for k, s in enumerate(streams)), "interleave deadlock"
                return
            self.emit(streams[best][pos[best]])
            pos[best] += 1

    def wait_all(self, eng, keys):
        for k in keys:
            if self.cnt[k] > 0:
                self.prog[eng].append(("w", k, self.cnt[k]))

    def replay(self, eng, e):
        for it in self.prog[eng]:
            if it[0] == "w":
                e.wait_ge(self.sem[it[1]], it[2])
            else:
                inst = it[1](e)
                inst.then_inc(self.sem[it[2]], it[3])


def build_nc(S, NSEG, cc_inc=1):
    NT = S // 128
    NM = S // 512
    TS = NT // NSEG
    SEG = S // NSEG
    MPS = NM // NSEG
    assert NM * 512 == S and MPS * NSEG == NM

    nc = bass.Bass("TRN2", target_bir_lowering=False)
    x_d = nc.dram_tensor("x_b", [S, D], F32, kind="ExternalInput").ap()
    win_d = nc.dram_tensor("w_in_c", [D, NCOL], F32, kind="ExternalInput").ap()
    wup_d = nc.dram_tensor("wup_c", [16, 128], F32, kind="ExternalInput").ap()
    wout_d = nc.dram_tensor("wout_c", [2048, 256], F32, kind="ExternalInput").ap()
    xres_d = nc.dram_tensor("xres_c", [S, 256], F32, kind="ExternalInput").ap()
    small_d = nc.dram_tensor("small_c", [128, NSMALL], F32, kind="ExternalInput").ap()
    y_d = nc.dram_tensor("y_c", [S, 256], F32, kind="ExternalOutput").ap()
    hsend = [nc.dram_tensor(f"hsend{g}", [512, SEG], BF16) for g in range(NSEG)]
    hall = [nc.dram_tensor(f"hall{g}", [2048, SEG], BF16) for g in range(NSEG)]
    ssq_i = [nc.dram_tensor(f"ssqi{g}", [128, TS], F32) for g in range(NSEG)]
    ssq_o = [nc.dram_tensor(f"ssqo{g}", [128, TS], F32) for g in range(NSEG)]
    GROUPS = [[0, 1, 2, 3], [4, 5, 6, 7]]
    import os
    CCQOS = os.environ.get('KQOS', 'P2')
    CCQOS = None if CCQOS == 'none' else CCQOS

    with contextlib.ExitStack() as st:
        def sb(name, shape, dt):
            return st.enter_context(nc.sbuf_tensor(name, shape, dt))

        def ps(name, shape, dt):
            return st.enter_context(nc.psum_tensor(name, shape, dt))

        xT_ps = ps("xT_ps", [128, 8, 128], BF16)
        PB = [ps(f"PB{i}", [128, 512], F32) for i in range(3)]
        MB0 = ps("MB0", [128, 512], F32)
        MB1a = ps("MB1a", [128, 512], F32)
        MB2a = ps("MB2a", [128, 512], F32)
        MB3 = ps("MB3", [128, 512], F32)
        UREP, ATG = MB0[:, 0:128], MB0[:, 384:512]
        GC0, GC1 = MB2a[:, 260:262], MB2a[:, 262:264]
        HTPM = MB0[:, 128:256].bitcast(BF16).rearrange("p (b t) -> p b t", b=2)
        HTPG = MB0[:, 256:384].bitcast(BF16).rearrange("p (b t) -> p b t", b=2)
        ATM, NUM = MB1a[:, 0:128], MB1a[:, 128:386]
        KTM = MB1a[:, 448:512].bitcast(BF16)
        DC = MB2a[:, 0:258]
        KTG = MB2a[:, 448:512].bitcast(BF16)
        OG, DS = MB3[:, 0:256], MB3[:, 256:512]

        W = sb("W", [128, 8, NCOL], BF16)
        Wout = sb("Wout", [128, 16, 256], BF16)
        wup = sb("wup", [16, 128], BF16)
        small = sb("small", [128, NSMALL], F32)
        ident_b = sb("ident_b", [128, 128], BF16)
        ident_f = sb("ident_f", [128, 128], F32)
        tri_f = sb("tri_f", [128, 128], F32)
        ones_f = sb("ones_f", [128, 128], F32)
        nbgk = sb("nbgk", [128, 1], F32)
        nbf = sb("nbf", [128, 1], F32)
        nwq = sb("nwq", [128, 256], F32)
        gnwh = sb("gnwh", [128, 256], F32)
        NEGH = sb("negh", [128, 8], F32)
        EPS_AP = sb("eps_ap", [128, 1], F32)
        ONE_AP = sb("one_ap", [128, 1], F32)

        NXT = 4
        xt = [sb(f"xt{i}", [128, D], F32) for i in range(NXT)]
        junk = sb("junk", [128, D], BF16)
        xs = [sb(f"xs{i}", [128, D], BF16) for i in range(2)]
        xT = [sb(f"xT{i}", [128, 8, 512], BF16) for i in range(2)]
        st_small = sb("st_small", [128, 8], F32)
        qkraw = [sb(f"qkraw{s}", [128, 2, 515], F32) for s in range(2)]
        cv = [sb(f"cv{s}", [128, 2, 512], F32) for s in range(2)]
        q_bf = [sb(f"q_bf{s}", [128, 512], BF16) for s in range(2)]
        kh_bf = [sb(f"kh_bf{s}", [128, 512], BF16) for s in range(2)]
        qg_bf = [sb(f"qg_bf{s}", [128, 512], BF16) for s in range(2)]
        kg_bf = [sb(f"kg_bf{s}", [128, 512], BF16) for s in range(2)]
        eBLc = [sb(f"eBLc{s}", [128, 4], F32) for s in range(3)]
        gkl_bf = sb("gkl_bf", [16, 512], BF16)
        spg = sb("spg", [128, 512], F32)
        nbc = sb("nbc", [128, 512], F32)
        eB = sb("eB", [128, 512], F32)
        eNB = sb("eNB", [128, 512], F32)
        Vm = [[sb(f"Vm{s}_{i}", [128, 258], BF16) for i in range(4)] for s in range(2)]
        Vg = [[sb(f"Vg{s}_{i}", [128, 256], BF16) for i in range(4)] for s in range(2)]
        zbuf = [sb(f"zbuf{s}", [128, 4, 512], F32) for s in range(2)]
        obuf = [sb(f"obuf{s}", [128, 4, 256], F32) for s in range(2)]
        iftok = [sb(f"iftok{s}", [128, 4, 2], F32) for s in range(2)]
        spf4 = [sb(f"spf4_{s}", [128, 4], F32) for s in range(2)]
        ef4 = sb("ef4", [128, 4], F32)
        tcv = sb("tcv", [128, 2, 512], F32)
        tz = [tcv[:, j, :] for j in range(2)]
        gsc = [sb(f"gsc{i}", [128, 16], F32) for i in range(4)]
        igd = sb("igd", [128, 128], F32)
        rmat = sb("rmat", [128, 128], F32)
        iot = rmat
        wrep = [sb(f"wrep{i}", [128, 128], F32) for i in range(4)]
        oscg = sb("oscg", [128, 16], F32)
        numS = [sb(f"numS{i}", [128, 258], F32) for i in range(2)]
        ogS = [sb(f"ogS{i}", [128, 256], F32) for i in range(2)]
        mst = [sb(f"mst{i}", [128, 1], F32) for i in range(2)]
        Cst = sb("Cst", [128, 258], F32)
        Ch_bf = sb("Ch_bf", [128, 258], BF16)
        Ust = sb("Ust", [128, 256], F32)
        Sh_bf = sb("Sh_bf", [128, 256], BF16)
        ATm_bf = sb("ATm_bf", [128, 128], BF16)
        ATg_bf = sb("ATg_bf", [128, 128], BF16)
        km_tok = sb("km_tok", [128, 128], BF16)
        kg_tok = sb("kg_tok", [128, 128], BF16)
        htile = sb("htile", [128, 512], BF16)
        osc = sb("osc", [128, 16], F32)
        hTst = [sb(f"hTst{i}", [128, 4, SEG], BF16) for i in range(2)]
        hTall = sb("hTall", [128, 16, SEG], BF16)
        xr = [sb(f"xr{i}", [128, 256], F32) for i in range(TS)]
        ypre = [sb(f"ypre{i}", [128, TS, 256], F32) for i in range(2)]
        ssq = [sb(f"ssq{i}", [128, TS], F32) for i in range(2)]
        ssr = sb("ssr", [128, TS], F32)
        rfin = sb("rfin", [128, TS], F32)
        yo = [sb(f"yo{i}", [128, 256], F32) for i in range(TS)]

        S_ = Sched(nc, st)
        op, dma = S_.op, S_.dma

        def smallc(c0, n=1):
            return small[:, c0:c0 + n]

        dma("sp", "wld", lambda e: e.dma_start(out=small[:, :], in_=small_d[:, :]),
            writes=["small"])
        dma("pool", "wout_ld", lambda e: e.dma_start(out=Wout[:, :, :],
                                                 in_=wout_d.rearrange("(k p) n -> p k n", p=128)),
            writes=["Wout"])
        dma("pool", "wup_ld", lambda e: e.dma_start(out=wup[:, :], in_=wup_d[:, :]), writes=["wup"])
        op("pool", lambda e: e.iota(iot[:, :], [[1, 128]], base=0, channel_multiplier=-1,
                                    allow_small_or_imprecise_dtypes=True), writes=["iot"])
        op("dve", lambda e: e.tensor_single_scalar(tri_f[:, :], iot[:, :], 0.0, ALU.is_ge),
           reads=["iot"], writes=["tri_f"])
        op("dve", lambda e: e.tensor_single_scalar(ident_f[:, :], iot[:, :], 0.0, ALU.is_equal),
           reads=["iot"], writes=["ident_f"])
        op("dve", lambda e: e.tensor_copy(ident_b[:, :], ident_f[:, :]),
           reads=["ident_f"], writes=["ident_b"])
        op("pool", lambda e: e.memset(ones_f[:, :], 1.0), writes=["ones_f"])
        op("pool", lambda e: e.memset(qkraw[1][:, :, 512:515], 0.0), writes=[("qkraw", 1, 0), ("qkraw", 1, 1)])
        op("pool", lambda e: e.memset(Cst[:, :], 0.0), writes=["Cst"])
        op("pool", lambda e: e.memset(Ust[:, :], 0.0), writes=["Ust"])
        op("pool", lambda e: e.memset(mst[0][:, :], NEG_INIT), writes=[("mst", 0)])
        op("pool", lambda e: e.memset(eBLc[2][:, :], 1.0), writes=[("eBLc", 2)])
        op("pool", lambda e: e.memset(EPS_AP[:, :], EPS), writes=["eps_ap"])
        op("pool", lambda e: e.memset(ONE_AP[:, :], 1.0), writes=["one_ap"])
        for s_ in range(2):
            for i in range(4):
                op("pool", lambda e, s_=s_, i=i: e.memset(Vm[s_][i][:, 256:258], 1.0),
                   writes=[("Vm1", s_, i)])
        op("dve", lambda e: e.tensor_scalar(nbgk[:, :], smallc(SP_BGK), -1.0, None, ALU.mult),
           reads=["small"], writes=["nbgk"])
        op("dve", lambda e: e.tensor_scalar(nbf[:, :], smallc(SP_BF), -1.0, None, ALU.mult),
           reads=["small"], writes=["nbf"])
        op("dve", lambda e: e.tensor_scalar(nwq[:, :], smallc(SP_MNW, 256), 0.25, None, ALU.mult),
           reads=["small"], writes=["nwq"])
        op("dve", lambda e: e.tensor_scalar(gnwh[:, :], smallc(SP_GNW, 256), 0.5, None, ALU.mult),
           reads=["small"], writes=["gnwh"])
        op("pool", lambda e: e.memset(NEGH[:, :], -0.5), writes=["negh"])
        win_v = win_d.rearrange("(k p) n -> p k n", p=128)
        WG = [(0, 528), (528, 1040), (1040, 1552), (1552, NCOL)]
        for gi, (c0, c1) in enumerate(WG):
            dma("pool", "wg%d" % gi, lambda e, c0=c0, c1=c1: e.dma_start(
                out=W[:, :, c0:c1], in_=win_v[:, :, c0:c1]), writes=[("W", gi)], n=4000)
        Wall = [("W", gi) for gi in range(4)]
        for e_ in ("act", "dve", "pe", "pool"):
            S_._wait(e_, {"pool": S_.cnt["pool"], "dve": S_.cnt["dve"]})

        pb_rr = [0]

        def next_pb():
            b = pb_rr[0]
            pb_rr[0] = (b + 1) % 2
            return b

        def a_tile(m, i):
            slot = m % 2
            ti = 4 * m + i
            xs_ = ti % 2
            xq = ti % NXT
            op("act", lambda e: e.activation(
                junk[:, :], xt[xq][:, :], AF.Square, accum_out=st_small[:, 0:1]),
               reads=[("xt", xq)], writes=["junk", "junk2", "junk3", "sa0"], n=1024)
            op("dve", lambda e: e.tensor_scalar(
                st_small[:, 1:2], st_small[:, 0:1], 1.0 / D, EPS, ALU.mult, ALU.add),
               reads=["sa0"], writes=["sa1"])
            op("pool", lambda e: e.tensor_tensor(
                st_small[:, 2:3], st_small[:, 1:2], NEGH[:, 0:1], ALU.pow),
               reads=["sa1", "negh"], writes=["sa2"])
            op("dve", lambda e: e.scalar_tensor_tensor(
                xs[xs_][:, :], xt[xq][:, :], st_small[:, 2:3], smallc(SP_NWROW, 1024),
                ALU.mult, ALU.mult),
               reads=[("xt", xq), "sa2", "small"], writes=[("xs", xs_)], n=1024)
            for kt in range(8):
                op("pe", lambda e, kt=kt: e.transpose(
                    xT_ps[:, kt, :], xs[xs_][:, kt * 128:(kt + 1) * 128], ident_b[:, :]),
                   reads=[("xs", xs_), "ident_b"], writes=["xT_ps"])
            op("act", lambda e: e.activation(
                xT[slot][:, :, i * 128:(i + 1) * 128], xT_ps[:, :, :], AF.Copy),
               reads=["xT_ps"], writes=[("xT", slot, i)], n=1024)

        def fm_group(m, col0, ncols):
            slot = m % 2
            b = next_pb()
            S_.group_begin()
            for kt in range(8):
                op("pe", lambda e, kt=kt, b=b: e.matmul(
                    PB[b][0:ncols, :], W[:, kt, col0:col0 + ncols], xT[slot][:, kt, :],
                    start=(kt == 0), stop=(kt == 7)),
                   reads=[("W", 0)] + [("xT", slot, i) for i in range(4)], writes=[("PB", b)], n=512)
            S_.group_end()
            return b

        def macro_fm(m):
            s = m % 2
            b = fm_group(m, FM_QM, 128)
            op("act", lambda e, b=b: e.activation(qkraw[s][:, 0, 3:515], PB[b][:, :], AF.Copy),
               reads=[("PB", b)], writes=[("qkraw", s, 0)], n=512)
            b = fm_group(m, FM_KM, 128)
            op("act", lambda e, b=b: e.activation(qkraw[s][:, 1, 3:515], PB[b][:, :], AF.Copy),
               reads=[("PB", b)], writes=[("qkraw", s, 1)], n=512)
            op("pool", lambda e: e.tensor_copy(qkraw[s][:, :, 0:3], qkraw[1 - s][:, :, 512:515]),
               reads=[("qkraw", 1 - s, 0), ("qkraw", 1 - s, 1)], writes=[("qkraw_h", s)])
            for w_ in range(2):
                op("dve", lambda e, w_=w_: e.tensor_scalar(
                    cv[s][:, w_, :], qkraw[s][:, w_, 0:512], smallc(SP_CONVW + 4 * w_),
                    smallc(SP_CONVB + w_), ALU.mult, ALU.add),
                   reads=[("qkraw", s, w_), ("qkraw_h", s), "small"], writes=[("cv", s, w_)], n=512)
                for j in range(1, 4):
                    op("dve", lambda e, w_=w_, j=j: e.scalar_tensor_tensor(
                        cv[s][:, w_, :], qkraw[s][:, w_, j:j + 512], smallc(SP_CONVW + 4 * w_ + j),
                        cv[s][:, w_, :], ALU.mult, ALU.add),
                       reads=[("qkraw", s, w_), ("qkraw_h", s), "small", ("cv", s, w_)],
                       writes=[("cv", s, w_)], n=512)
            cvn = [("cv", s, 0), ("cv", s, 1)]
            op("act", lambda e: e.activation(tcv[:, :, :], cv[s][:, :, :], AF.Tanh, scale=0.5),
               reads=cvn, writes=["tcv", ("tz", 0, 0), ("tz", 0, 1), ("tz", 1, 0), ("tz", 1, 1)], n=1024)
            op("dve", lambda e: e.scalar_tensor_tensor(
                cv[s][:, :, :], tcv[:, :, :], 1.0, cv[s][:, :, :], ALU.add, ALU.mult),
               reads=cvn + ["tcv", ("tz", 0, 0), ("tz", 0, 1), ("tz", 1, 0), ("tz", 1, 1)], writes=cvn, n=1024)
            op("pool", lambda e: e.tensor_scalar(q_bf[s][:, :], cv[s][:, 0, :], 0.5 * QSCALE, 1.0,
                                                 ALU.mult, ALU.mult),
               reads=[("cv", s, 0)], writes=[("q_bf", s)], n=512)

        def macro_gla(m):
            s = m % 2
            b = fm_group(m, FM_GKL, 16)
            op("act", lambda e, b=b: e.activation(gkl_bf[:, :], PB[b][0:16, :], AF.Copy),
               reads=[("PB", b)], writes=["gkl_bf"])
            b = next_pb()
            op("pe", lambda e, b=b: e.matmul(PB[b][:, :], wup[:, :], gkl_bf[:, :],
                                             start=True, stop=True),
               reads=["wup", "gkl_bf"], writes=[("PB", b)], n=512)
            op("act", lambda e, b=b: e.activation(spg[:, :], PB[b][:, :], AF.Exp,
                                                  bias=nbgk[:, :], scale=-1.0),
               reads=[("PB", b), "nbgk"], writes=["spg"], n=512)
            ifn = [("iftok", s, i) for i in range(4)]
            op("act", lambda e: e.activation(ef4[:, :], iftok[s][:, :, 1], AF.Exp,
                                             bias=nbf[:, :], scale=-1.0),
               reads=ifn + ["nbf"], writes=["ef4"])
            S_.group_begin()
            op("act", lambda e: e.activation(spg[:, :], spg[:, :], AF.Ln, bias=ONE_AP[:, :], scale=1.0),
               reads=["spg", "one_ap"], writes=["spg"], n=1800)
            op("act", lambda e: e.activation(spf4[s][:, :], ef4[:, :], AF.Ln, bias=ONE_AP[:, :], scale=1.0),
               reads=["ef4", "one_ap"], writes=[("spf4", s)], n=1500)
            S_.group_end()
            for c in range(4):
                op("dve", lambda e, c=c: e.tensor_tensor_scan(
                    nbc[:, c * 128:(c + 1) * 128], ones_f[:, :], spg[:, c * 128:(c + 1) * 128],
                    0.0, ALU.mult, ALU.add),
                   reads=["spg", "ones_f"], writes=[("nbc", c)], n=256)
            nbcs = [("nbc", c) for c in range(4)]
            op("act", lambda e: e.activation(eB[:, :], nbc[:, :], AF.Exp, scale=-1.0 / 16.0),
               reads=nbcs, writes=["eB"], n=512)
            op("act", lambda e: e.activation(eNB[:, :], nbc[:, :], AF.Exp, scale=1.0 / 16.0),
               reads=nbcs, writes=["eNB"], n=512)
            op("pool", lambda e: e.tensor_copy(
                eBLc[m % 3][:, :], eB[:, :].rearrange("p (c t) -> p c t", t=128)[:, :, 127]),
               reads=["eB"], writes=[("eBLc", m % 3)])
            b = fm_group(m, FM_QG, 128)
            op("dve", lambda e, b=b: e.scalar_tensor_tensor(
                qg_bf[s][:, :], PB[b][:, :], QSCALE, eB[:, :], ALU.mult, ALU.mult),
               reads=[("PB", b), "eB"], writes=[("qg_bf", s)], n=512)
            b = fm_group(m, FM_KG, 128)
            op("dve", lambda e, b=b: e.tensor_tensor(kg_bf[s][:, :], PB[b][:, :], eNB[:, :], ALU.mult),
               reads=[("PB", b), "eNB"], writes=[("kg_bf", s)], n=512)

        def tm_tile(m, i):
            s = m % 2
            tsl = slice(i * 128, (i + 1) * 128)
            for (c0, n) in ((TM0, 512), (TM1, 512), (TM2, 258)):
                b = next_pb()
                S_.group_begin()
                for kt in range(8):
                    op("pe", lambda e, kt=kt, b=b, c0=c0, n=n: e.matmul(
                        PB[b][:, 0:n], xT[s][:, kt, tsl], W[:, kt, c0:c0 + n],
                        start=(kt == 0), stop=(kt == 7)),
                       reads=[("W", {TM0: 1, TM1: 2, TM2: 3}[c0]), ("xT", s, i)], writes=[("PB", b)], n=n)
                S_.group_end()
                j = i % 2
                if c0 == TM0:
                    op("dve", lambda e, b=b: e.tensor_copy(Vm[s][i][:, 0:256], PB[b][:, 0:256]),
                       reads=[("PB", b)], writes=[("Vm", s, i)], n=256)
                    op("act", lambda e, b=b: e.activation(obuf[s][:, i, :], PB[b][:, 256:512],
                                                          AF.Tanh, scale=0.5),
                       reads=[("PB", b)], writes=[("obuf", s, i)], n=256)
                elif c0 == TM1:
                    op("act", lambda e, b=b: e.activation(tz[j][:, 0:256], PB[b][:, 0:256],
                                                          AF.Tanh, scale=0.5),
                       reads=[("PB", b)], writes=[("tz", j, 0)], n=256)
                    op("dve", lambda e, b=b: e.scalar_tensor_tensor(
                        zbuf[s][:, i, 0:256], tz[j][:, 0:256], 1.0, PB[b][:, 0:256], ALU.add, ALU.mult),
                       reads=[("PB", b), ("tz", j, 0)], writes=[("zm", s, i)], n=256)
                    op("dve", lambda e, b=b: e.tensor_copy(Vg[s][i][:, :], PB[b][:, 256:512]),
                       reads=[("PB", b)], writes=[("Vg", s, i)], n=256)
                else:
                    op("act", lambda e, b=b: e.activation(tz[j][:, 256:512], PB[b][:, 0:256],
                                                          AF.Tanh, scale=0.5),
                       reads=[("PB", b)], writes=[("tz", j, 1)], n=256)
                    op("dve", lambda e, b=b: e.scalar_tensor_tensor(
                        zbuf[s][:, i, 256:512], tz[j][:, 256:512], 1.0, PB[b][:, 0:256], ALU.add, ALU.mult),
                       reads=[("PB", b), ("tz", j, 1)], writes=[("zg", s, i)], n=256)
                    op("dve", lambda e, b=b: e.tensor_copy(iftok[s][:, i, :], PB[b][:, 256:258]),
                       reads=[("PB", b)], writes=[("iftok", s, i)])
            gprod(s, i)

        def gprod(s, i):
            op("pool", lambda e: e.tensor_tensor(zbuf[s][:, i, 0:256], zbuf[s][:, i, 0:256],
                                                 nwq[:, :], ALU.mult),
               reads=[("zm", s, i), "nwq"], writes=[("zm", s, i)], n=256)
            op("pool", lambda e: e.tensor_scalar(obuf[s][:, i, :], obuf[s][:, i, :], 1.0, 1.0,
                                                 ALU.mult, ALU.add),
               reads=[("obuf", s, i)], writes=[("obuf", s, i)], n=256)
            op("pool", lambda e: e.tensor_tensor(obuf[s][:, i, :], obuf[s][:, i, :],
                                                 zbuf[s][:, i, 0:256], ALU.mult),
               reads=[("obuf", s, i), ("zm", s, i)], writes=[("obuf", s, i)], n=256)
            op("pool", lambda e: e.tensor_tensor(zbuf[s][:, i, 256:512], zbuf[s][:, i, 256:512],
                                                 gnwh[:, :], ALU.mult),
               reads=[("zg", s, i), "gnwh"], writes=[("zg", s, i)], n=256)

        def x_loads(m):
            for i in range(4):
                x_load(4 * m + i)

        def x_load(ti):
            xq = ti % NXT
            dma("sp", ("xld", xq), lambda e: e.dma_start(
                out=xt[xq][:, :], in_=x_d[ti * 128:(ti + 1) * 128, :]), writes=[("xt", xq)], n=1024)

        def stage_a_stream(m):
            for i in range(4):
                a_tile(m, i)

        def fmtm_stream(m):
            macro_fm(m)
            for i in range(4):
                tm_tile(m, i)
            macro_gla(m)

        def cidx(m, i):
            return 4 * m + i

        def gates(m, i):
            s = m % 2
            cn = cidx(m, i)
            p = cn % 4
            G = gsc[p]
            mp, mn = mst[cn % 2], mst[(cn + 1) % 2]
            mpn, mnn = ("mst", cn % 2), ("mst", (cn + 1) % 2)
            if cn >= 4:
                S_.need(("M", cn - 4))
            op("dve", lambda e: e.tensor_copy(G[:, 1:2], spf4[s][:, i:i + 1]),
               reads=[("spf4", s)], writes=[("g1", p)])
            op("dve", lambda e: e.tensor_scalar(G[:, 2:3], iftok[s][:, i, 0:1], smallc(SP_BI), None,
                                                ALU.add),
               reads=[("iftok", s, i), "small"], writes=[("g2", p)])
            op("pool", lambda e: e.tensor_scalar(igd[:, :], ident_f[:, :], G[:, 2:3], 1.0,
                                                 ALU.mult, ALU.mult),
               reads=["ident_f", ("g2", p)], writes=["igd"])
            op("dve", lambda e: e.scalar_tensor_tensor(rmat[:, :], tri_f[:, :], G[:, 1:2], igd[:, :],
                                                       ALU.mult, ALU.add),
               reads=["tri_f", ("g1", p), "igd"], writes=["rmat"])
            op("pe", lambda e: e.matmul(UREP, ones_f[:, :], rmat[:, :], start=True, stop=True),
               reads=["rmat", "ones_f"], writes=["urep"])
            op("pe", lambda e: e.matmul(GC0, tri_f[:, :], G[:, 1:3], start=True, stop=True),
               reads=["tri_f", ("g1", p), ("g2", p)], writes=["gc0"])
            op("pe", lambda e: e.matmul(GC1, ones_f[:, :], G[:, 1:3], start=True, stop=True),
               reads=["ones_f", ("g1", p), ("g2", p)], writes=["gc1"])
            op("dve", lambda e: e.tensor_reduce(G[:, 3:4], UREP, AX.X, ALU.max),
               reads=["urep"], writes=[("g3", p)])
            op("dve", lambda e: e.tensor_tensor(G[:, 4:5], G[:, 3:4], mp[:, :], ALU.max),
               reads=[("g3", p), mpn], writes=[("g4", p)])
            op("dve", lambda e: e.tensor_scalar(G[:, 5:6], G[:, 4:5], -1.0, None, ALU.mult),
               reads=[("g4", p)], writes=[("g5", p)])
            op("act", lambda e: e.activation(wrep[p][:, :], UREP, AF.Exp, bias=G[:, 5:6], scale=1.0),
               reads=["urep", ("g5", p)], writes=[("wrep", p)])
            op("act", lambda e: e.activation(G[:, 6:7], mp[:, :], AF.Exp, bias=G[:, 5:6], scale=1.0),
               reads=[mpn, ("g5", p)], writes=[("g6", p)])
            op("act", lambda e: e.activation(G[:, 7:8], MB2a[:, 260:261], AF.Exp, bias=G[:, 5:6], scale=1.0),
               reads=["gc0", ("g5", p)], writes=[("g7", p)])
            op("dve", lambda e: e.scalar_tensor_tensor(mn[:, :], MB2a[:, 262:263], -1.0, G[:, 4:5],
                                                       ALU.mult, ALU.add),
               reads=["gc1", ("g4", p)], writes=[mnn])
            S_.mark(("G", cn))

        def hsl(m, i):
            g = m // MPS
            return g % 2, (4 * m + i) * 128 - g * SEG

        def mlstm(m, i):
            s = m % 2
            p = cidx(m, i) % 4
            G = gsc[p]
            csl = slice(i * 128, (i + 1) * 128)
            hslot, tok0 = hsl(m, i)
            p2 = cidx(m, i) % 2
            S_.need(("G", cidx(m, i)))
            if cidx(m, i) >= 2:
                S_.need(("M", cidx(m, i) - 2))
            op("dve", lambda e: e.scalar_tensor_tensor(kh_bf[s][:, csl], cv[s][:, 1, csl], 0.5,
                                                       wrep[p][:, :], ALU.mult, ALU.mult),
               reads=[("cv", s, 1), ("wrep", p)], writes=[("kh_bf", s, i)])
            op("pe", lambda e: e.matmul(ATM, kh_bf[s][:, csl], q_bf[s][:, csl], start=True, stop=True),
               reads=[("kh_bf", s, i), ("q_bf", s)], writes=["ATm"])
            op("pe", lambda e: e.transpose(KTM, kh_bf[s][:, csl], ident_b[:, :]),
               reads=[("kh_bf", s, i), "ident_b"], writes=["kTm"])
            op("dve", lambda e: e.tensor_tensor(ATm_bf[:, :], ATM, tri_f[:, :], ALU.mult),
               reads=["ATm", "tri_f"], writes=["ATm_bf"])
            op("act", lambda e: e.activation(km_tok[:, :], KTM, AF.Copy),
               reads=["kTm"], writes=["km_tok"])
            op("pool", lambda e: e.tensor_scalar(Ch_bf[:, :], Cst[:, :], G[:, 6:7], 1.0,
                                                 ALU.mult, ALU.mult),
               reads=["Cst", ("g6", p)], writes=["Ch_bf"], n=258)
            S_.group_begin()
            op("pe", lambda e: e.matmul(NUM, ATm_bf[:, :], Vm[s][i][:, :], start=True, stop=False),
               reads=["ATm_bf", ("Vm", s, i), ("Vm1", s, i)], writes=["num"], n=258)
            op("pe", lambda e: e.matmul(NUM, q_bf[s][:, csl], Ch_bf[:, :], start=False, stop=True),
               reads=[("q_bf", s), "Ch_bf"], writes=["num"], n=258)
            S_.group_end()
            op("act", lambda e: e.activation(numS[p2][:, :], NUM, AF.Copy),
               reads=["num"], writes=[("numS", p2)], n=258)
            op("pe", lambda e: e.matmul(DC, km_tok[:, :], Vm[s][i][:, :], start=True, stop=True),
               reads=["km_tok", ("Vm", s, i), ("Vm1", s, i)], writes=["dC"], n=258)
            op("dve", lambda e: e.scalar_tensor_tensor(Cst[:, :], Cst[:, :], G[:, 6:7], DC,
                                                       ALU.mult, ALU.add),
               reads=["Cst", ("g6", p), "dC", "Ch_bf"], writes=["Cst"], n=258)
            S_.mark(("A", cidx(m, i)))

        def mlstm_out(m, i):
            s = m % 2
            cn = cidx(m, i)
            p = cn % 4
            p2 = cn % 2
            G = gsc[p]
            hslot, tok0 = hsl(m, i)
            Gm_ = obuf[s][:, i, :]
            NS = numS[p2]
            nsn = ("numS", p2)
            S_.need(("A", cn))
            op("act", lambda e: e.activation(osc[:, 11:12], NS[:, 256:257], AF.Abs),
               reads=[nsn], writes=["o11"])
            op("act", lambda e: e.activation(junk[:, 0:256], NS[:, 0:256], AF.Square, scale=1.0 / 16.0,
                                             accum_out=osc[:, 2:3]),
               reads=[nsn], writes=["junk", "o2"], n=256)
            op("dve", lambda e: e.tensor_tensor(osc[:, 0:1], osc[:, 11:12], G[:, 7:8], ALU.max),
               reads=["o11", ("g7", p)], writes=["o0"])
            op("dve", lambda e: e.tensor_tensor(osc[:, 3:4], osc[:, 0:1], osc[:, 0:1], ALU.mult),
               reads=["o0"], writes=["o3"])
            op("dve", lambda e: e.scalar_tensor_tensor(osc[:, 4:5], osc[:, 3:4], EPS, osc[:, 2:3],
                                                       ALU.mult, ALU.add),
               reads=["o3", "o2"], writes=["o4"])
            op("pool", lambda e: e.tensor_tensor(osc[:, 7:8], osc[:, 4:5], NEGH[:, 1:2], ALU.pow),
               reads=["o4", "negh"], writes=["o7"])
            op("dve", lambda e: e.scalar_tensor_tensor(htile[:, 0:256], NS[:, 0:256], osc[:, 7:8],
                                                       Gm_, ALU.mult, ALU.mult),
               reads=[nsn, "o7", ("obuf", s, i)], writes=["htile_m"], n=256)
            S_.group_begin()
            for bk in range(2):
                op("pe", lambda e, bk=bk: e.transpose(HTPM[:, bk, :],
                                                      htile[:, bk * 128:(bk + 1) * 128], ident_b[:, :]),
                   reads=["htile_m", "ident_b"], writes=["hTm_ps"])
            S_.group_end()
            op("act", lambda e: e.activation(
                hTst[hslot][:, 0:2, tok0:tok0 + 128], HTPM[:, :, :], AF.Copy),
               reads=["hTm_ps"], writes=[("hTst", hslot, 0)], n=256)
            S_.mark(("M", cidx(m, i)))

        def gla(m, i):
            s = m % 2
            csl = slice(i * 128, (i + 1) * 128)
            hslot, tok0 = hsl(m, i)
            if i == 0:
                dec, decn = eBLc[(m + 2) % 3][:, 3:4], ("eBLc", (m + 2) % 3)
            else:
                dec, decn = eBLc[m % 3][:, i - 1:i], ("eBLc", m % 3)
            if cidx(m, i) >= 2:
                S_.need(("GM", cidx(m, i) - 2))
            op("pe", lambda e: e.matmul(ATG, kg_bf[s][:, csl], qg_bf[s][:, csl], start=True, stop=True),
               reads=[("kg_bf", s), ("qg_bf", s)], writes=["ATg"])
            op("pe", lambda e: e.transpose(KTG, kg_bf[s][:, csl], ident_b[:, :]),
               reads=[("kg_bf", s), "ident_b"], writes=["kTg"])
            op("dve", lambda e: e.tensor_tensor(ATg_bf[:, :], ATG, tri_f[:, :], ALU.mult),
               reads=["ATg", "tri_f"], writes=["ATg_bf"])
            op("act", lambda e: e.activation(kg_tok[:, :], KTG, AF.Copy),
               reads=["kTg"], writes=["kg_tok"])
            op("pool", lambda e: e.tensor_scalar(Sh_bf[:, :], Ust[:, :], dec, 1.0, ALU.mult, ALU.mult),
               reads=["Ust", decn], writes=["Sh_bf"], n=256)
            S_.group_begin()
            op("pe", lambda e: e.matmul(OG, ATg_bf[:, :], Vg[s][i][:, :], start=True, stop=False),
               reads=["ATg_bf", ("Vg", s, i)], writes=["og"], n=256)
            op("pe", lambda e: e.matmul(OG, qg_bf[s][:, csl], Sh_bf[:, :], start=False, stop=True),
               reads=[("qg_bf", s), "Sh_bf"], writes=["og"], n=256)
            S_.group_end()
            op("act", lambda e: e.activation(ogS[cidx(m, i) % 2][:, :], OG, AF.Copy),
               reads=["og"], writes=[("ogS", cidx(m, i) % 2)], n=256)
            op("pe", lambda e: e.matmul(DS, kg_tok[:, :], Vg[s][i][:, :], start=True, stop=True),
               reads=["kg_tok", ("Vg", s, i)], writes=["dS"], n=256)
            op("dve", lambda e: e.scalar_tensor_tensor(Ust[:, :], Ust[:, :], dec, DS, ALU.mult, ALU.add),
               reads=["Ust", decn, "dS", "Sh_bf"], writes=["Ust"], n=256)
            S_.mark(("GA", cidx(m, i)))

        def gla_out(m, i):
            s = m % 2
            cn = cidx(m, i)
            p2 = cn % 2
            hslot, tok0 = hsl(m, i)
            Gg_ = zbuf[s][:, i, 256:512]
            OS = ogS[p2]
            osn = ("ogS", p2)
            S_.need(("GA", cn))
            op("act", lambda e: e.activation(junk[:, 256:512], OS[:, :], AF.Square, scale=1.0 / 16.0,
                                             accum_out=oscg[:, 0:1]),
               reads=[osn], writes=["junk2", "p0"], n=256)
            op("dve", lambda e: e.tensor_scalar(oscg[:, 1:2], oscg[:, 0:1], EPS, None, ALU.add),
               reads=["p0"], writes=["p1"])
            op("pool", lambda e: e.tensor_tensor(oscg[:, 2:3], oscg[:, 1:2], NEGH[:, 2:3], ALU.pow),
               reads=["p1", "negh"], writes=["p2"])
            op("dve", lambda e: e.scalar_tensor_tensor(htile[:, 256:512], OS[:, :], oscg[:, 2:3],
                                                       Gg_, ALU.mult, ALU.mult),
               reads=[osn, "p2", ("zg", s, i)], writes=["htile_g"], n=256)
            S_.group_begin()
            for bk in range(2):
                op("pe", lambda e, bk=bk: e.transpose(HTPG[:, bk, :],
                                                      htile[:, 256 + bk * 128:256 + (bk + 1) * 128],
                                                      ident_b[:, :]),
                   reads=["htile_g", "ident_b"], writes=["hTg_ps"])
            S_.group_end()
            op("act", lambda e: e.activation(
                hTst[hslot][:, 2:4, tok0:tok0 + 128], HTPG[:, :, :], AF.Copy),
               reads=["hTg_ps"], writes=[("hTst", hslot, 1)], n=256)
            S_.mark(("GM", cidx(m, i)))

        def gates_stream(m):
            for i in range(4):
                gates(m, i)

        def mlstm_stream(m):
            for i in range(4):
                mlstm(m, i)

        def gla_stream(m):
            for i in range(4):
                gla(m, i)

        def mlstm_out_stream(m):
            for i in range(4):
                mlstm_out(m, i)

        def gla_out_stream(m):
            for i in range(4):
                gla_out(m, i)

        def seg_send(g):
            hslot = g % 2
            dma("sp", ("hst", hslot), lambda e: e.dma_start(
                out=hsend[g].ap().rearrange("(b p) t -> p b t", p=128), in_=hTst[hslot][:, :, :]),
                reads=[("hTst", hslot, 0), ("hTst", hslot, 1)], writes=[("hsend", g)])
            dma("pool", "ccA%d" % (g % 2), lambda e: e.collective_compute(
                "AllGather", ALU.bypass, replica_groups=GROUPS,
                ins=[hsend[g].ap().opt()], outs=[hall[g].ap().opt()], dma_qos=CCQOS),
                reads=[("hsend", g)], writes=[("hall", g)], inc=cc_inc, n=15000)

        def outproj_loads(g):
            dma("sp", "hld", lambda e: e.dma_start(
                out=hTall[:, :, :], in_=hall[g].ap().rearrange("(k p) t -> p k t", p=128)),
                reads=[("hall", g)], writes=["hTall"])
            for tt in range(TS):
                xr_load(g, tt)

        def xr_load(g, tt):
            ti = g * TS + tt
            dma("sp", ("xrld", tt), lambda e: e.dma_start(
                out=xr[tt][:, :], in_=xres_d[ti * 128:(ti + 1) * 128, :]), writes=[("xr", tt)])

        def outproj_a(g):
            ys_ = g % 2
            for tt in range(TS):
                outproj_tile(g, tt)
            ssqs = [("ssq", ys_, tt) for tt in range(TS)]
            dma("sp", "sst", lambda e: e.dma_start(out=ssq_i[g].ap(), in_=ssq[ys_][:, :]),
                reads=ssqs, writes=[("ssq_i", g)])
            dma("pool", "ccR%d" % (g % 2), lambda e: e.collective_compute(
                "AllReduce", ALU.add, replica_groups=GROUPS,
                ins=[ssq_i[g].ap().opt()], outs=[ssq_o[g].ap().opt()], dma_qos=CCQOS),
                reads=[("ssq_i", g)], writes=[("ssq_o", g)], inc=cc_inc, n=12000)

        def outproj_tile(g, tt):
            ys_ = g % 2
            ti = g * TS + tt
            xs_ = tt
            for k in range(16):
                op("pe", lambda e, k=k: e.matmul(
                    PB[2][:, 0:256], hTall[:, k, tt * 128:(tt + 1) * 128], Wout[:, k, :],
                    start=(k == 0), stop=(k == 15)),
                   reads=["hTall", "Wout"], writes=[("PB", 2)], n=256)
            op("dve", lambda e: e.tensor_tensor(
                ypre[ys_][:, tt, :], PB[2][:, 0:256], xr[xs_][:, :], ALU.add),
               reads=[("PB", 2), ("xr", xs_)], writes=[("ypre", ys_, tt)], n=256)
            op("act", lambda e: e.activation(junk[:, 512:768], ypre[ys_][:, tt, :], AF.Square,
                                             accum_out=ssq[ys_][:, tt:tt + 1]),
               reads=[("ypre", ys_, tt)], writes=["junk3", ("ssq", ys_, tt)], n=256)

        def outproj_b(g):
            ys_ = g % 2
            dma("sp", "sld", lambda e: e.dma_start(out=ssr[:, :], in_=ssq_o[g].ap()),
                reads=[("ssq_o", g)], writes=["ssr"])
            op("dve", lambda e: e.tensor_scalar(rfin[:, :], ssr[:, :], 1.0 / D, EPS, ALU.mult, ALU.add),
               reads=["ssr"], writes=["rfin0"])
            op("pool", lambda e: e.tensor_tensor(rfin[:, :], rfin[:, :], NEGH[:, 0:TS], ALU.pow),
               reads=["rfin0", "negh"], writes=["rfin"])
            for tt in range(TS):
                outproj_b_tile(g, tt)

        def outproj_b_tile(g, tt):
            ys_ = g % 2
            ti = g * TS + tt
            yb = tt
            op("dve", lambda e: e.scalar_tensor_tensor(
                yo[yb][:, :], ypre[ys_][:, tt, :], rfin[:, tt:tt + 1], smallc(SP_FNW, 256),
                ALU.mult, ALU.mult),
               reads=[("ypre", ys_, tt), "rfin", "small"], writes=[("yo", yb)], n=256)
            dma("sp", ("yst", yb), lambda e: e.dma_start(
                out=y_d[ti * 128:(ti + 1) * 128, :], in_=yo[yb][:, :]),
                reads=[("yo", yb)], writes=[("y", ti)])

        cap = S_.capture
        x_loads(0)
        for it in cap(stage_a_stream, 0):
            S_.emit(it)
        if NM > 1:
            x_loads(1)
        S_.interleave([cap(fmtm_stream, 0)] + ([cap(stage_a_stream, 1)] if NM > 1 else []))
        if NM > 2:
            x_loads(2)
        LAG_A, LAG_B = int(os.environ.get("KLA", "2")), int(os.environ.get("KLB", "3"))
        loads_done = set()
        for m in range(NM):
            g = m // MPS
            import os
            if os.environ.get("KSEQ") == "1":
                def seq_stream(m):
                    for i in range(4):
                        gates(m, i)
                        mlstm(m, i)
                        mlstm_out(m, i)
                        gla(m, i)
                        gla_out(m, i)
                streams = [cap(seq_stream, m)]
            else:
                streams = [cap(gates_stream, m), cap(mlstm_stream, m), cap(gla_stream, m),
                           cap(mlstm_out_stream, m), cap(gla_out_stream, m)]
            WG = float(os.environ.get("KWG", "0.5"))
            WP = float(os.environ.get("KWP", "1.0"))
            weights = [WG, 1.0, 1.0, 1.0, 1.0][:len(streams)] if len(streams) == 5 else [1.0]
            if m + 1 < NM:
                if os.environ.get("KPF", "0") == "1":
                    streams.insert(0, cap(fmtm_stream, m + 1))
                else:
                    streams.append(cap(fmtm_stream, m + 1))
                weights.append(WP)
            last_of_seg = (m + 1) % MPS == 0
            if last_of_seg and g - LAG_A >= 0:
                streams.append(cap(outproj_a, g - LAG_A))
                weights.append(1.0)
            if last_of_seg and g - LAG_B >= 0:
                streams.append(cap(outproj_b, g - LAG_B))
                weights.append(1.0)
            if m + 2 < NM:
                streams.append(cap(stage_a_stream, m + 2))
                weights.append(1.0)
            S_.interleave(streams, weights)
            if m + 3 < NM:
                x_loads(m + 3)
            if last_of_seg:
                if 0 <= g + 1 - LAG_A:
                    outproj_loads(g + 1 - LAG_A)
                    loads_done.add(g + 1 - LAG_A)
                seg_send(g)
        na, nb = max(0, NSEG - LAG_A), max(0, NSEG - LAG_B)
        while na < NSEG or nb < NSEG:
            if na < NSEG:
                if na not in loads_done:
                    outproj_loads(na)
                outproj_a(na)
                na += 1
            while nb < NSEG and (nb < na - 1 or na == NSEG):
                outproj_b(nb)
                nb += 1

        S_.wait_all("sp", list(S_.sem.keys()))
        with nc.Block() as block:
            block.tensor(lambda e: S_.replay("pe", e))
            block.vector(lambda e: S_.replay("dve", e))
            block.scalar(lambda e: S_.replay("act", e))
            block.gpsimd(lambda e: S_.replay("pool", e))
            block.sync(lambda e: S_.replay("sp", e))
        print("ninst", S_.ninst)
    return nc


def _prep_inputs(x, norm_w, w_in, conv_w, conv_b, b_igate, b_fgate, mlstm_norm_w,
                 w_gk_up, b_gk, gla_norm_w, w_out, final_norm_w):
    f = np.float32
    x = np.asarray(x, f)
    w_in = np.asarray(w_in, f)
    o = 0
    offs = {}
    for name, n in (("qm", 512), ("km", 512), ("vm", 1024), ("i", 4), ("f", 4), ("o", 1024),
                    ("zm", 1024), ("qg", 512), ("kg", 512), ("vg", 1024), ("gkl", 16), ("zg", 1024)):
        offs[name] = o
        o += n
    conv_w = np.asarray(conv_w, f)
    conv_b = np.asarray(conv_b, f)
    w_out = np.asarray(w_out, f)
    in_maps = []
    for c in range(8):
        b, j = c // 4, c % 4
        cols = np.concatenate([
            np.arange(offs["qm"] + 128 * j, offs["qm"] + 128 * (j + 1)),
            np.arange(offs["km"] + 128 * j, offs["km"] + 128 * (j + 1)),
            np.arange(offs["qg"] + 128 * j, offs["qg"] + 128 * (j + 1)),
            np.arange(offs["kg"] + 128 * j, offs["kg"] + 128 * (j + 1)),
            np.arange(offs["gkl"], offs["gkl"] + 16),
            np.arange(offs["vm"] + 256 * j, offs["vm"] + 256 * (j + 1)),
            np.arange(offs["o"] + 256 * j, offs["o"] + 256 * (j + 1)),
            np.arange(offs["zm"] + 256 * j, offs["zm"] + 256 * (j + 1)),
            np.arange(offs["vg"] + 256 * j, offs["vg"] + 256 * (j + 1)),
            np.arange(offs["zg"] + 256 * j, offs["zg"] + 256 * (j + 1)),
            np.array([offs["i"] + j, offs["f"] + j]),
        ])
        assert cols.size == NCOL
        w_in_c = np.ascontiguousarray(w_in[:, cols])
        wup_c = np.ascontiguousarray(np.asarray(w_gk_up, f)[:, 128 * j:128 * (j + 1)])
        rows = np.concatenate([np.concatenate([np.arange(256 * r, 256 * (r + 1)),
                                               np.arange(1024 + 256 * r, 1024 + 256 * (r + 1))])
                               for r in range(4)])
        wout_c = np.ascontiguousarray(w_out[rows][:, 256 * j:256 * (j + 1)])
        xres_c = np.ascontiguousarray(x[b][:, 256 * j:256 * (j + 1)])
        small = np.zeros((128, NSMALL), f)
        small[:, SP_NORMW:SP_NORMW + 8] = np.asarray(norm_w, f).reshape(8, 128).T
        small[:, SP_CONVW:SP_CONVW + 4] = conv_w[:, 128 * j:128 * (j + 1)].T
        small[:, SP_CONVW + 4:SP_CONVW + 8] = conv_w[:, 512 + 128 * j:512 + 128 * (j + 1)].T
        small[:, SP_CONVB] = conv_b[128 * j:128 * (j + 1)]
        small[:, SP_CONVB + 1] = conv_b[512 + 128 * j:512 + 128 * (j + 1)]
        small[:, SP_BGK] = np.asarray(b_gk, f)[128 * j:128 * (j + 1)]
        small[:, SP_BI] = np.asarray(b_igate, f)[j]
        small[:, SP_BF] = np.asarray(b_fgate, f)[j]
        small[:, SP_MNW:SP_MNW + 256] = np.asarray(mlstm_norm_w, f)[256 * j:256 * (j + 1)][None, :]
        small[:, SP_GNW:SP_GNW + 256] = np.asarray(gla_norm_w, f)[256 * j:256 * (j + 1)][None, :]
        small[:, SP_FNW:SP_FNW + 256] = np.asarray(final_norm_w, f)[256 * j:256 * (j + 1)][None, :]
        small[:, SP_NWROW:SP_NWROW + 1024] = np.asarray(norm_w, f)[None, :]
        in_maps.append({"x_b": np.ascontiguousarray(x[b]), "w_in_c": w_in_c, "wup_c": wup_c,
                        "wout_c": wout_c, "xres_c": xres_c, "small_c": small})
    return in_maps


def run(inputs, NSEG=None, trace=False):
    x = np.asarray(inputs["x"])
    B, S, _ = x.shape
    assert B == 2
    if NSEG is None:
        NSEG = S // 512
    nc = build_nc(S, NSEG)
    in_maps = _prep_inputs(**inputs)
    res = run_bass_kernel_spmd(nc, in_maps, core_ids=list(range(8)), trace=trace)
    out = np.empty((B, S, D), np.float32)
    for c in range(8):
        b, j = c // 4, c % 4
        out[b, :, 256 * j:256 * (j + 1)] = np.asarray(res.results[c]["y_c"], np.float32)
    return out, res


def kernel(**inputs):
    out, _ = run(inputs)
    return out
```

```python
import contextlib
import numpy as np
import ml_dtypes
import concourse.bass as bass
import concourse.mybir as mybir
from concourse.bass_utils import run_bass_kernel_spmd

import os
SAMELAT = float(os.environ.get("KSL", "0.03"))
F32 = mybir.dt.float32
BF16 = mybir.dt.bfloat16
AF = mybir.ActivationFunctionType
ALU = mybir.AluOpType
AX = mybir.AxisListType

D = 1024
EPS = 1e-6
NEG_INIT = -1e30
QSCALE = 128 ** -0.5
NCOL = 1810
FM_QM, FM_KM, FM_QG, FM_KG, FM_GKL = 0, 128, 256, 384, 512
TM0, TM1, TM2 = 528, 1040, 1552
SP_NORMW, SP_CONVW, SP_CONVB, SP_BGK, SP_BI, SP_BF = 0, 8, 16, 18, 19, 20
SP_MNW, SP_GNW, SP_FNW = 21, 277, 533
SP_NWROW = 789
NSMALL = 789 + 1024


class Sched:
    def __init__(self, nc, stack):
        self.nc = nc
        self.stack = stack
        self.eng = {"pe": nc.tensor, "act": nc.scalar, "dve": nc.vector,
                    "pool": nc.gpsimd, "sp": nc.sync}
        self.sem, self.cnt = {}, {}
        self.known = {e: {} for e in self.eng}
        self.bufs = {}
        self.ninst = 0
        self.prog = {e: [] for e in self.eng}
        self.cap = None
        self.marks = set()
        self.bank_of = {"xT_ps": 0, ("PB", 0): 1, ("PB", 1): 2, ("PB", 2): 3,
                        "urep": 4, "gc0": 6, "gc1": 6, "hTm_ps": 4, "hTg_ps": 4, "ATg": 4,
                        "ATm": 5, "num": 5, "kTm": 5, "dC": 6, "kTg": 6, "og": 7, "dS": 7}
        self.bank_last = {i: {} for i in range(8)}
        for e in self.eng:
            self._mk(e)

    def _mk(self, key):
        self.sem[key] = self.stack.enter_context(self.nc.semaphore("s_" + str(key)))
        self.cnt[key] = 0

    def _deps(self, eng, reads, writes, is_dma):
        deps = {}

        def add(k, v):
            if v > deps.get(k, 0):
                deps[k] = v
        for b in reads:
            st = self.bufs.get(b)
            if st and st["w"]:
                k, v = st["w"]
                if not (k == eng and eng == "pe" and not is_dma):
                    add(k, v)
        for b in writes:
            st = self.bufs.get(b)
            if st:
                if st["w"]:
                    k, v = st["w"]
                    if is_dma or k != eng or eng != "pe":
                        add(k, v)
                for k, v in st["r"].items():
                    if is_dma or k != eng or eng != "pe":
                        add(k, v)
        for b in list(reads) + list(writes):
            bank = self.bank_of.get(b)
            if bank is not None:
                for k, v in self.bank_last[bank].items():
                    if k != eng:
                        add(k, v)
        return deps

    def _wait(self, eng, deps):
        for k, v in deps.items():
            if self.known[eng].get(k, 0) >= v:
                continue
            self.prog[eng].append(("w", k, v))
            self.known[eng][k] = v

    def _record(self, key, val, reads, writes):
        for b in reads:
            st = self.bufs.setdefault(b, {"w": None, "r": {}})
            if st["r"].get(key, 0) < val:
                st["r"][key] = val
        for b in writes:
            self.bufs[b] = {"w": (key, val), "r": {}}

    def op(self, eng, fn, reads=(), writes=(), n=128):
        if self.cap is not None:
            self.cap.append(("op", eng, fn, tuple(reads), tuple(writes), n))
            return
        self.sim_commit(("op", eng, fn, tuple(reads), tuple(writes), n))
        self._wait(eng, self._deps(eng, reads, writes, False))
        self.cnt[eng] += 1
        self.prog[eng].append(("i", fn, eng, 1))
        self._record(eng, self.cnt[eng], reads, writes)
        for b in list(reads) + list(writes):
            bank = self.bank_of.get(b)
            if bank is not None:
                self.bank_last[bank][eng] = self.cnt[eng]
        self.ninst += 1

    def dma(self, q, key, fn, reads=(), writes=(), inc=16, n=128):
        if self.cap is not None:
            self.cap.append(("dma", q, key, fn, tuple(reads), tuple(writes), inc, n))
            return
        self.sim_commit(("dma", q, key, fn, tuple(reads), tuple(writes), inc, n))
        if key not in self.sem:
            self._mk(key)
        self._wait(q, self._deps(q, reads, writes, True))
        self.cnt[key] += inc
        self.prog[q].append(("i", fn, key, inc))
        self._record(key, self.cnt[key], reads, writes)
        self.ninst += 1

    def mark(self, key):
        if self.cap is not None:
            self.cap.append(("mark", key))

    def need(self, key):
        if self.cap is not None:
            self.cap.append(("need", key))

    def group_begin(self):
        if self.cap is not None:
            self._gstack = self.cap
            self.cap = []

    def group_end(self):
        if self.cap is not None:
            items = self.cap
            self.cap = self._gstack
            self.cap.append(("group", items))

    def capture(self, f, *a):
        assert self.cap is None
        self.cap = []
        f(*a)
        out, self.cap = self.cap, None
        return out

    def emit(self, it):
        if it[0] == "mark":
            self.marks.add(it[1])
        elif it[0] == "need":
            assert it[1] in self.marks, it
        elif it[0] == "group":
            for x in it[1]:
                self.emit(x)
        elif it[0] == "op":
            self.op(it[1], it[2], it[3], it[4], it[5])
        else:
            self.dma(it[1], it[2], it[3], it[4], it[5], it[6], it[7])

    FIX = {"pe": float(os.environ.get("KFPE", "0.06")), "act": float(os.environ.get("KFACT", "0.2")),
           "dve": float(os.environ.get("KFDVE", "0.12")), "pool": float(os.environ.get("KFPOOL", "0.3")), "sp": 0.05}
    PER = {"pe": 0.00042, "act": 0.0009, "dve": 0.0012, "pool": 0.0008, "sp": 0.0}
    LAT = 0.8

    def _sim_init(self):
        if not hasattr(self, "ef"):
            self.ef = {e: 0.0 for e in self.eng}
            self.tw, self.trd = {}, {}
            self.bank_t = {i: {} for i in range(8)}

    def _first(self, it):
        while it[0] == "group":
            it = it[1][0]
        return it

    def est_start(self, it):
        self._sim_init()
        it = self._first(it)
        eng = it[1]
        reads, writes = (it[3], it[4]) if it[0] == "op" else (it[4], it[5])
        t = self.ef[eng]

        def rdy(tt_e):
            tt, e2 = tt_e
            return tt + (self.LAT if e2 != eng else SAMELAT)
        for b_ in reads:
            if b_ in self.tw:
                t = max(t, rdy(self.tw[b_]))
        for b_ in writes:
            if b_ in self.tw:
                t = max(t, rdy(self.tw[b_]))
            for e2, tt in self.trd.get(b_, {}).items():
                t = max(t, rdy((tt, e2)))
        for b_ in list(reads) + list(writes):
            bank = self.bank_of.get(b_)
            if bank is not None:
                for e2, tt in self.bank_t[bank].items():
                    if e2 != eng:
                        t = max(t, tt + self.LAT)
        return t

    def sim_commit(self, it):
        self._sim_init()
        eng = it[1]
        if it[0] == "op":
            reads, writes, n = it[3], it[4], it[5]
            start = self.est_start(it)
            end = start + self.FIX[eng] + self.PER[eng] * n
            self.ef[eng] = end
            who = eng
        else:
            reads, writes, n = it[4], it[5], it[7]
            start = self.est_start(it)
            self.ef[eng] = start + 0.06
            end = start + 2.0 + 0.002 * n
            who = "dma"
        for b_ in reads:
            d = self.trd.setdefault(b_, {})
            d[who] = max(d.get(who, 0.0), end)
        for b_ in writes:
            self.tw[b_] = (end, who)
            self.trd[b_] = {}
        for b_ in list(reads) + list(writes):
            bank = self.bank_of.get(b_)
            if bank is not None:
                self.bank_t[bank][eng] = end

    def interleave(self, streams, weights=None):
        pos = [0] * len(streams)
        marks = self.marks
        while True:
            best, bt = -1, 1e30
            for k, s in enumerate(streams):
                while pos[k] < len(s) and (s[pos[k]][0] == "mark" or
                                           (s[pos[k]][0] == "need" and s[pos[k]][1] in marks)):
                    if s[pos[k]][0] == "mark":
                        marks.add(s[pos[k]][1])
                    pos[k] += 1
                if pos[k] < len(s) and s[pos[k]][0] != "need":
                    t = self.est_start(s[pos[k]])
                    if t < bt:
                        best, bt = k, t
            if best < 0:
                assert all(pos[k] >= len(s) for k, s in enumerate(streams)), "interleave deadlock"
                return
            self.emit(streams[best][pos[best]])
            pos[best] += 1

    def wait_all(self, eng, keys):
        for k in keys:
            if self.cnt[k] > 0:
                self.prog[eng].append(("w", k, self.cnt[k]))

    def replay(self, eng, e):
        for it in self.prog[eng]:
            if it[0] == "w":
                e.wait_ge(self.sem[it[1]], it[2])
            else:
                inst = it[1](e)
                inst.then_inc(self.sem[it[2]], it[3])


def build_nc(S, NSEG, cc_inc=1):
    NT = S // 128
    NM = S // 512
    TS = NT // NSEG
    SEG = S // NSEG
    MPS = NM // NSEG
    assert NM * 512 == S and MPS * NSEG == NM

    nc = bass.Bass("TRN2", target_bir_lowering=False)
    x_d = nc.dram_tensor("x_b", [S, D], F32, kind="ExternalInput").ap()
    win_d = nc.dram_tensor("w_in_c", [D, NCOL], F32, kind="ExternalInput").ap()
    wup_d = nc.dram_tensor("wup_c", [16, 128], F32, kind="ExternalInput").ap()
    wout_d = nc.dram_tensor("wout_c", [2048, 256], F32, kind="ExternalInput").ap()
    xres_d = nc.dram_tensor("xres_c", [S, 256], F32, kind="ExternalInput").ap()
    small_d = nc.dram_tensor("small_c", [128, NSMALL], F32, kind="ExternalInput").ap()
    y_d = nc.dram_tensor("y_c", [S, 256], F32, kind="ExternalOutput").ap()
    hsend = [nc.dram_tensor(f"hsend{g}", [512, SEG], BF16) for g in range(NSEG)]
    hall = [nc.dram_tensor(f"hall{g}", [2048, SEG], BF16) for g in range(NSEG)]
    ssq_i = [nc.dram_tensor(f"ssqi{g}", [128, TS], F32) for g in range(NSEG)]
    ssq_o = [nc.dram_tensor(f"ssqo{g}", [128, TS], F32) for g in range(NSEG)]
    GROUPS = [[0, 1, 2, 3], [4, 5, 6, 7]]
    import os
    CCQOS = os.environ.get('KQOS', 'P2')
    CCQOS = None if CCQOS == 'none' else CCQOS

    with contextlib.ExitStack() as st:
        def sb(name, shape, dt):
            return st.enter_context(nc.sbuf_tensor(name, shape, dt))

        def ps(name, shape, dt):
            return st.enter_context(nc.psum_tensor(name, shape, dt))

        xT_ps = ps("xT_ps", [128, 8, 128], BF16)
        PB = [ps(f"PB{i}", [128, 512], F32) for i in range(3)]
        MB0 = ps("MB0", [128, 512], F32)
        MB1a = ps("MB1a", [128, 512], F32)
        MB2a = ps("MB2a", [128, 512], F32)
        MB3 = ps("MB3", [128, 512], F32)
        UREP, ATG = MB0[:, 0:128], MB0[:, 384:512]
        GC0, GC1 = MB2a[:, 260:262], MB2a[:, 262:264]
        HTPM = MB0[:, 128:256].bitcast(BF16).rearrange("p (b t) -> p b t", b=2)
        HTPG = MB0[:, 256:384].bitcast(BF16).rearrange("p (b t) -> p b t", b=2)
        ATM, NUM = MB1a[:, 0:128], MB1a[:, 128:386]
        KTM = MB1a[:, 448:512].bitcast(BF16)
        DC = MB2a[:, 0:258]
        KTG = MB2a[:, 448:512].bitcast(BF16)
        OG, DS = MB3[:, 0:256], MB3[:, 256:512]

        W = sb("W", [128, 8, NCOL], BF16)
        Wout = sb("Wout", [128, 16, 256], BF16)
        wup = sb("wup", [16, 128], BF16)
        small = sb("small", [128, NSMALL], F32)
        ident_b = sb("ident_b", [128, 128], BF16)
        ident_f = sb("ident_f", [128, 128], F32)
        tri_f = sb("tri_f", [128, 128], F32)
        ones_f = sb("ones_f", [128, 128], F32)
        nbgk = sb("nbgk", [128, 1], F32)
        nbf = sb("nbf", [128, 1], F32)
        nwq = sb("nwq", [128, 256], F32)
        gnwh = sb("gnwh", [128, 256], F32)
        NEGH = sb("negh", [128, 8], F32)
        EPS_AP = sb("eps_ap", [128, 1], F32)
        ONE_AP = sb("one_ap", [128, 1], F32)

        NXT = 4
        xt = [sb(f"xt{i}", [128, D], F32) for i in range(NXT)]
        junk = sb("junk", [128, D], BF16)
        xs = [sb(f"xs{i}", [128, D], BF16) for i in range(2)]
        xT = [sb(f"xT{i}", [128, 8, 512], BF16) for i in range(2)]
        st_small = sb("st_small", [128, 8], F32)
        qkraw = [sb(f"qkraw{s}", [128, 2, 515], F32) for s in range(2)]
        cv = [sb(f"cv{s}", [128, 2, 512], F32) for s in range(2)]
        q_bf = [sb(f"q_bf{s}", [128, 512], BF16) for s in range(2)]
        kh_bf = [sb(f"kh_bf{s}", [128, 512], BF16) for s in range(2)]
        qg_bf = [sb(f"qg_bf{s}", [128, 512], BF16) for s in range(2)]
        kg_bf = [sb(f"kg_bf{s}", [128, 512], BF16) for s in range(2)]
        eBLc = [sb(f"eBLc{s}", [128, 4], F32) for s in range(3)]
        gkl_bf = sb("gkl_bf", [16, 512], BF16)
        spg = sb("spg", [128, 512], F32)
        nbc = sb("nbc", [128, 512], F32)
        eB = sb("eB", [128, 512], F32)
        eNB = sb("eNB", [128, 512], F32)
        Vm = [[sb(f"Vm{s}_{i}", [128, 258], BF16) for i in range(4)] for s in range(2)]
        Vg = [[sb(f"Vg{s}_{i}", [128, 256], BF16) for i in range(4)] for s in range(2)]
        zbuf = [sb(f"zbuf{s}", [128, 4, 512], F32) for s in range(2)]
        obuf = [sb(f"obuf{s}", [128, 4, 256], F32) for s in range(2)]
        iftok = [sb(f"iftok{s}", [128, 4, 2], F32) for s in range(2)]
        spf4 = [sb(f"spf4_{s}", [128, 4], F32) for s in range(2)]
        ef4 = sb("ef4", [128, 4], F32)
        tcv = sb("tcv", [128, 2, 512], F32)
        tz = [tcv[:, j, :] for j in range(2)]
        gsc = [sb(f"gsc{i}", [128, 16], F32) for i in range(4)]
        igd = sb("igd", [128, 128], F32)
        rmat = sb("rmat", [128, 128], F32)
        iot = rmat
        wrep = [sb(f"wrep{i}", [128, 128], F32) for i in range(4)]
        oscg = sb("oscg", [128, 16], F32)
        numS = [sb(f"numS{i}", [128, 258], F32) for i in range(2)]
        ogS = [sb(f"ogS{i}", [128, 256], F32) for i in range(2)]
        mst = [sb(f"mst{i}", [128, 1], F32) for i in range(2)]
        Cst = sb("Cst", [128, 258], F32)
        Ch_bf = sb("Ch_bf", [128, 258], BF16)
        Ust = sb("Ust", [128, 256], F32)
        Sh_bf = sb("Sh_bf", [128, 256], BF16)
        ATm_bf = sb("ATm_bf", [128, 128], BF16)
        ATg_bf = sb("ATg_bf", [128, 128], BF16)
        km_tok = sb("km_tok", [128, 128], BF16)
        kg_tok = sb("kg_tok", [128, 128], BF16)
        htile = sb("htile", [128, 512], BF16)
        osc = sb("osc", [128, 16], F32)
        hTst = [sb(f"hTst{i}", [128, 4, SEG], BF16) for i in range(2)]
        hTall = sb("hTall", [128, 16, SEG], BF16)
        xr = [sb(f"xr{i}", [128, 256], F32) for i in range(TS)]
        ypre = [sb(f"ypre{i}", [128, TS, 256], F32) for i in range(2)]
        ssq = [sb(f"ssq{i}", [128, TS], F32) for i in range(2)]
        ssr = sb("ssr", [128, TS], F32)
        rfin = sb("rfin", [128, TS], F32)
        yo = [sb(f"yo{i}", [128, 256], F32) for i in range(TS)]

        S_ = Sched(nc, st)
        op, dma = S_.op, S_.dma

        def smallc(c0, n=1):
            return small[:, c0:c0 + n]

        dma("sp", "wld", lambda e: e.dma_start(out=small[:, :], in_=small_d[:, :]),
            writes=["small"])
        dma("pool", "wout_ld", lambda e: e.dma_start(out=Wout[:, :, :],
                                                 in_=wout_d.rearrange("(k p) n -> p k n", p=128)),
            writes=["Wout"])
        dma("pool", "wup_ld", lambda e: e.dma_start(out=wup[:, :], in_=wup_d[:, :]), writes=["wup"])
        op("pool", lambda e: e.iota(iot[:, :], [[1, 128]], base=0, channel_multiplier=-1,
                                    allow_small_or_imprecise_dtypes=True), writes=["iot"])
        op("dve", lambda e: e.tensor_single_scalar(tri_f[:, :], iot[:, :], 0.0, ALU.is_ge),
           reads=["iot"], writes=["tri_f"])
        op("dve", lambda e: e.tensor_single_scalar(ident_f[:, :], iot[:, :], 0.0, ALU.is_equal),
           reads=["iot"], writes=["ident_f"])
        op("dve", lambda e: e.tensor_copy(ident_b[:, :], ident_f[:, :]),
           reads=["ident_f"], writes=["ident_b"])
        op("pool", lambda e: e.memset(ones_f[:, :], 1.0), writes=["ones_f"])
        op("pool", lambda e: e.memset(qkraw[1][:, :, 512:515], 0.0), writes=[("qkraw", 1, 0), ("qkraw", 1, 1)])
        op("pool", lambda e: e.memset(Cst[:, :], 0.0), writes=["Cst"])
        op("pool", lambda e: e.memset(Ust[:, :], 0.0), writes=["Ust"])
        op("pool", lambda e: e.memset(mst[0][:, :], NEG_INIT), writes=[("mst", 0)])
        op("pool", lambda e: e.memset(eBLc[2][:, :], 1.0), writes=[("eBLc", 2)])
        op("pool", lambda e: e.memset(EPS_AP[:, :], EPS), writes=["eps_ap"])
        op("pool", lambda e: e.memset(ONE_AP[:, :], 1.0), writes=["one_ap"])
        for s_ in range(2):
            for i in range(4):
                op("pool", lambda e, s_=s_, i=i: e.memset(Vm[s_][i][:, 256:258], 1.0),
                   writes=[("Vm1", s_, i)])
        op("dve", lambda e: e.tensor_scalar(nbgk[:, :], smallc(SP_BGK), -1.0, None, ALU.mult),
           reads=["small"], writes=["nbgk"])
        op("dve", lambda e: e.tensor_scalar(nbf[:, :], smallc(SP_BF), -1.0, None, ALU.mult),
           reads=["small"], writes=["nbf"])
        op("dve", lambda e: e.tensor_scalar(nwq[:, :], smallc(SP_MNW, 256), 0.25, None, ALU.mult),
           reads=["small"], writes=["nwq"])
        op("dve", lambda e: e.tensor_scalar(gnwh[:, :], smallc(SP_GNW, 256), 0.5, None, ALU.mult),
           reads=["small"], writes=["gnwh"])
        op("pool", lambda e: e.memset(NEGH[:, :], -0.5), writes=["negh"])
        win_v = win_d.rearrange("(k p) n -> p k n", p=128)
        WG = [(0, 528), (528, 1040), (1040, 1552), (1552, NCOL)]
        for gi, (c0, c1) in enumerate(WG):
            dma("pool", "wg%d" % gi, lambda e, c0=c0, c1=c1: e.dma_start(
                out=W[:, :, c0:c1], in_=win_v[:, :, c0:c1]), writes=[("W", gi)], n=4000)
        Wall = [("W", gi) for gi in range(4)]
        for e_ in ("act", "dve", "pe", "pool"):
            S_._wait(e_, {"pool": S_.cnt["pool"], "dve": S_.cnt["dve"]})

        pb_rr = [0]

        def next_pb():
            b = pb_rr[0]
            pb_rr[0] = (b + 1) % 2
            return b

        def a_tile(m, i):
            slot = m % 2
            ti = 4 * m + i
            xs_ = ti % 2
            xq = ti % NXT
            op("act", lambda e: e.activation(
                junk[:, :], xt[xq][:, :], AF.Square, accum_out=st_small[:, 0:1]),
               reads=[("xt", xq)], writes=["junk", "junk2", "junk3", "sa0"], n=1024)
            op("dve", lambda e: e.tensor_scalar(
                st_small[:, 1:2], st_small[:, 0:1], 1.0 / D, EPS, ALU.mult, ALU.add),
               reads=["sa0"], writes=["sa1"])
            op("pool", lambda e: e.tensor_tensor(
                st_small[:, 2:3], st_small[:, 1:2], NEGH[:, 0:1], ALU.pow),
               reads=["sa1", "negh"], writes=["sa2"])
            op("dve", lambda e: e.scalar_tensor_tensor(
                xs[xs_][:, :], xt[xq][:, :], st_small[:, 2:3], smallc(SP_NWROW, 1024),
                ALU.mult, ALU.mult),
               reads=[("xt", xq), "sa2", "small"], writes=[("xs", xs_)], n=1024)
            for kt in range(8):
                op("pe", lambda e, kt=kt: e.transpose(
                    xT_ps[:, kt, :], xs[xs_][:, kt * 128:(kt + 1) * 128], ident_b[:, :]),
                   reads=[("xs", xs_), "ident_b"], writes=["xT_ps"])
            op("act", lambda e: e.activation(
                xT[slot][:, :, i * 128:(i + 1) * 128], xT_ps[:, :, :], AF.Copy),
               reads=["xT_ps"], writes=[("xT", slot, i)], n=1024)

        def fm_group(m, col0, ncols):
            slot = m % 2
            b = next_pb()
            S_.group_begin()
            for kt in range(8):
                op("pe", lambda e, kt=kt, b=b: e.matmul(
                    PB[b][0:ncols, :], W[:, kt, col0:col0 + ncols], xT[slot][:, kt, :],
                    start=(kt == 0), stop=(kt == 7)),
                   reads=[("W", 0)] + [("xT", slot, i) for i in range(4)], writes=[("PB", b)], n=512)
            S_.group_end()
            return b

        def macro_fm(m):
            s = m % 2
            b = fm_group(m, FM_QM, 128)
            op("act", lambda e, b=b: e.activation(qkraw[s][:, 0, 3:515], PB[b][:, :], AF.Copy),
               reads=[("PB", b)], writes=[("qkraw", s, 0)], n=512)
            b = fm_group(m, FM_KM, 128)
            op("act", lambda e, b=b: e.activation(qkraw[s][:, 1, 3:515], PB[b][:, :], AF.Copy),
               reads=[("PB", b)], writes=[("qkraw", s, 1)], n=512)
            op("pool", lambda e: e.tensor_copy(qkraw[s][:, :, 0:3], qkraw[1 - s][:, :, 512:515]),
               reads=[("qkraw", 1 - s, 0), ("qkraw", 1 - s, 1)], writes=[("qkraw_h", s)])
            for w_ in range(2):
                op("dve", lambda e, w_=w_: e.tensor_scalar(
                    cv[s][:, w_, :], qkraw[s][:, w_, 0:512], smallc(SP_CONVW + 4 * w_),
                    smallc(SP_CONVB + w_), ALU.mult, ALU.add),
                   reads=[("qkraw", s, w_), ("qkraw_h", s), "small"], writes=[("cv", s, w_)], n=512)
                for j in range(1, 4):
                    op("dve", lambda e, w_=w_, j=j: e.scalar_tensor_tensor(
                        cv[s][:, w_, :], qkraw[s][:, w_, j:j + 512], smallc(SP_CONVW + 4 * w_ + j),
                        cv[s][:, w_, :], ALU.mult, ALU.add),
                       reads=[("qkraw", s, w_), ("qkraw_h", s), "small", ("cv", s, w_)],
                       writes=[("cv", s, w_)], n=512)
            cvn = [("cv", s, 0), ("cv", s, 1)]
            op("act", lambda e: e.activation(tcv[:, :, :], cv[s][:, :, :], AF.Tanh, scale=0.5),
               reads=cvn, writes=["tcv", ("tz", 0, 0), ("tz", 0, 1), ("tz", 1, 0), ("tz", 1, 1)], n=1024)
            op("dve", lambda e: e.scalar_tensor_tensor(
                cv[s][:, :, :], tcv[:, :, :], 1.0, cv[s][:, :, :], ALU.add, ALU.mult),
               reads=cvn + ["tcv", ("tz", 0, 0), ("tz", 0, 1), ("tz", 1, 0), ("tz", 1, 1)], writes=cvn, n=1024)
            op("pool", lambda e: e.tensor_scalar(q_bf[s][:, :], cv[s][:, 0, :], 0.5 * QSCALE, 1.0,
                                                 ALU.mult, ALU.mult),
               reads=[("cv", s, 0)], writes=[("q_bf", s)], n=512)

        def macro_gla(m):
            s = m % 2
            b = fm_group(m, FM_GKL, 16)
            op("act", lambda e, b=b: e.activation(gkl_bf[:, :], PB[b][0:16, :], AF.Copy),
               reads=[("PB", b)], writes=["gkl_bf"])
            b = next_pb()
            op("pe", lambda e, b=b: e.matmul(PB[b][:, :], wup[:, :], gkl_bf[:, :],
                                             start=True, stop=True),
               reads=["wup", "gkl_bf"], writes=[("PB", b)], n=512)
            op("act", lambda e, b=b: e.activation(spg[:, :], PB[b][:, :], AF.Exp,
                                                  bias=nbgk[:, :], scale=-1.0),
               reads=[("PB", b), "nbgk"], writes=["spg"], n=512)
            ifn = [("iftok", s, i) for i in range(4)]
            op("act", lambda e: e.activation(ef4[:, :], iftok[s][:, :, 1], AF.Exp,
                                             bias=nbf[:, :], scale=-1.0),
               reads=ifn + ["nbf"], writes=["ef4"])
            S_.group_begin()
            op("act", lambda e: e.activation(spg[:, :], spg[:, :], AF.Ln, bias=ONE_AP[:, :], scale=1.0),
               reads=["spg", "one_ap"], writes=["spg"], n=1800)
            op("act", lambda e: e.activation(spf4[s][:, :], ef4[:, :], AF.Ln, bias=ONE_AP[:, :], scale=1.0),
               reads=["ef4", "one_ap"], writes=[("spf4", s)], n=1500)
            S_.group_end()
            for c in range(4):
                op("dve", lambda e, c=c: e.tensor_tensor_scan(
                    nbc[:, c * 128:(c + 1) * 128], ones_f[:, :], spg[:, c * 128:(c + 1) * 128],
                    0.0, ALU.mult, ALU.add),
                   reads=["spg", "ones_f"], writes=[("nbc", c)], n=256)
            nbcs = [("nbc", c) for c in range(4)]
            op("act", lambda e: e.activation(eB[:, :], nbc[:, :], AF.Exp, scale=-1.0 / 16.0),
               reads=nbcs, writes=["eB"], n=512)
            op("act", lambda e: e.activation(eNB[:, :], nbc[:, :], AF.Exp, scale=1.0 / 16.0),
               reads=nbcs, writes=["eNB"], n=512)
            op("pool", lambda e: e.tensor_copy(
                eBLc[m % 3][:, :], eB[:, :].rearrange("p (c t) -> p c t", t=128)[:, :, 127]),
               reads=["eB"], writes=[("eBLc", m % 3)])
            b = fm_group(m, FM_QG, 128)
            op("dve", lambda e, b=b: e.scalar_tensor_tensor(
                qg_bf[s][:, :], PB[b][:, :], QSCALE, eB[:, :], ALU.mult, ALU.mult),
               reads=[("PB", b), "eB"], writes=[("qg_bf", s)], n=512)
            b = fm_group(m, FM_KG, 128)
            op("dve", lambda e, b=b: e.tensor_tensor(kg_bf[s][:, :], PB[b][:, :], eNB[:, :], ALU.mult),
               reads=[("PB", b), "eNB"], writes=[("kg_bf", s)], n=512)

        def tm_tile(m, i):
            s = m % 2
            tsl = slice(i * 128, (i + 1) * 128)
            for (c0, n) in ((TM0, 512), (TM1, 512), (TM2, 258)):
                b = next_pb()
                S_.group_begin()
                for kt in range(8):
                    op("pe", lambda e, kt=kt, b=b, c0=c0, n=n: e.matmul(
                        PB[b][:, 0:n], xT[s][:, kt, tsl], W[:, kt, c0:c0 + n],
                        start=(kt == 0), stop=(kt == 7)),
                       reads=[("W", {TM0: 1, TM1: 2, TM2: 3}[c0]), ("xT", s, i)], writes=[("PB", b)], n=n)
                S_.group_end()
                j = i % 2
                if c0 == TM0:
                    op("dve", lambda e, b=b: e.tensor_copy(Vm[s][i][:, 0:256], PB[b][:, 0:256]),
                       reads=[("PB", b)], writes=[("Vm", s, i)], n=256)
                    op("act", lambda e, b=b: e.activation(obuf[s][:, i, :], PB[b][:, 256:512],
                                                          AF.Tanh, scale=0.5),
                       reads=[("PB", b)], writes=[("obuf", s, i)], n=256)
                elif c0 == TM1:
                    op("act", lambda e, b=b: e.activation(tz[j][:, 0:256], PB[b][:, 0:256],
                                                          AF.Tanh, scale=0.5),
                       reads=[("PB", b)], writes=[("tz", j, 0)], n=256)
                    op("dve", lambda e, b=b: e.scalar_tensor_tensor(
                        zbuf[s][:, i, 0:256], tz[j][:, 0:256], 1.0, PB[b][:, 0:256], ALU.add, ALU.mult),
                       reads=[("PB", b), ("tz", j, 0)], writes=[("zm", s, i)], n=256)
                    op("dve", lambda e, b=b: e.tensor_copy(Vg[s][i][:, :], PB[b][:, 256:512]),
                       reads=[("PB", b)], writes=[("Vg", s, i)], n=256)
                else:
                    op("act", lambda e, b=b: e.activation(tz[j][:, 256:512], PB[b][:, 0:256],
                                                          AF.Tanh, scale=0.5),
                       reads=[("PB", b)], writes=[("tz", j, 1)], n=256)
                    op("dve", lambda e, b=b: e.scalar_tensor_tensor(
                        zbuf[s][:, i, 256:512], tz[j][:, 256:512], 1.0, PB[b][:, 0:256], ALU.add, ALU.mult),
                       reads=[("PB", b), ("tz", j, 1)], writes=[("zg", s, i)], n=256)
                    op("dve", lambda e, b=b: e.tensor_copy(iftok[s][:, i, :], PB[b][:, 256:258]),
                       reads=[("PB", b)], writes=[("iftok", s, i)])
            gprod(s, i)

        def gprod(s, i):
            op("pool", lambda e: e.tensor_tensor(zbuf[s][:, i, 0:256], zbuf[s][:, i, 0:256],
                                                 nwq[:, :], ALU.mult),
               reads=[("zm", s, i), "nwq"], writes=[("zm", s, i)], n=256)
            op("pool", lambda e: e.tensor_scalar(obuf[s][:, i, :], obuf[s][:, i, :], 1.0, 1.0,
                                                 ALU.mult, ALU.add),
               reads=[("obuf", s, i)], writes=[("obuf", s, i)], n=256)
            op("pool", lambda e: e.tensor_tensor(obuf[s][:, i, :], obuf[s][:, i, :],
                                                 zbuf[s][:, i, 0:256], ALU.mult),
               reads=[("obuf", s, i), ("zm", s, i)], writes=[("obuf", s, i)], n=256)
            op("pool", lambda e: e.tensor_tensor(zbuf[s][:, i, 256:512], zbuf[s][:, i, 256:512],
                                                 gnwh[:, :], ALU.mult),
               reads=[("zg", s, i), "gnwh"], writes=[("zg", s, i)], n=256)

        def x_loads(m):
            for i in range(4):
                x_load(4 * m + i)

        def x_load(ti):
            xq = ti % NXT
            dma("sp", ("xld", xq), lambda e: e.dma_start(
                out=xt[xq][:, :], in_=x_d[ti * 128:(ti + 1) * 128, :]), writes=[("xt", xq)], n=1024)

        def stage_a_stream(m):
            for i in range(4):
                a_tile(m, i)

        def fmtm_stream(m):
            macro_fm(m)
            for i in range(4):
                tm_tile(m, i)
            macro_gla(m)

        def cidx(m, i):
            return 4 * m + i

        def gates(m, i):
            s = m % 2
            cn = cidx(m, i)
            p = cn % 4
            G = gsc[p]
            mp, mn = mst[cn % 2], mst[(cn + 1) % 2]
            mpn, mnn = ("mst", cn % 2), ("mst", (cn + 1) % 2)
            if cn >= 4:
                S_.need(("M", cn - 4))
            op("dve", lambda e: e.tensor_copy(G[:, 1:2], spf4[s][:, i:i + 1]),
               reads=[("spf4", s)], writes=[("g1", p)])
            op("dve", lambda e: e.tensor_scalar(G[:, 2:3], iftok[s][:, i, 0:1], smallc(SP_BI), None,
                                                ALU.add),
               reads=[("iftok", s, i), "small"], writes=[("g2", p)])
            op("pool", lambda e: e.tensor_scalar(igd[:, :], ident_f[:, :], G[:, 2:3], 1.0,
                                                 ALU.mult, ALU.mult),
               reads=["ident_f", ("g2", p)], writes=["igd"])
            op("dve", lambda e: e.scalar_tensor_tensor(rmat[:, :], tri_f[:, :], G[:, 1:2], igd[:, :],
                                                       ALU.mult, ALU.add),
               reads=["tri_f", ("g1", p), "igd"], writes=["rmat"])
            op("pe", lambda e: e.matmul(UREP, ones_f[:, :], rmat[:, :], start=True, stop=True),
               reads=["rmat", "ones_f"], writes=["urep"])
            op("pe", lambda e: e.matmul(GC0, tri_f[:, :], G[:, 1:3], start=True, stop=True),
               reads=["tri_f", ("g1", p), ("g2", p)], writes=["gc0"])
            op("pe", lambda e: e.matmul(GC1, ones_f[:, :], G[:, 1:3], start=True, stop=True),
               reads=["ones_f", ("g1", p), ("g2", p)], writes=["gc1"])
            op("dve", lambda e: e.tensor_reduce(G[:, 3:4], UREP, AX.X, ALU.max),
               reads=["urep"], writes=[("g3", p)])
            op("dve", lambda e: e.tensor_tensor(G[:, 4:5], G[:, 3:4], mp[:, :], ALU.max),
               reads=[("g3", p), mpn], writes=[("g4", p)])
            op("dve", lambda e: e.tensor_scalar(G[:, 5:6], G[:, 4:5], -1.0, None, ALU.mult),
               reads=[("g4", p)], writes=[("g5", p)])
            op("act", lambda e: e.activation(wrep[p][:, :], UREP, AF.Exp, bias=G[:, 5:6], scale=1.0),
               reads=["urep", ("g5", p)], writes=[("wrep", p)])
            op("act", lambda e: e.activation(G[:, 6:7], mp[:, :], AF.Exp, bias=G[:, 5:6], scale=1.0),
               reads=[mpn, ("g5", p)], writes=[("g6", p)])
            op("act", lambda e: e.activation(G[:, 7:8], MB2a[:, 260:261], AF.Exp, bias=G[:, 5:6], scale=1.0),
               reads=["gc0", ("g5", p)], writes=[("g7", p)])
            op("dve", lambda e: e.scalar_tensor_tensor(mn[:, :], MB2a[:, 262:263], -1.0, G[:, 4:5],
                                                       ALU.mult, ALU.add),
               reads=["gc1", ("g4", p)], writes=[mnn])
            S_.mark(("G", cn))

        def hsl(m, i):
            g = m // MPS
            return g % 2, (4 * m + i) * 128 - g * SEG

        def mlstm(m, i):
            s = m % 2
            p = cidx(m, i) % 4
            G = gsc[p]
            csl = slice(i * 128, (i + 1) * 128)
            hslot, tok0 = hsl(m, i)
            p2 = cidx(m, i) % 2
            S_.need(("G", cidx(m, i)))
            if cidx(m, i) >= 2:
                S_.need(("M", cidx(m, i) - 2))
            op("dve", lambda e: e.scalar_tensor_tensor(kh_bf[s][:, csl], cv[s][:, 1, csl], 0.5,
                                                       wrep[p][:, :], ALU.mult, ALU.mult),
               reads=[("cv", s, 1), ("wrep", p)], writes=[("kh_bf", s, i)])
            op("pe", lambda e: e.matmul(ATM, kh_bf[s][:, csl], q_bf[s][:, csl], start=True, stop=True),
               reads=[("kh_bf", s, i), ("q_bf", s)], writes=["ATm"])
            op("pe", lambda e: e.transpose(KTM, kh_bf[s][:, csl], ident_b[:, :]),
               reads=[("kh_bf", s, i), "ident_b"], writes=["kTm"])
            op("dve", lambda e: e.tensor_tensor(ATm_bf[:, :], ATM, tri_f[:, :], ALU.mult),
               reads=["ATm", "tri_f"], writes=["ATm_bf"])
            op("act", lambda e: e.activation(km_tok[:, :], KTM, AF.Copy),
               reads=["kTm"], writes=["km_tok"])
            op("pool", lambda e: e.tensor_scalar(Ch_bf[:, :], Cst[:, :], G[:, 6:7], 1.0,
                                                 ALU.mult, ALU.mult),
               reads=["Cst", ("g6", p)], writes=["Ch_bf"], n=258)
            S_.group_begin()
            op("pe", lambda e: e.matmul(NUM, ATm_bf[:, :], Vm[s][i][:, :], start=True, stop=False),
               reads=["ATm_bf", ("Vm", s, i), ("Vm1", s, i)], writes=["num"], n=258)
            op("pe", lambda e: e.matmul(NUM, q_bf[s][:, csl], Ch_bf[:, :], start=False, stop=True),
               reads=[("q_bf", s), "Ch_bf"], writes=["num"], n=258)
            S_.group_end()
            op("act", lambda e: e.activation(numS[p2][:, :], NUM, AF.Copy),
               reads=["num"], writes=[("numS", p2)], n=258)
            op("pe", lambda e: e.matmul(DC, km_tok[:, :], Vm[s][i][:, :], start=True, stop=True),
               reads=["km_tok", ("Vm", s, i), ("Vm1", s, i)], writes=["dC"], n=258)
            op("dve", lambda e: e.scalar_tensor_tensor(Cst[:, :], Cst[:, :], G[:, 6:7], DC,
                                                       ALU.mult, ALU.add),
               reads=["Cst", ("g6", p), "dC", "Ch_bf"], writes=["Cst"], n=258)
            S_.mark(("A", cidx(m, i)))

        def mlstm_out(m, i):
            s = m % 2
            cn = cidx(m, i)
            p = cn % 4
            p2 = cn % 2
            G = gsc[p]
            hslot, tok0 = hsl(m, i)
            Gm_ = obuf[s][:, i, :]
            NS = numS[p2]
            nsn = ("numS", p2)
            S_.need(("A", cn))
            op("act", lambda e: e.activation(osc[:, 11:12], NS[:, 256:257], AF.Abs),
               reads=[nsn], writes=["o11"])
            op("act", lambda e: e.activation(junk[:, 0:256], NS[:, 0:256], AF.Square, scale=1.0 / 16.0,
                                             accum_out=osc[:, 2:3]),
               reads=[nsn], writes=["junk", "o2"], n=256)
            op("dve", lambda e: e.tensor_tensor(osc[:, 0:1], osc[:, 11:12], G[:, 7:8], ALU.max),
               reads=["o11", ("g7", p)], writes=["o0"])
            op("dve", lambda e: e.tensor_tensor(osc[:, 3:4], osc[:, 0:1], osc[:, 0:1], ALU.mult),
               reads=["o0"], writes=["o3"])
            op("dve", lambda e: e.scalar_tensor_tensor(osc[:, 4:5], osc[:, 3:4], EPS, osc[:, 2:3],
                                                       ALU.mult, ALU.add),
               reads=["o3", "o2"], writes=["o4"])
            op("pool", lambda e: e.tensor_tensor(osc[:, 7:8], osc[:, 4:5], NEGH[:, 1:2], ALU.pow),
               reads=["o4", "negh"], writes=["o7"])
            op("dve", lambda e: e.scalar_tensor_tensor(htile[:, 0:256], NS[:, 0:256], osc[:, 7:8],
                                                       Gm_, ALU.mult, ALU.mult),
               reads=[nsn, "o7", ("obuf", s, i)], writes=["htile_m"], n=256)
            S_.group_begin()
            for bk in range(2):
                op("pe", lambda e, bk=bk: e.transpose(HTPM[:, bk, :],
                                                      htile[:, bk * 128:(bk + 1) * 128], ident_b[:, :]),
                   reads=["htile_m", "ident_b"], writes=["hTm_ps"])
            S_.group_end()
            op("act", lambda e: e.activation(
                hTst[hslot][:, 0:2, tok0:tok0 + 128], HTPM[:, :, :], AF.Copy),
               reads=["hTm_ps"], writes=[("hTst", hslot, 0)], n=256)
            S_.mark(("M", cidx(m, i)))

        def gla(m, i):
            s = m % 2
            csl = slice(i * 128, (i + 1) * 128)
            hslot, tok0 = hsl(m, i)
            if i == 0:
                dec, decn = eBLc[(m + 2) % 3][:, 3:4], ("eBLc", (m + 2) % 3)
            else:
                dec, decn = eBLc[m % 3][:, i - 1:i], ("eBLc", m % 3)
            if cidx(m, i) >= 2:
                S_.need(("GM", cidx(m, i) - 2))
            op("pe", lambda e: e.matmul(ATG, kg_bf[s][:, csl], qg_bf[s][:, csl], start=True, stop=True),
               reads=[("kg_bf", s), ("qg_bf", s)], writes=["ATg"])
            op("pe", lambda e: e.transpose(KTG, kg_bf[s][:, csl], ident_b[:, :]),
               reads=[("kg_bf", s), "ident_b"], writes=["kTg"])
            op("dve", lambda e: e.tensor_tensor(ATg_bf[:, :], ATG, tri_f[:, :], ALU.mult),
               reads=["ATg", "tri_f"], writes=["ATg_bf"])
            op("act", lambda e: e.activation(kg_tok[:, :], KTG, AF.Copy),
               reads=["kTg"], writes=["kg_tok"])
            op("pool", lambda e: e.tensor_scalar(Sh_bf[:, :], Ust[:, :], dec, 1.0, ALU.mult, ALU.mult),
               reads=["Ust", decn], writes=["Sh_bf"], n=256)
            S_.group_begin()
            op("pe", lambda e: e.matmul(OG, ATg_bf[:, :], Vg[s][i][:, :], start=True, stop=False),
               reads=["ATg_bf", ("Vg", s, i)], writes=["og"], n=256)
            op("pe", lambda e: e.matmul(OG, qg_bf[s][:, csl], Sh_bf[:, :], start=False, stop=True),
               reads=[("qg_bf", s), "Sh_bf"], writes=["og"], n=256)
            S_.group_end()
            op("act", lambda e: e.activation(ogS[cidx(m, i) % 2][:, :], OG, AF.Copy),
               reads=["og"], writes=[("ogS", cidx(m, i) % 2)], n=256)
            op("pe", lambda e: e.matmul(DS, kg_tok[:, :], Vg[s][i][:, :], start=True, stop=True),
               reads=["kg_tok", ("Vg", s, i)], writes=["dS"], n=256)
            op("dve", lambda e: e.scalar_tensor_tensor(Ust[:, :], Ust[:, :], dec, DS, ALU.mult, ALU.add),
               reads=["Ust", decn, "dS", "Sh_bf"], writes=["Ust"], n=256)
            S_.mark(("GA", cidx(m, i)))

        def gla_out(m, i):
            s = m % 2
            cn = cidx(m, i)
            p2 = cn % 2
            hslot, tok0 = hsl(m, i)
            Gg_ = zbuf[s][:, i, 256:512]
            OS = ogS[p2]
            osn = ("ogS", p2)
            S_.need(("GA", cn))
            op("act", lambda e: e.activation(junk[:, 256:512], OS[:, :], AF.Square, scale=1.0 / 16.0,
                                             accum_out=oscg[:, 0:1]),
               reads=[osn], writes=["junk2", "p0"], n=256)
            op("dve", lambda e: e.tensor_scalar(oscg[:, 1:2], oscg[:, 0:1], EPS, None, ALU.add),
               reads=["p0"], writes=["p1"])
            op("pool", lambda e: e.tensor_tensor(oscg[:, 2:3], oscg[:, 1:2], NEGH[:, 2:3], ALU.pow),
               reads=["p1", "negh"], writes=["p2"])
            op("dve", lambda e: e.scalar_tensor_tensor(htile[:, 256:512], OS[:, :], oscg[:, 2:3],
                                                       Gg_, ALU.mult, ALU.mult),
               reads=[osn, "p2", ("zg", s, i)], writes=["htile_g"], n=256)
            S_.group_begin()
            for bk in range(2):
                op("pe", lambda e, bk=bk: e.transpose(HTPG[:, bk, :],
                                                      htile[:, 256 + bk * 128:256 + (bk + 1) * 128],
                                                      ident_b[:, :]),
                   reads=["htile_g", "ident_b"], writes=["hTg_ps"])
            S_.group_end()
            op("act", lambda e: e.activation(
                hTst[hslot][:, 2:4, tok0:tok0 + 128], HTPG[:, :, :], AF.Copy),
               reads=["hTg_ps"], writes=[("hTst", hslot, 1)], n=256)
            S_.mark(("GM", cidx(m, i)))

        def gates_stream(m):
            for i in range(4):
                gates(m, i)

        def mlstm_stream(m):
            for i in range(4):
                mlstm(m, i)

        def gla_stream(m):
            for i in range(4):
                gla(m, i)

        def mlstm_out_stream(m):
            for i in range(4):
                mlstm_out(m, i)

        def gla_out_stream(m):
            for i in range(4):
                gla_out(m, i)

        def seg_send(g):
            hslot = g % 2
            dma("sp", ("hst", hslot), lambda e: e.dma_start(
                out=hsend[g].ap().rearrange("(b p) t -> p b t", p=128), in_=hTst[hslot][:, :, :]),
                reads=[("hTst", hslot, 0), ("hTst", hslot, 1)], writes=[("hsend", g)])
            dma("pool", "ccA%d" % (g % 2), lambda e: e.collective_compute(
                "AllGather", ALU.bypass, replica_groups=GROUPS,
                ins=[hsend[g].ap().opt()], outs=[hall[g].ap().opt()], dma_qos=CCQOS),
                reads=[("hsend", g)], writes=[("hall", g)], inc=cc_inc, n=15000)

        def outproj_loads(g):
            dma("sp", "hld", lambda e: e.dma_start(
                out=hTall[:, :, :], in_=hall[g].ap().rearrange("(k p) t -> p k t", p=128)),
                reads=[("hall", g)], writes=["hTall"])
            for tt in range(TS):
                xr_load(g, tt)

        def xr_load(g, tt):
            ti = g * TS + tt
            dma("sp", ("xrld", tt), lambda e: e.dma_start(
                out=xr[tt][:, :], in_=xres_d[ti * 128:(ti + 1) * 128, :]), writes=[("xr", tt)])

        def outproj_a(g):
            ys_ = g % 2
            for tt in range(TS):
                outproj_tile(g, tt)
            ssqs = [("ssq", ys_, tt) for tt in range(TS)]
            dma("sp", "sst", lambda e: e.dma_start(out=ssq_i[g].ap(), in_=ssq[ys_][:, :]),
                reads=ssqs, writes=[("ssq_i", g)])
            dma("pool", "ccR%d" % (g % 2), lambda e: e.collective_compute(
                "AllReduce", ALU.add, replica_groups=GROUPS,
                ins=[ssq_i[g].ap().opt()], outs=[ssq_o[g].ap().opt()], dma_qos=CCQOS),
                reads=[("ssq_i", g)], writes=[("ssq_o", g)], inc=cc_inc, n=12000)

        def outproj_tile(g, tt):
            ys_ = g % 2
            ti = g * TS + tt
            xs_ = tt
            for k in range(16):
                op("pe", lambda e, k=k: e.matmul(
                    PB[2][:, 0:256], hTall[:, k, tt * 128:(tt + 1) * 128], Wout[:, k, :],
                    start=(k == 0), stop=(k == 15)),
                   reads=["hTall", "Wout"], writes=[("PB", 2)], n=256)
            op("dve", lambda e: e.tensor_tensor(
                ypre[ys_][:, tt, :], PB[2][:, 0:256], xr[xs_][:, :], ALU.add),
               reads=[("PB", 2), ("xr", xs_)], writes=[("ypre", ys_, tt)], n=256)
            op("act", lambda e: e.activation(junk[:, 512:768], ypre[ys_][:, tt, :], AF.Square,
                                             accum_out=ssq[ys_][:, tt:tt + 1]),
               reads=[("ypre", ys_, tt)], writes=["junk3", ("ssq", ys_, tt)], n=256)

        def outproj_b(g):
            ys_ = g % 2
            dma("sp", "sld", lambda e: e.dma_start(out=ssr[:, :], in_=ssq_o[g].ap()),
                reads=[("ssq_o", g)], writes=["ssr"])
            op("dve", lambda e: e.tensor_scalar(rfin[:, :], ssr[:, :], 1.0 / D, EPS, ALU.mult, ALU.add),
               reads=["ssr"], writes=["rfin0"])
            op("pool", lambda e: e.tensor_tensor(rfin[:, :], rfin[:, :], NEGH[:, 0:TS], ALU.pow),
               reads=["rfin0", "negh"], writes=["rfin"])
            for tt in range(TS):
                outproj_b_tile(g, tt)

        def outproj_b_tile(g, tt):
            ys_ = g % 2
            ti = g * TS + tt
            yb = tt
            op("dve", lambda e: e.scalar_tensor_tensor(
                yo[yb][:, :], ypre[ys_][:, tt, :], rfin[:, tt:tt + 1], smallc(SP_FNW, 256),
                ALU.mult, ALU.mult),
               reads=[("ypre", ys_, tt), "rfin", "small"], writes=[("yo", yb)], n=256)
            dma("sp", ("yst", yb), lambda e: e.dma_start(
                out=y_d[ti * 128:(ti + 1) * 128, :], in_=yo[yb][:, :]),
                reads=[("yo", yb)], writes=[("y", ti)])

        cap = S_.capture
        x_loads(0)
        for it in cap(stage_a_stream, 0):
            S_.emit(it)
        if NM > 1:
            x_loads(1)
        S_.interleave([cap(fmtm_stream, 0)] + ([cap(stage_a_stream, 1)] if NM > 1 else []))
        if NM > 2:
            x_loads(2)
        LAG_A, LAG_B = int(os.environ.get("KLA", "2")), int(os.environ.get("KLB", "3"))
        loads_done = set()
        for m in range(NM):
            g = m // MPS
            import os
            if os.environ.get("KSEQ") == "1":
                def seq_stream(m):
                    for i in range(4):
                        gates(m, i)
                        mlstm(m, i)
                        mlstm_out(m, i)
                        gla(m, i)
                        gla_out(m, i)
                streams = [cap(seq_stream, m)]
            else:
                streams = [cap(gates_stream, m), cap(mlstm_stream, m), cap(gla_stream, m),
                           cap(mlstm_out_stream, m), cap(gla_out_stream, m)]
            WG = float(os.environ.get("KWG", "0.5"))
            WP = float(os.environ.get("KWP", "1.0"))
            weights = [WG, 1.0, 1.0, 1.0, 1.0][:len(streams)] if len(streams) == 5 else [1.0]
            if m + 1 < NM:
                if os.environ.get("KPF", "0") == "1":
                    streams.insert(0, cap(fmtm_stream, m + 1))
                else:
                    streams.append(cap(fmtm_stream, m + 1))
                weights.append(WP)
            last_of_seg = (m + 1) % MPS == 0
            if last_of_seg and g - LAG_A >= 0:
                streams.append(cap(outproj_a, g - LAG_A))
                weights.append(1.0)
            if last_of_seg and g - LAG_B >= 0:
                streams.append(cap(outproj_b, g - LAG_B))
                weights.append(1.0)
            if m + 2 < NM:
                streams.append(cap(stage_a_stream, m + 2))
                weights.append(1.0)
            S_.interleave(streams, weights)
            if m + 3 < NM:
                x_loads(m + 3)
            if last_of_seg:
                if 0 <= g + 1 - LAG_A:
                    outproj_loads(g + 1 - LAG_A)
                    loads_done.add(g + 1 - LAG_A)
                seg_send(g)
        na, nb = max(0, NSEG - LAG_A), max(0, NSEG - LAG_B)
        while na < NSEG or nb < NSEG:
            if na < NSEG:
                if na not in loads_done:
                    outproj_loads(na)
                outproj_a(na)
                na += 1
            while nb < NSEG and (nb < na - 1 or na == NSEG):
                outproj_b(nb)
                nb += 1

        S_.wait_all("sp", list(S_.sem.keys()))
        with nc.Block() as block:
            block.tensor(lambda e: S_.replay("pe", e))
            block.vector(lambda e: S_.replay("dve", e))
            block.scalar(lambda e: S_.replay("act", e))
            block.gpsimd(lambda e: S_.replay("pool", e))
            block.sync(lambda e: S_.replay("sp", e))
        print("ninst", S_.ninst)
    return nc


def _prep_inputs(x, norm_w, w_in, conv_w, conv_b, b_igate, b_fgate, mlstm_norm_w,
                 w_gk_up, b_gk, gla_norm_w, w_out, final_norm_w):
    f = np.float32
    x = np.asarray(x, f)
    w_in = np.asarray(w_in, f)
    o = 0
    offs = {}
    for name, n in (("qm", 512), ("km", 512), ("vm", 1024), ("i", 4), ("f", 4), ("o", 1024),
                    ("zm", 1024), ("qg", 512), ("kg", 512), ("vg", 1024), ("gkl", 16), ("zg", 1024)):
        offs[name] = o
        o += n
    conv_w = np.asarray(conv_w, f)
    conv_b = np.asarray(conv_b, f)
    w_out = np.asarray(w_out, f)
    in_maps = []
    for c in range(8):
        b, j = c // 4, c % 4
        cols = np.concatenate([
            np.arange(offs["qm"] + 128 * j, offs["qm"] + 128 * (j + 1)),
            np.arange(offs["km"] + 128 * j, offs["km"] + 128 * (j + 1)),
            np.arange(offs["qg"] + 128 * j, offs["qg"] + 128 * (j + 1)),
            np.arange(offs["kg"] + 128 * j, offs["kg"] + 128 * (j + 1)),
            np.arange(offs["gkl"], offs["gkl"] + 16),
            np.arange(offs["vm"] + 256 * j, offs["vm"] + 256 * (j + 1)),
            np.arange(offs["o"] + 256 * j, offs["o"] + 256 * (j + 1)),
            np.arange(offs["zm"] + 256 * j, offs["zm"] + 256 * (j + 1)),
            np.arange(offs["vg"] + 256 * j, offs["vg"] + 256 * (j + 1)),
            np.arange(offs["zg"] + 256 * j, offs["zg"] + 256 * (j + 1)),
            np.array([offs["i"] + j, offs["f"] + j]),
        ])
        assert cols.size == NCOL
        w_in_c = np.ascontiguousarray(w_in[:, cols])
        wup_c = np.ascontiguousarray(np.asarray(w_gk_up, f)[:, 128 * j:128 * (j + 1)])
        rows = np.concatenate([np.concatenate([np.arange(256 * r, 256 * (r + 1)),
                                               np.arange(1024 + 256 * r, 1024 + 256 * (r + 1))])
                               for r in range(4)])
        wout_c = np.ascontiguousarray(w_out[rows][:, 256 * j:256 * (j + 1)])
        xres_c = np.ascontiguousarray(x[b][:, 256 * j:256 * (j + 1)])
        small = np.zeros((128, NSMALL), f)
        small[:, SP_NORMW:SP_NORMW + 8] = np.asarray(norm_w, f).reshape(8, 128).T
        small[:, SP_CONVW:SP_CONVW + 4] = conv_w[:, 128 * j:128 * (j + 1)].T
        small[:, SP_CONVW + 4:SP_CONVW + 8] = conv_w[:, 512 + 128 * j:512 + 128 * (j + 1)].T
        small[:, SP_CONVB] = conv_b[128 * j:128 * (j + 1)]
        small[:, SP_CONVB + 1] = conv_b[512 + 128 * j:512 + 128 * (j + 1)]
        small[:, SP_BGK] = np.asarray(b_gk, f)[128 * j:128 * (j + 1)]
        small[:, SP_BI] = np.asarray(b_igate, f)[j]
        small[:, SP_BF] = np.asarray(b_fgate, f)[j]
        small[:, SP_MNW:SP_MNW + 256] = np.asarray(mlstm_norm_w, f)[256 * j:256 * (j + 1)][None, :]
        small[:, SP_GNW:SP_GNW + 256] = np.asarray(gla_norm_w, f)[256 * j:256 * (j + 1)][None, :]
        small[:, SP_FNW:SP_FNW + 256] = np.asarray(final_norm_w, f)[256 * j:256 * (j + 1)][None, :]
        small[:, SP_NWROW:SP_NWROW + 1024] = np.asarray(norm_w, f)[None, :]
        in_maps.append({"x_b": np.ascontiguousarray(x[b]), "w_in_c": w_in_c, "wup_c": wup_c,
                        "wout_c": wout_c, "xres_c": xres_c, "small_c": small})
    return in_maps


def run(inputs, NSEG=None, trace=False):
    x = np.asarray(inputs["x"])
    B, S, _ = x.shape
    assert B == 2
    if NSEG is None:
        NSEG = S // 512
    nc = build_nc(S, NSEG)
    in_maps = _prep_inputs(**inputs)
    res = run_bass_kernel_spmd(nc, in_maps, core_ids=list(range(8)), trace=trace)
    out = np.empty((B, S, D), np.float32)
    for c in range(8):
        b, j = c // 4, c % 4
        out[b, :, 256 * j:256 * (j + 1)] = np.asarray(res.results[c]["y_c"], np.float32)
    return out, res


def kernel(**inputs):
    out, _ = run(inputs)
    return out
```

```python
import contextlib
import numpy as np
import ml_dtypes
import concourse.bass as bass
import concourse.mybir as mybir
from concourse.bass_utils import run_bass_kernel_spmd

import os
SAMELAT = float(os.environ.get("KSL", "0.03"))
LATPE = float(os.environ.get("KLPE", "0.8"))
F32 = mybir.dt.float32
BF16 = mybir.dt.bfloat16
AF = mybir.ActivationFunctionType
ALU = mybir.AluOpType
AX = mybir.AxisListType

D = 1024
EPS = 1e-6
NEG_INIT = -1e30
QSCALE = 128 ** -0.5
NCOL = 1810
FM_QM, FM_KM, FM_QG, FM_KG, FM_GKL = 0, 128, 256, 384, 512
TM0, TM1, TM2 = 528, 1040, 1552
SP_NORMW, SP_CONVW, SP_CONVB, SP_BGK, SP_BI, SP_BF = 0, 8, 16, 18, 19, 20
SP_MNW, SP_GNW, SP_FNW = 21, 277, 533
SP_NWROW = 789
NSMALL = 789 + 1024


class Sched:
    def __init__(self, nc, stack):
        self.nc = nc
        self.stack = stack
        self.eng = {"pe": nc.tensor, "act": nc.scalar, "dve": nc.vector,
                    "pool": nc.gpsimd, "sp": nc.sync}
        self.sem, self.cnt = {}, {}
        self.known = {e: {} for e in self.eng}
        self.bufs = {}
        self.ninst = 0
        self.prog = {e: [] for e in self.eng}
        self.cap = None
        self.marks = set()
        self.bank_of = {"xT_ps": 0, ("PB", 0): 1, ("PB", 1): 2, ("PB", 2): 3,
                        "urep": 4, "gc0": 6, "gc1": 6, "hTm_ps": 4, "hTg_ps": 4, "ATg": 4,
                        "ATm": 5, "num": 5, "kTm": 5, "dC": 6, "kTg": 6, "og": 7, "dS": 7}
        self.bank_last = {i: {} for i in range(8)}
        for e in self.eng:
            self._mk(e)

    def _mk(self, key):
        self.sem[key] = self.stack.enter_context(self.nc.semaphore("s_" + str(key)))
        self.cnt[key] = 0

    def _deps(self, eng, reads, writes, is_dma):
        deps = {}

        def add(k, v):
            if v > deps.get(k, 0):
                deps[k] = v
        for b in reads:
            st = self.bufs.get(b)
            if st and st["w"]:
                k, v = st["w"]
                if not (k == eng and eng == "pe" and not is_dma):
                    add(k, v)
        for b in writes:
            st = self.bufs.get(b)
            if st:
                if st["w"]:
                    k, v = st["w"]
                    if is_dma or k != eng or eng != "pe":
                        add(k, v)
                for k, v in st["r"].items():
                    if is_dma or k != eng or eng != "pe":
                        add(k, v)
        for b in list(reads) + list(writes):
            bank = self.bank_of.get(b)
            if bank is not None:
                for k, v in self.bank_last[bank].items():
                    if k != eng:
                        add(k, v)
        return deps

    def _wait(self, eng, deps):
        for k, v in deps.items():
            if self.known[eng].get(k, 0) >= v:
                continue
            self.prog[eng].append(("w", k, v))
            self.known[eng][k] = v

    def _record(self, key, val, reads, writes):
        for b in reads:
            st = self.bufs.setdefault(b, {"w": None, "r": {}})
            if st["r"].get(key, 0) < val:
                st["r"][key] = val
        for b in writes:
            self.bufs[b] = {"w": (key, val), "r": {}}

    def op(self, eng, fn, reads=(), writes=(), n=128):
        if self.cap is not None:
            self.cap.append(("op", eng, fn, tuple(reads), tuple(writes), n))
            return
        self.sim_commit(("op", eng, fn, tuple(reads), tuple(writes), n))
        self._wait(eng, self._deps(eng, reads, writes, False))
        self.cnt[eng] += 1
        self.prog[eng].append(("i", fn, eng, 1))
        self._record(eng, self.cnt[eng], reads, writes)
        for b in list(reads) + list(writes):
            bank = self.bank_of.get(b)
            if bank is not None:
                self.bank_last[bank][eng] = self.cnt[eng]
        self.ninst += 1

    def dma(self, q, key, fn, reads=(), writes=(), inc=16, n=128):
        if self.cap is not None:
            self.cap.append(("dma", q, key, fn, tuple(reads), tuple(writes), inc, n))
            return
        self.sim_commit(("dma", q, key, fn, tuple(reads), tuple(writes), inc, n))
        if key not in self.sem:
            self._mk(key)
        self._wait(q, self._deps(q, reads, writes, True))
        self.cnt[key] += inc
        self.prog[q].append(("i", fn, key, inc))
        self._record(key, self.cnt[key], reads, writes)
        self.ninst += 1

    def mark(self, key):
        if self.cap is not None:
            self.cap.append(("mark", key))

    def need(self, key):
        if self.cap is not None:
            self.cap.append(("need", key))

    def group_begin(self):
        if self.cap is not None:
            self._gstack = self.cap
            self.cap = []

    def group_end(self):
        if self.cap is not None:
            items = self.cap
            self.cap = self._gstack
            self.cap.append(("group", items))

    def capture(self, f, *a):
        assert self.cap is None
        self.cap = []
        f(*a)
        out, self.cap = self.cap, None
        return out

    def emit(self, it):
        if it[0] == "mark":
            self.marks.add(it[1])
        elif it[0] == "need":
            assert it[1] in self.marks, it
        elif it[0] == "group":
            for x in it[1]:
                self.emit(x)
        elif it[0] == "op":
            self.op(it[1], it[2], it[3], it[4], it[5])
        else:
            self.dma(it[1], it[2], it[3], it[4], it[5], it[6], it[7])

    FIX = {"pe": float(os.environ.get("KFPE", "0.1")), "act": float(os.environ.get("KFACT", "0.2")),
           "dve": float(os.environ.get("KFDVE", "0.12")), "pool": float(os.environ.get("KFPOOL", "0.3")), "sp": 0.05}
    PER = {"pe": 0.00042, "act": 0.0009, "dve": 0.0012, "pool": 0.0008, "sp": 0.0}
    LAT = 0.8

    def _sim_init(self):
        if not hasattr(self, "ef"):
            self.ef = {e: 0.0 for e in self.eng}
            self.tw, self.trd = {}, {}
            self.bank_t = {i: {} for i in range(8)}

    def _first(self, it):
        while it[0] == "group":
            it = it[1][0]
        return it

    def est_start(self, it):
        self._sim_init()
        it = self._first(it)
        eng = it[1]
        reads, writes = (it[3], it[4]) if it[0] == "op" else (it[4], it[5])
        t = self.ef[eng]

        def rdy(tt_e):
            tt, e2 = tt_e
            return tt + ((LATPE if eng == "pe" else self.LAT) if e2 != eng else SAMELAT)
        for b_ in reads:
            if b_ in self.tw:
                t = max(t, rdy(self.tw[b_]))
        for b_ in writes:
            if b_ in self.tw:
                t = max(t, rdy(self.tw[b_]))
            for e2, tt in self.trd.get(b_, {}).items():
                t = max(t, rdy((tt, e2)))
        for b_ in list(reads) + list(writes):
            bank = self.bank_of.get(b_)
            if bank is not None:
                for e2, tt in self.bank_t[bank].items():
                    if e2 != eng:
                        t = max(t, tt + (LATPE if eng == "pe" else self.LAT))
        return t

    def sim_commit(self, it):
        self._sim_init()
        eng = it[1]
        if it[0] == "op":
            reads, writes, n = it[3], it[4], it[5]
            start = self.est_start(it)
            end = start + self.FIX[eng] + self.PER[eng] * n
            self.ef[eng] = end
            who = eng
        else:
            reads, writes, n = it[4], it[5], it[7]
            start = self.est_start(it)
            self.ef[eng] = start + 0.06
            end = start + 2.0 + 0.002 * n
            who = "dma"
        for b_ in reads:
            d = self.trd.setdefault(b_, {})
            d[who] = max(d.get(who, 0.0), end)
        for b_ in writes:
            self.tw[b_] = (end, who)
            self.trd[b_] = {}
        for b_ in list(reads) + list(writes):
            bank = self.bank_of.get(b_)
            if bank is not None:
                self.bank_t[bank][eng] = end

    def interleave(self, streams, weights=None):
        pos = [0] * len(streams)
        marks = self.marks
        while True:
            best, bt = -1, 1e30
            for k, s in enumerate(streams):
                while pos[k] < len(s) and (s[pos[k]][0] == "mark" or
                                           (s[pos[k]][0] == "need" and s[pos[k]][1] in marks)):
                    if s[pos[k]][0] == "mark":
                        marks.add(s[pos[k]][1])
                    pos[k] += 1
                if pos[k] < len(s) and s[pos[k]][0] != "need":
                    t = self.est_start(s[pos[k]])
                    if t < bt:
                        best, bt = k, t
            if best < 0:
                assert all(pos[k] >= len(s) for k, s in enumerate(streams)), "interleave deadlock"
                return
            self.emit(streams[best][pos[best]])
            pos[best] += 1

    def wait_all(self, eng, keys):
        for k in keys:
            if self.cnt[k] > 0:
                self.prog[eng].append(("w", k, self.cnt[k]))

    def replay(self, eng, e):
        for it in self.prog[eng]:
            if it[0] == "w":
                e.wait_ge(self.sem[it[1]], it[2])
            else:
                inst = it[1](e)
                inst.then_inc(self.sem[it[2]], it[3])


def build_nc(S, NSEG, cc_inc=1):
    NT = S // 128
    NM = S // 512
    TS = NT // NSEG
    SEG = S // NSEG
    MPS = NM // NSEG
    assert NM * 512 == S and MPS * NSEG == NM

    nc = bass.Bass("TRN2", target_bir_lowering=False)
    x_d = nc.dram_tensor("x_b", [S, D], F32, kind="ExternalInput").ap()
    win_d = nc.dram_tensor("w_in_c", [D, NCOL], F32, kind="ExternalInput").ap()
    wup_d = nc.dram_tensor("wup_c", [16, 128], F32, kind="ExternalInput").ap()
    wout_d = nc.dram_tensor("wout_c", [2048, 256], F32, kind="ExternalInput").ap()
    xres_d = nc.dram_tensor("xres_c", [S, 256], F32, kind="ExternalInput").ap()
    small_d = nc.dram_tensor("small_c", [128, NSMALL], F32, kind="ExternalInput").ap()
    y_d = nc.dram_tensor("y_c", [S, 256], F32, kind="ExternalOutput").ap()
    hsend = [nc.dram_tensor(f"hsend{g}", [512, SEG], BF16) for g in range(NSEG)]
    hall = [nc.dram_tensor(f"hall{g}", [2048, SEG], BF16) for g in range(NSEG)]
    ssq_i = [nc.dram_tensor(f"ssqi{g}", [128, TS], F32) for g in range(NSEG)]
    ssq_o = [nc.dram_tensor(f"ssqo{g}", [128, TS], F32) for g in range(NSEG)]
    GROUPS = [[0, 1, 2, 3], [4, 5, 6, 7]]
    import os
    CCQOS = os.environ.get('KQOS', 'P2')
    CCQOS = None if CCQOS == 'none' else CCQOS

    with contextlib.ExitStack() as st:
        def sb(name, shape, dt):
            return st.enter_context(nc.sbuf_tensor(name, shape, dt))

        def ps(name, shape, dt):
            return st.enter_context(nc.psum_tensor(name, shape, dt))

        xT_ps = ps("xT_ps", [128, 8, 128], BF16)
        PB = [ps(f"PB{i}", [128, 512], F32) for i in range(3)]
        MB0 = ps("MB0", [128, 512], F32)
        MB1a = ps("MB1a", [128, 512], F32)
        MB2a = ps("MB2a", [128, 512], F32)
        MB3 = ps("MB3", [128, 512], F32)
        UREP, ATG = MB0[:, 0:128], MB0[:, 384:512]
        GC0, GC1 = MB2a[:, 260:262], MB2a[:, 262:264]
        HTPM = MB0[:, 128:256].bitcast(BF16).rearrange("p (b t) -> p b t", b=2)
        HTPG = MB0[:, 256:384].bitcast(BF16).rearrange("p (b t) -> p b t", b=2)
        ATM, NUM = MB1a[:, 0:128], MB1a[:, 128:386]
        KTM = MB1a[:, 448:512].bitcast(BF16)
        DC = MB2a[:, 0:258]
        KTG = MB2a[:, 448:512].bitcast(BF16)
        OG, DS = MB3[:, 0:256], MB3[:, 256:512]

        W = sb("W", [128, 8, NCOL], BF16)
        Wout = sb("Wout", [128, 16, 256], BF16)
        wup = sb("wup", [16, 128], BF16)
        small = sb("small", [128, NSMALL], F32)
        ident_b = sb("ident_b", [128, 128], BF16)
        ident_f = sb("ident_f", [128, 128], F32)
        tri_f = sb("tri_f", [128, 128], F32)
        ones_f = sb("ones_f", [128, 128], F32)
        nbgk = sb("nbgk", [128, 1], F32)
        nbf = sb("nbf", [128, 1], F32)
        nwq = sb("nwq", [128, 256], F32)
        gnwh = sb("gnwh", [128, 256], F32)
        NEGH = sb("negh", [128, 8], F32)
        EPS_AP = sb("eps_ap", [128, 1], F32)
        ONE_AP = sb("one_ap", [128, 1], F32)

        NXT = 4
        xt = [sb(f"xt{i}", [128, D], F32) for i in range(NXT)]
        junk = sb("junk", [128, D], BF16)
        xs = [sb(f"xs{i}", [128, D], BF16) for i in range(2)]
        xT = [sb(f"xT{i}", [128, 8, 512], BF16) for i in range(2)]
        st_small = sb("st_small", [128, 8], F32)
        qkraw = [sb(f"qkraw{s}", [128, 2, 515], F32) for s in range(2)]
        cv = [sb(f"cv{s}", [128, 2, 512], F32) for s in range(2)]
        q_bf = [sb(f"q_bf{s}", [128, 512], BF16) for s in range(2)]
        kh_bf = [sb(f"kh_bf{s}", [128, 512], BF16) for s in range(2)]
        qg_bf = [sb(f"qg_bf{s}", [128, 512], BF16) for s in range(2)]
        kg_bf = [sb(f"kg_bf{s}", [128, 512], BF16) for s in range(2)]
        eBLc = [sb(f"eBLc{s}", [128, 4], F32) for s in range(3)]
        gkl_bf = sb("gkl_bf", [16, 512], BF16)
        spg = sb("spg", [128, 512], F32)
        nbc = sb("nbc", [128, 512], F32)
        eB = sb("eB", [128, 512], F32)
        eNB = sb("eNB", [128, 512], F32)
        Vm = [[sb(f"Vm{s}_{i}", [128, 258], BF16) for i in range(4)] for s in range(2)]
        Vg = [[sb(f"Vg{s}_{i}", [128, 256], BF16) for i in range(4)] for s in range(2)]
        zbuf = [sb(f"zbuf{s}", [128, 4, 512], F32) for s in range(2)]
        obuf = [sb(f"obuf{s}", [128, 4, 256], F32) for s in range(2)]
        iftok = [sb(f"iftok{s}", [128, 4, 2], F32) for s in range(2)]
        spf4 = [sb(f"spf4_{s}", [128, 4], F32) for s in range(2)]
        ef4 = sb("ef4", [128, 4], F32)
        tcv = sb("tcv", [128, 2, 512], F32)
        tz = [tcv[:, j, :] for j in range(2)]
        gsc = [sb(f"gsc{i}", [128, 16], F32) for i in range(4)]
        igd = sb("igd", [128, 128], F32)
        rmat = sb("rmat", [128, 128], F32)
        iot = rmat
        wrep = [sb(f"wrep{i}", [128, 128], F32) for i in range(4)]
        oscg = sb("oscg", [128, 16], F32)
        numS = [sb(f"numS{i}", [128, 258], F32) for i in range(2)]
        ogS = [sb(f"ogS{i}", [128, 256], F32) for i in range(2)]
        mst = [sb(f"mst{i}", [128, 1], F32) for i in range(2)]
        Cst = sb("Cst", [128, 258], F32)
        Ch_bf = sb("Ch_bf", [128, 258], BF16)
        Ust = sb("Ust", [128, 256], F32)
        Sh_bf = sb("Sh_bf", [128, 256], BF16)
        ATm_bf = sb("ATm_bf", [128, 128], BF16)
        ATg_bf = sb("ATg_bf", [128, 128], BF16)
        km_tok = sb("km_tok", [128, 128], BF16)
        kg_tok = sb("kg_tok", [128, 128], BF16)
        htile = sb("htile", [128, 512], BF16)
        osc = sb("osc", [128, 16], F32)
        hTst = [sb(f"hTst{i}", [128, 4, SEG], BF16) for i in range(2)]
        hTall = sb("hTall", [128, 16, SEG], BF16)
        xr = [sb(f"xr{i}", [128, 256], F32) for i in range(TS)]
        ypre = [sb(f"ypre{i}", [128, TS, 256], F32) for i in range(2)]
        ssq = [sb(f"ssq{i}", [128, TS], F32) for i in range(2)]
        ssr = sb("ssr", [128, TS], F32)
        rfin = sb("rfin", [128, TS], F32)
        yo = [sb(f"yo{i}", [128, 256], F32) for i in range(TS)]

        S_ = Sched(nc, st)
        op, dma = S_.op, S_.dma

        def smallc(c0, n=1):
            return small[:, c0:c0 + n]

        dma("sp", "wld", lambda e: e.dma_start(out=small[:, :], in_=small_d[:, :]),
            writes=["small"])
        dma("pool", "wout_ld", lambda e: e.dma_start(out=Wout[:, :, :],
                                                 in_=wout_d.rearrange("(k p) n -> p k n", p=128)),
            writes=["Wout"])
        dma("pool", "wup_ld", lambda e: e.dma_start(out=wup[:, :], in_=wup_d[:, :]), writes=["wup"])
        op("pool", lambda e: e.iota(iot[:, :], [[1, 128]], base=0, channel_multiplier=-1,
                                    allow_small_or_imprecise_dtypes=True), writes=["iot"])
        op("dve", lambda e: e.tensor_single_scalar(tri_f[:, :], iot[:, :], 0.0, ALU.is_ge),
           reads=["iot"], writes=["tri_f"])
        op("dve", lambda e: e.tensor_single_scalar(ident_f[:, :], iot[:, :], 0.0, ALU.is_equal),
           reads=["iot"], writes=["ident_f"])
        op("dve", lambda e: e.tensor_copy(ident_b[:, :], ident_f[:, :]),
           reads=["ident_f"], writes=["ident_b"])
        op("pool", lambda e: e.memset(ones_f[:, :], 1.0), writes=["ones_f"])
        op("pool", lambda e: e.memset(qkraw[1][:, :, 512:515], 0.0), writes=[("qkraw", 1, 0), ("qkraw", 1, 1)])
        op("pool", lambda e: e.memset(Cst[:, :], 0.0), writes=["Cst"])
        op("pool", lambda e: e.memset(Ust[:, :], 0.0), writes=["Ust"])
        op("pool", lambda e: e.memset(mst[0][:, :], NEG_INIT), writes=[("mst", 0)])
        op("pool", lambda e: e.memset(eBLc[2][:, :], 1.0), writes=[("eBLc", 2)])
        op("pool", lambda e: e.memset(EPS_AP[:, :], EPS), writes=["eps_ap"])
        op("pool", lambda e: e.memset(ONE_AP[:, :], 1.0), writes=["one_ap"])
        for s_ in range(2):
            for i in range(4):
                op("pool", lambda e, s_=s_, i=i: e.memset(Vm[s_][i][:, 256:258], 1.0),
                   writes=[("Vm1", s_, i)])
        op("dve", lambda e: e.tensor_scalar(nbgk[:, :], smallc(SP_BGK), -1.0, None, ALU.mult),
           reads=["small"], writes=["nbgk"])
        op("dve", lambda e: e.tensor_scalar(nbf[:, :], smallc(SP_BF), -1.0, None, ALU.mult),
           reads=["small"], writes=["nbf"])
        op("dve", lambda e: e.tensor_scalar(nwq[:, :], smallc(SP_MNW, 256), 0.25, None, ALU.mult),
           reads=["small"], writes=["nwq"])
        op("dve", lambda e: e.tensor_scalar(gnwh[:, :], smallc(SP_GNW, 256), 0.5, None, ALU.mult),
           reads=["small"], writes=["gnwh"])
        op("pool", lambda e: e.memset(NEGH[:, :], -0.5), writes=["negh"])
        win_v = win_d.rearrange("(k p) n -> p k n", p=128)
        WG = [(0, 528), (528, 1040), (1040, 1552), (1552, NCOL)]
        for gi, (c0, c1) in enumerate(WG):
            dma("pool", "wg%d" % gi, lambda e, c0=c0, c1=c1: e.dma_start(
                out=W[:, :, c0:c1], in_=win_v[:, :, c0:c1]), writes=[("W", gi)], n=4000)
        Wall = [("W", gi) for gi in range(4)]
        for e_ in ("act", "dve", "pe", "pool"):
            S_._wait(e_, {"pool": S_.cnt["pool"], "dve": S_.cnt["dve"]})

        pb_rr = [0]

        def next_pb():
            b = pb_rr[0]
            pb_rr[0] = (b + 1) % 2
            return b

        def a_tile(m, i):
            slot = m % 2
            ti = 4 * m + i
            xs_ = ti % 2
            xq = ti % NXT
            op("act", lambda e: e.activation(
                junk[:, :], xt[xq][:, :], AF.Square, accum_out=st_small[:, 0:1]),
               reads=[("xt", xq)], writes=["junk", "junk2", "junk3", "sa0"], n=1024)
            op("dve", lambda e: e.tensor_scalar(
                st_small[:, 1:2], st_small[:, 0:1], 1.0 / D, EPS, ALU.mult, ALU.add),
               reads=["sa0"], writes=["sa1"])
            op("pool", lambda e: e.tensor_tensor(
                st_small[:, 2:3], st_small[:, 1:2], NEGH[:, 0:1], ALU.pow),
               reads=["sa1", "negh"], writes=["sa2"])
            op("dve", lambda e: e.scalar_tensor_tensor(
                xs[xs_][:, :], xt[xq][:, :], st_small[:, 2:3], smallc(SP_NWROW, 1024),
                ALU.mult, ALU.mult),
               reads=[("xt", xq), "sa2", "small"], writes=[("xs", xs_)], n=1024)
            for kt in range(8):
                op("pe", lambda e, kt=kt: e.transpose(
                    xT_ps[:, kt, :], xs[xs_][:, kt * 128:(kt + 1) * 128], ident_b[:, :]),
                   reads=[("xs", xs_), "ident_b"], writes=["xT_ps"])
            op("act", lambda e: e.activation(
                xT[slot][:, :, i * 128:(i + 1) * 128], xT_ps[:, :, :], AF.Copy),
               reads=["xT_ps"], writes=[("xT", slot, i)], n=1024)

        def fm_group(m, col0, ncols):
            slot = m % 2
            b = next_pb()
            S_.group_begin()
            for kt in range(8):
                op("pe", lambda e, kt=kt, b=b: e.matmul(
                    PB[b][0:ncols, :], W[:, kt, col0:col0 + ncols], xT[slot][:, kt, :],
                    start=(kt == 0), stop=(kt == 7)),
                   reads=[("W", 0)] + [("xT", slot, i) for i in range(4)], writes=[("PB", b)], n=512)
            S_.group_end()
            return b

        def macro_fm(m):
            s = m % 2
            b = fm_group(m, FM_QM, 128)
            op("act", lambda e, b=b: e.activation(qkraw[s][:, 0, 3:515], PB[b][:, :], AF.Copy),
               reads=[("PB", b)], writes=[("qkraw", s, 0)], n=512)
            b = fm_group(m, FM_KM, 128)
            op("act", lambda e, b=b: e.activation(qkraw[s][:, 1, 3:515], PB[b][:, :], AF.Copy),
               reads=[("PB", b)], writes=[("qkraw", s, 1)], n=512)
            op("pool", lambda e: e.tensor_copy(qkraw[s][:, :, 0:3], qkraw[1 - s][:, :, 512:515]),
               reads=[("qkraw", 1 - s, 0), ("qkraw", 1 - s, 1)], writes=[("qkraw_h", s)])
            for w_ in range(2):
                op("dve", lambda e, w_=w_: e.tensor_scalar(
                    cv[s][:, w_, :], qkraw[s][:, w_, 0:512], smallc(SP_CONVW + 4 * w_),
                    smallc(SP_CONVB + w_), ALU.mult, ALU.add),
                   reads=[("qkraw", s, w_), ("qkraw_h", s), "small"], writes=[("cv", s, w_)], n=512)
                for j in range(1, 4):
                    op("dve", lambda e, w_=w_, j=j: e.scalar_tensor_tensor(
                        cv[s][:, w_, :], qkraw[s][:, w_, j:j + 512], smallc(SP_CONVW + 4 * w_ + j),
                        cv[s][:, w_, :], ALU.mult, ALU.add),
                       reads=[("qkraw", s, w_), ("qkraw_h", s), "small", ("cv", s, w_)],
                       writes=[("cv", s, w_)], n=512)
            cvn = [("cv", s, 0), ("cv", s, 1)]
            op("act", lambda e: e.activation(tcv[:, :, :], cv[s][:, :, :], AF.Tanh, scale=0.5),
               reads=cvn, writes=["tcv", ("tz", 0, 0), ("tz", 0, 1), ("tz", 1, 0), ("tz", 1, 1)], n=1024)
            op("dve", lambda e: e.scalar_tensor_tensor(
                cv[s][:, :, :], tcv[:, :, :], 1.0, cv[s][:, :, :], ALU.add, ALU.mult),
               reads=cvn + ["tcv", ("tz", 0, 0), ("tz", 0, 1), ("tz", 1, 0), ("tz", 1, 1)], writes=cvn, n=1024)
            op("pool", lambda e: e.tensor_scalar(q_bf[s][:, :], cv[s][:, 0, :], 0.5 * QSCALE, 1.0,
                                                 ALU.mult, ALU.mult),
               reads=[("cv", s, 0)], writes=[("q_bf", s)], n=512)

        def macro_gla(m):
            s = m % 2
            b = fm_group(m, FM_GKL, 16)
            op("act", lambda e, b=b: e.activation(gkl_bf[:, :], PB[b][0:16, :], AF.Copy),
               reads=[("PB", b)], writes=["gkl_bf"])
            b = next_pb()
            op("pe", lambda e, b=b: e.matmul(PB[b][:, :], wup[:, :], gkl_bf[:, :],
                                             start=True, stop=True),
               reads=["wup", "gkl_bf"], writes=[("PB", b)], n=512)
            op("act", lambda e, b=b: e.activation(spg[:, :], PB[b][:, :], AF.Exp,
                                                  bias=nbgk[:, :], scale=-1.0),
               reads=[("PB", b), "nbgk"], writes=["spg"], n=512)
            ifn = [("iftok", s, i) for i in range(4)]
            op("act", lambda e: e.activation(ef4[:, :], iftok[s][:, :, 1], AF.Exp,
                                             bias=nbf[:, :], scale=-1.0),
               reads=ifn + ["nbf"], writes=["ef4"])
            S_.group_begin()
            op("act", lambda e: e.activation(spg[:, :], spg[:, :], AF.Ln, bias=ONE_AP[:, :], scale=1.0),
               reads=["spg", "one_ap"], writes=["spg"], n=1800)
            op("act", lambda e: e.activation(spf4[s][:, :], ef4[:, :], AF.Ln, bias=ONE_AP[:, :], scale=1.0),
               reads=["ef4", "one_ap"], writes=[("spf4", s)], n=1500)
            S_.group_end()
            for c in range(4):
                op("dve", lambda e, c=c: e.tensor_tensor_scan(
                    nbc[:, c * 128:(c + 1) * 128], ones_f[:, :], spg[:, c * 128:(c + 1) * 128],
                    0.0, ALU.mult, ALU.add),
                   reads=["spg", "ones_f"], writes=[("nbc", c)], n=256)
            nbcs = [("nbc", c) for c in range(4)]
            op("act", lambda e: e.activation(eB[:, :], nbc[:, :], AF.Exp, scale=-1.0 / 16.0),
               reads=nbcs, writes=["eB"], n=512)
            op("act", lambda e: e.activation(eNB[:, :], nbc[:, :], AF.Exp, scale=1.0 / 16.0),
               reads=nbcs, writes=["eNB"], n=512)
            op("pool", lambda e: e.tensor_copy(
                eBLc[m % 3][:, :], eB[:, :].rearrange("p (c t) -> p c t", t=128)[:, :, 127]),
               reads=["eB"], writes=[("eBLc", m % 3)])
            b = fm_group(m, FM_QG, 128)
            op("dve", lambda e, b=b: e.scalar_tensor_tensor(
                qg_bf[s][:, :], PB[b][:, :], QSCALE, eB[:, :], ALU.mult, ALU.mult),
               reads=[("PB", b), "eB"], writes=[("qg_bf", s)], n=512)
            b = fm_group(m, FM_KG, 128)
            op("dve", lambda e, b=b: e.tensor_tensor(kg_bf[s][:, :], PB[b][:, :], eNB[:, :], ALU.mult),
               reads=[("PB", b), "eNB"], writes=[("kg_bf", s)], n=512)

        def tm_tile(m, i):
            s = m % 2
            tsl = slice(i * 128, (i + 1) * 128)
            for (c0, n) in ((TM0, 512), (TM1, 512), (TM2, 258)):
                b = next_pb()
                S_.group_begin()
                for kt in range(8):
                    op("pe", lambda e, kt=kt, b=b, c0=c0, n=n: e.matmul(
                        PB[b][:, 0:n], xT[s][:, kt, tsl], W[:, kt, c0:c0 + n],
                        start=(kt == 0), stop=(kt == 7)),
                       reads=[("W", {TM0: 1, TM1: 2, TM2: 3}[c0]), ("xT", s, i)], writes=[("PB", b)], n=n)
                S_.group_end()
                j = i % 2
                if c0 == TM0:
                    op("dve", lambda e, b=b: e.tensor_copy(Vm[s][i][:, 0:256], PB[b][:, 0:256]),
                       reads=[("PB", b)], writes=[("Vm", s, i)], n=256)
                    op("act", lambda e, b=b: e.activation(obuf[s][:, i, :], PB[b][:, 256:512],
                                                          AF.Tanh, scale=0.5),
                       reads=[("PB", b)], writes=[("obuf", s, i)], n=256)
                elif c0 == TM1:
                    op("act", lambda e, b=b: e.activation(tz[j][:, 0:256], PB[b][:, 0:256],
                                                          AF.Tanh, scale=0.5),
                       reads=[("PB", b)], writes=[("tz", j, 0)], n=256)
                    op("dve", lambda e, b=b: e.scalar_tensor_tensor(
                        zbuf[s][:, i, 0:256], tz[j][:, 0:256], 1.0, PB[b][:, 0:256], ALU.add, ALU.mult),
                       reads=[("PB", b), ("tz", j, 0)], writes=[("zm", s, i)], n=256)
                    op("dve", lambda e, b=b: e.tensor_copy(Vg[s][i][:, :], PB[b][:, 256:512]),
                       reads=[("PB", b)], writes=[("Vg", s, i)], n=256)
                else:
                    op("act", lambda e, b=b: e.activation(tz[j][:, 256:512], PB[b][:, 0:256],
                                                          AF.Tanh, scale=0.5),
                       reads=[("PB", b)], writes=[("tz", j, 1)], n=256)
                    op("dve", lambda e, b=b: e.scalar_tensor_tensor(
                        zbuf[s][:, i, 256:512], tz[j][:, 256:512], 1.0, PB[b][:, 0:256], ALU.add, ALU.mult),
                       reads=[("PB", b), ("tz", j, 1)], writes=[("zg", s, i)], n=256)
                    op("dve", lambda e, b=b: e.tensor_copy(iftok[s][:, i, :], PB[b][:, 256:258]),
                       reads=[("PB", b)], writes=[("iftok", s, i)])
            gprod(s, i)

        def gprod(s, i):
            op("pool", lambda e: e.tensor_tensor(zbuf[s][:, i, 0:256], zbuf[s][:, i, 0:256],
                                                 nwq[:, :], ALU.mult),
               reads=[("zm", s, i), "nwq"], writes=[("zm", s, i)], n=256)
            op("pool", lambda e: e.tensor_scalar(obuf[s][:, i, :], obuf[s][:, i, :], 1.0, 1.0,
                                                 ALU.mult, ALU.add),
               reads=[("obuf", s, i)], writes=[("obuf", s, i)], n=256)
            op("pool", lambda e: e.tensor_tensor(obuf[s][:, i, :], obuf[s][:, i, :],
                                                 zbuf[s][:, i, 0:256], ALU.mult),
               reads=[("obuf", s, i), ("zm", s, i)], writes=[("obuf", s, i)], n=256)
            op("pool", lambda e: e.tensor_tensor(zbuf[s][:, i, 256:512], zbuf[s][:, i, 256:512],
                                                 gnwh[:, :], ALU.mult),
               reads=[("zg", s, i), "gnwh"], writes=[("zg", s, i)], n=256)

        def x_loads(m):
            for i in range(4):
                x_load(4 * m + i)

        def x_load(ti):
            xq = ti % NXT
            dma("sp", ("xld", xq), lambda e: e.dma_start(
                out=xt[xq][:, :], in_=x_d[ti * 128:(ti + 1) * 128, :]), writes=[("xt", xq)], n=1024)

        def stage_a_stream(m):
            for i in range(4):
                a_tile(m, i)

        def fmtm_stream(m):
            macro_fm(m)
            for i in range(4):
                tm_tile(m, i)
            macro_gla(m)

        def cidx(m, i):
            return 4 * m + i

        def gates(m, i):
            s = m % 2
            cn = cidx(m, i)
            p = cn % 4
            G = gsc[p]
            mp, mn = mst[cn % 2], mst[(cn + 1) % 2]
            mpn, mnn = ("mst", cn % 2), ("mst", (cn + 1) % 2)
            if cn >= 4:
                S_.need(("M", cn - 4))
            op("dve", lambda e: e.tensor_copy(G[:, 1:2], spf4[s][:, i:i + 1]),
               reads=[("spf4", s)], writes=[("g1", p)])
            op("dve", lambda e: e.tensor_scalar(G[:, 2:3], iftok[s][:, i, 0:1], smallc(SP_BI), None,
                                                ALU.add),
               reads=[("iftok", s, i), "small"], writes=[("g2", p)])
            op("pool", lambda e: e.tensor_scalar(igd[:, :], ident_f[:, :], G[:, 2:3], 1.0,
                                                 ALU.mult, ALU.mult),
               reads=["ident_f", ("g2", p)], writes=["igd"])
            op("dve", lambda e: e.scalar_tensor_tensor(rmat[:, :], tri_f[:, :], G[:, 1:2], igd[:, :],
                                                       ALU.mult, ALU.add),
               reads=["tri_f", ("g1", p), "igd"], writes=["rmat"])
            op("pe", lambda e: e.matmul(UREP, ones_f[:, :], rmat[:, :], start=True, stop=True),
               reads=["rmat", "ones_f"], writes=["urep"])
            op("pe", lambda e: e.matmul(GC0, tri_f[:, :], G[:, 1:3], start=True, stop=True),
               reads=["tri_f", ("g1", p), ("g2", p)], writes=["gc0"])
            op("pe", lambda e: e.matmul(GC1, ones_f[:, :], G[:, 1:3], start=True, stop=True),
               reads=["ones_f", ("g1", p), ("g2", p)], writes=["gc1"])
            op("dve", lambda e: e.tensor_reduce(G[:, 3:4], UREP, AX.X, ALU.max),
               reads=["urep"], writes=[("g3", p)])
            op("dve", lambda e: e.tensor_tensor(G[:, 4:5], G[:, 3:4], mp[:, :], ALU.max),
               reads=[("g3", p), mpn], writes=[("g4", p)])
            op("dve", lambda e: e.tensor_scalar(G[:, 5:6], G[:, 4:5], -1.0, None, ALU.mult),
               reads=[("g4", p)], writes=[("g5", p)])
            op("act", lambda e: e.activation(wrep[p][:, :], UREP, AF.Exp, bias=G[:, 5:6], scale=1.0),
               reads=["urep", ("g5", p)], writes=[("wrep", p)])
            op("act", lambda e: e.activation(G[:, 6:7], mp[:, :], AF.Exp, bias=G[:, 5:6], scale=1.0),
               reads=[mpn, ("g5", p)], writes=[("g6", p)])
            op("act", lambda e: e.activation(G[:, 7:8], MB2a[:, 260:261], AF.Exp, bias=G[:, 5:6], scale=1.0),
               reads=["gc0", ("g5", p)], writes=[("g7", p)])
            op("dve", lambda e: e.scalar_tensor_tensor(mn[:, :], MB2a[:, 262:263], -1.0, G[:, 4:5],
                                                       ALU.mult, ALU.add),
               reads=["gc1", ("g4", p)], writes=[mnn])
            S_.mark(("G", cn))

        def hsl(m, i):
            g = m // MPS
            return g % 2, (4 * m + i) * 128 - g * SEG

        def mlstm(m, i):
            s = m % 2
            p = cidx(m, i) % 4
            G = gsc[p]
            csl = slice(i * 128, (i + 1) * 128)
            hslot, tok0 = hsl(m, i)
            p2 = cidx(m, i) % 2
            S_.need(("G", cidx(m, i)))
            if cidx(m, i) >= 2:
                S_.need(("M", cidx(m, i) - 2))
            op("dve", lambda e: e.scalar_tensor_tensor(kh_bf[s][:, csl], cv[s][:, 1, csl], 0.5,
                                                       wrep[p][:, :], ALU.mult, ALU.mult),
               reads=[("cv", s, 1), ("wrep", p)], writes=[("kh_bf", s, i)])
            op("pe", lambda e: e.matmul(ATM, kh_bf[s][:, csl], q_bf[s][:, csl], start=True, stop=True),
               reads=[("kh_bf", s, i), ("q_bf", s)], writes=["ATm"])
            op("pe", lambda e: e.transpose(KTM, kh_bf[s][:, csl], ident_b[:, :]),
               reads=[("kh_bf", s, i), "ident_b"], writes=["kTm"])
            op("dve", lambda e: e.tensor_tensor(ATm_bf[:, :], ATM, tri_f[:, :], ALU.mult),
               reads=["ATm", "tri_f"], writes=["ATm_bf"])
            op("act", lambda e: e.activation(km_tok[:, :], KTM, AF.Copy),
               reads=["kTm"], writes=["km_tok"])
            op("pool", lambda e: e.tensor_scalar(Ch_bf[:, :], Cst[:, :], G[:, 6:7], 1.0,
                                                 ALU.mult, ALU.mult),
               reads=["Cst", ("g6", p)], writes=["Ch_bf"], n=258)
            S_.group_begin()
            op("pe", lambda e: e.matmul(NUM, ATm_bf[:, :], Vm[s][i][:, :], start=True, stop=False),
               reads=["ATm_bf", ("Vm", s, i), ("Vm1", s, i)], writes=["num"], n=258)
            op("pe", lambda e: e.matmul(NUM, q_bf[s][:, csl], Ch_bf[:, :], start=False, stop=True),
               reads=[("q_bf", s), "Ch_bf"], writes=["num"], n=258)
            S_.group_end()
            op("act", lambda e: e.activation(numS[p2][:, :], NUM, AF.Copy),
               reads=["num"], writes=[("numS", p2)], n=258)
            op("pe", lambda e: e.matmul(DC, km_tok[:, :], Vm[s][i][:, :], start=True, stop=True),
               reads=["km_tok", ("Vm", s, i), ("Vm1", s, i)], writes=["dC"], n=258)
            op("dve", lambda e: e.scalar_tensor_tensor(Cst[:, :], Cst[:, :], G[:, 6:7], DC,
                                                       ALU.mult, ALU.add),
               reads=["Cst", ("g6", p), "dC", "Ch_bf"], writes=["Cst"], n=258)
            S_.mark(("A", cidx(m, i)))

        def mlstm_out(m, i):
            s = m % 2
            cn = cidx(m, i)
            p = cn % 4
            p2 = cn % 2
            G = gsc[p]
            hslot, tok0 = hsl(m, i)
            Gm_ = obuf[s][:, i, :]
            NS = numS[p2]
            nsn = ("numS", p2)
            S_.need(("A", cn))
            op("act", lambda e: e.activation(osc[:, 11:12], NS[:, 256:257], AF.Abs),
               reads=[nsn], writes=["o11"])
            op("act", lambda e: e.activation(junk[:, 0:256], NS[:, 0:256], AF.Square, scale=1.0 / 16.0,
                                             accum_out=osc[:, 2:3]),
               reads=[nsn], writes=["junk", "o2"], n=256)
            op("dve", lambda e: e.tensor_tensor(osc[:, 0:1], osc[:, 11:12], G[:, 7:8], ALU.max),
               reads=["o11", ("g7", p)], writes=["o0"])
            op("dve", lambda e: e.tensor_tensor(osc[:, 3:4], osc[:, 0:1], osc[:, 0:1], ALU.mult),
               reads=["o0"], writes=["o3"])
            op("dve", lambda e: e.scalar_tensor_tensor(osc[:, 4:5], osc[:, 3:4], EPS, osc[:, 2:3],
                                                       ALU.mult, ALU.add),
               reads=["o3", "o2"], writes=["o4"])
            op("pool", lambda e: e.tensor_tensor(osc[:, 7:8], osc[:, 4:5], NEGH[:, 1:2], ALU.pow),
               reads=["o4", "negh"], writes=["o7"])
            op("dve", lambda e: e.scalar_tensor_tensor(htile[:, 0:256], NS[:, 0:256], osc[:, 7:8],
                                                       Gm_, ALU.mult, ALU.mult),
               reads=[nsn, "o7", ("obuf", s, i)], writes=["htile_m"], n=256)
            S_.group_begin()
            for bk in range(2):
                op("pe", lambda e, bk=bk: e.transpose(HTPM[:, bk, :],
                                                      htile[:, bk * 128:(bk + 1) * 128], ident_b[:, :]),
                   reads=["htile_m", "ident_b"], writes=["hTm_ps"])
            S_.group_end()
            op("act", lambda e: e.activation(
                hTst[hslot][:, 0:2, tok0:tok0 + 128], HTPM[:, :, :], AF.Copy),
               reads=["hTm_ps"], writes=[("hTst", hslot, 0)], n=256)
            S_.mark(("M", cidx(m, i)))

        def gla(m, i):
            s = m % 2
            csl = slice(i * 128, (i + 1) * 128)
            hslot, tok0 = hsl(m, i)
            if i == 0:
                dec, decn = eBLc[(m + 2) % 3][:, 3:4], ("eBLc", (m + 2) % 3)
            else:
                dec, decn = eBLc[m % 3][:, i - 1:i], ("eBLc", m % 3)
            if cidx(m, i) >= 2:
                S_.need(("GM", cidx(m, i) - 2))
            op("pe", lambda e: e.matmul(ATG, kg_bf[s][:, csl], qg_bf[s][:, csl], start=True, stop=True),
               reads=[("kg_bf", s), ("qg_bf", s)], writes=["ATg"])
            op("pe", lambda e: e.transpose(KTG, kg_bf[s][:, csl], ident_b[:, :]),
               reads=[("kg_bf", s), "ident_b"], writes=["kTg"])
            op("dve", lambda e: e.tensor_tensor(ATg_bf[:, :], ATG, tri_f[:, :], ALU.mult),
               reads=["ATg", "tri_f"], writes=["ATg_bf"])
            op("act", lambda e: e.activation(kg_tok[:, :], KTG, AF.Copy),
               reads=["kTg"], writes=["kg_tok"])
            op("pool", lambda e: e.tensor_scalar(Sh_bf[:, :], Ust[:, :], dec, 1.0, ALU.mult, ALU.mult),
               reads=["Ust", decn], writes=["Sh_bf"], n=256)
            S_.group_begin()
            op("pe", lambda e: e.matmul(OG, ATg_bf[:, :], Vg[s][i][:, :], start=True, stop=False),
               reads=["ATg_bf", ("Vg", s, i)], writes=["og"], n=256)
            op("pe", lambda e: e.matmul(OG, qg_bf[s][:, csl], Sh_bf[:, :], start=False, stop=True),
               reads=[("qg_bf", s), "Sh_bf"], writes=["og"], n=256)
            S_.group_end()
            op("act", lambda e: e.activation(ogS[cidx(m, i) % 2][:, :], OG, AF.Copy),
               reads=["og"], writes=[("ogS", cidx(m, i) % 2)], n=256)
            op("pe", lambda e: e.matmul(DS, kg_tok[:, :], Vg[s][i][:, :], start=True, stop=True),
               reads=["kg_tok", ("Vg", s, i)], writes=["dS"], n=256)
            op("dve", lambda e: e.scalar_tensor_tensor(Ust[:, :], Ust[:, :], dec, DS, ALU.mult, ALU.add),
               reads=["Ust", decn, "dS", "Sh_bf"], writes=["Ust"], n=256)
            S_.mark(("GA", cidx(m, i)))

        def gla_out(m, i):
            s = m % 2
            cn = cidx(m, i)
            p2 = cn % 2
            hslot, tok0 = hsl(m, i)
            Gg_ = zbuf[s][:, i, 256:512]
            OS = ogS[p2]
            osn = ("ogS", p2)
            S_.need(("GA", cn))
            op("act", lambda e: e.activation(junk[:, 256:512], OS[:, :], AF.Square, scale=1.0 / 16.0,
                                             accum_out=oscg[:, 0:1]),
               reads=[osn], writes=["junk2", "p0"], n=256)
            op("dve", lambda e: e.tensor_scalar(oscg[:, 1:2], oscg[:, 0:1], EPS, None, ALU.add),
               reads=["p0"], writes=["p1"])
            op("pool", lambda e: e.tensor_tensor(oscg[:, 2:3], oscg[:, 1:2], NEGH[:, 2:3], ALU.pow),
               reads=["p1", "negh"], writes=["p2"])
            op("dve", lambda e: e.scalar_tensor_tensor(htile[:, 256:512], OS[:, :], oscg[:, 2:3],
                                                       Gg_, ALU.mult, ALU.mult),
               reads=[osn, "p2", ("zg", s, i)], writes=["htile_g"], n=256)
            S_.group_begin()
            for bk in range(2):
                op("pe", lambda e, bk=bk: e.transpose(HTPG[:, bk, :],
                                                      htile[:, 256 + bk * 128:256 + (bk + 1) * 128],
                                                      ident_b[:, :]),
                   reads=["htile_g", "ident_b"], writes=["hTg_ps"])
            S_.group_end()
            op("act", lambda e: e.activation(
                hTst[hslot][:, 2:4, tok0:tok0 + 128], HTPG[:, :, :], AF.Copy),
               reads=["hTg_ps"], writes=[("hTst", hslot, 1)], n=256)
            S_.mark(("GM", cidx(m, i)))

        def gates_stream(m):
            for i in range(4):
                gates(m, i)

        def mlstm_stream(m):
            for i in range(4):
                mlstm(m, i)

        def gla_stream(m):
            for i in range(4):
                gla(m, i)

        def mlstm_out_stream(m):
            for i in range(4):
                mlstm_out(m, i)

        def gla_out_stream(m):
            for i in range(4):
                gla_out(m, i)

        def seg_send(g):
            hslot = g % 2
            dma("sp", ("hst", hslot), lambda e: e.dma_start(
                out=hsend[g].ap().rearrange("(b p) t -> p b t", p=128), in_=hTst[hslot][:, :, :]),
                reads=[("hTst", hslot, 0), ("hTst", hslot, 1)], writes=[("hsend", g)])
            dma("pool", "ccA%d" % (g % 2), lambda e: e.collective_compute(
                "AllGather", ALU.bypass, replica_groups=GROUPS,
                ins=[hsend[g].ap().opt()], outs=[hall[g].ap().opt()], dma_qos=CCQOS),
                reads=[("hsend", g)], writes=[("hall", g)], inc=cc_inc, n=15000)

        def outproj_loads(g):
            dma("sp", "hld", lambda e: e.dma_start(
                out=hTall[:, :, :], in_=hall[g].ap().rearrange("(k p) t -> p k t", p=128)),
                reads=[("hall", g)], writes=["hTall"])
            for tt in range(TS):
                xr_load(g, tt)

        def xr_load(g, tt):
            ti = g * TS + tt
            dma("sp", ("xrld", tt), lambda e: e.dma_start(
                out=xr[tt][:, :], in_=xres_d[ti * 128:(ti + 1) * 128, :]), writes=[("xr", tt)])

        def outproj_a(g):
            ys_ = g % 2
            for tt in range(TS):
                outproj_tile(g, tt)
            ssqs = [("ssq", ys_, tt) for tt in range(TS)]
            dma("sp", "sst", lambda e: e.dma_start(out=ssq_i[g].ap(), in_=ssq[ys_][:, :]),
                reads=ssqs, writes=[("ssq_i", g)])
            dma("pool", "ccR%d" % (g % 2), lambda e: e.collective_compute(
                "AllReduce", ALU.add, replica_groups=GROUPS,
                ins=[ssq_i[g].ap().opt()], outs=[ssq_o[g].ap().opt()], dma_qos=CCQOS),
                reads=[("ssq_i", g)], writes=[("ssq_o", g)], inc=cc_inc, n=12000)

        def outproj_tile(g, tt):
            ys_ = g % 2
            ti = g * TS + tt
            xs_ = tt
            for k in range(16):
                op("pe", lambda e, k=k: e.matmul(
                    PB[2][:, 0:256], hTall[:, k, tt * 128:(tt + 1) * 128], Wout[:, k, :],
                    start=(k == 0), stop=(k == 15)),
                   reads=["hTall", "Wout"], writes=[("PB", 2)], n=256)
            op("dve", lambda e: e.tensor_tensor(
                ypre[ys_][:, tt, :], PB[2][:, 0:256], xr[xs_][:, :], ALU.add),
               reads=[("PB", 2), ("xr", xs_)], writes=[("ypre", ys_, tt)], n=256)
            op("act", lambda e: e.activation(junk[:, 512:768], ypre[ys_][:, tt, :], AF.Square,
                                             accum_out=ssq[ys_][:, tt:tt + 1]),
               reads=[("ypre", ys_, tt)], writes=["junk3", ("ssq", ys_, tt)], n=256)

        def outproj_b(g):
            ys_ = g % 2
            dma("sp", "sld", lambda e: e.dma_start(out=ssr[:, :], in_=ssq_o[g].ap()),
                reads=[("ssq_o", g)], writes=["ssr"])
            op("dve", lambda e: e.tensor_scalar(rfin[:, :], ssr[:, :], 1.0 / D, EPS, ALU.mult, ALU.add),
               reads=["ssr"], writes=["rfin0"])
            op("pool", lambda e: e.tensor_tensor(rfin[:, :], rfin[:, :], NEGH[:, 0:TS], ALU.pow),
               reads=["rfin0", "negh"], writes=["rfin"])
            for tt in range(TS):
                outproj_b_tile(g, tt)

        def outproj_b_tile(g, tt):
            ys_ = g % 2
            ti = g * TS + tt
            yb = tt
            op("dve", lambda e: e.scalar_tensor_tensor(
                yo[yb][:, :], ypre[ys_][:, tt, :], rfin[:, tt:tt + 1], smallc(SP_FNW, 256),
                ALU.mult, ALU.mult),
               reads=[("ypre", ys_, tt), "rfin", "small"], writes=[("yo", yb)], n=256)
            dma("sp", ("yst", yb), lambda e: e.dma_start(
                out=y_d[ti * 128:(ti + 1) * 128, :], in_=yo[yb][:, :]),
                reads=[("yo", yb)], writes=[("y", ti)])

        cap = S_.capture
        x_loads(0)
        for it in cap(stage_a_stream, 0):
            S_.emit(it)
        if NM > 1:
            x_loads(1)
        S_.interleave([cap(fmtm_stream, 0)] + ([cap(stage_a_stream, 1)] if NM > 1 else []))
        if NM > 2:
            x_loads(2)
        LAG_A, LAG_B = int(os.environ.get("KLA", "2")), int(os.environ.get("KLB", "3"))
        loads_done = set()
        for m in range(NM):
            g = m // MPS
            import os
            if os.environ.get("KSEQ") == "1":
                def seq_stream(m):
                    for i in range(4):
                        gates(m, i)
                        mlstm(m, i)
                        mlstm_out(m, i)
                        gla(m, i)
                        gla_out(m, i)
                streams = [cap(seq_stream, m)]
            else:
                streams = [cap(gates_stream, m), cap(mlstm_stream, m), cap(gla_stream, m),
                           cap(mlstm_out_stream, m), cap(gla_out_stream, m)]
            WG = float(os.environ.get("KWG", "0.5"))
            WP = float(os.environ.get("KWP", "1.0"))
            weights = [WG, 1.0, 1.0, 1.0, 1.0][:len(streams)] if len(streams) == 5 else [1.0]
            if m + 1 < NM:
                if os.environ.get("KPF", "0") == "1":
                    streams.insert(0, cap(fmtm_stream, m + 1))
                else:
                    streams.append(cap(fmtm_stream, m + 1))
                weights.append(WP)
            last_of_seg = (m + 1) % MPS == 0
            if last_of_seg and g - LAG_A >= 0:
                streams.append(cap(outproj_a, g - LAG_A))
                weights.append(1.0)
            if last_of_seg and g - LAG_B >= 0:
                streams.append(cap(outproj_b, g - LAG_B))
                weights.append(1.0)
            if m + 2 < NM:
                streams.append(cap(stage_a_stream, m + 2))
                weights.append(1.0)
            S_.interleave(streams, weights)
            if m + 3 < NM:
                x_loads(m + 3)
            if last_of_seg:
                if 0 <= g + 1 - LAG_A:
                    outproj_loads(g + 1 - LAG_A)
                    loads_done.add(g + 1 - LAG_A)
                seg_send(g)
        na, nb = max(0, NSEG - LAG_A), max(0, NSEG - LAG_B)
        while na < NSEG or nb < NSEG:
            if na < NSEG:
                if na not in loads_done:
                    outproj_loads(na)
                outproj_a(na)
                na += 1
            while nb < NSEG and (nb < na - 1 or na == NSEG):
                outproj_b(nb)
                nb += 1

        S_.wait_all("sp", list(S_.sem.keys()))
        with nc.Block() as block:
            block.tensor(lambda e: S_.replay("pe", e))
            block.vector(lambda e: S_.replay("dve", e))
            block.scalar(lambda e: S_.replay("act", e))
            block.gpsimd(lambda e: S_.replay("pool", e))
            block.sync(lambda e: S_.replay("sp", e))
        print("ninst", S_.ninst)
    return nc


def _prep_inputs(x, norm_w, w_in, conv_w, conv_b, b_igate, b_fgate, mlstm_norm_w,
                 w_gk_up, b_gk, gla_norm_w, w_out, final_norm_w):
    f = np.float32
    x = np.asarray(x, f)
    w_in = np.asarray(w_in, f)
    o = 0
    offs = {}
    for name, n in (("qm", 512), ("km", 512), ("vm", 1024), ("i", 4), ("f", 4), ("o", 1024),
                    ("zm", 1024), ("qg", 512), ("kg", 512), ("vg", 1024), ("gkl", 16), ("zg", 1024)):
        offs[name] = o
        o += n
    conv_w = np.asarray(conv_w, f)
    conv_b = np.asarray(conv_b, f)
    w_out = np.asarray(w_out, f)
    in_maps = []
    for c in range(8):
        b, j = c // 4, c % 4
        cols = np.concatenate([
            np.arange(offs["qm"] + 128 * j, offs["qm"] + 128 * (j + 1)),
            np.arange(offs["km"] + 128 * j, offs["km"] + 128 * (j + 1)),
            np.arange(offs["qg"] + 128 * j, offs["qg"] + 128 * (j + 1)),
            np.arange(offs["kg"] + 128 * j, offs["kg"] + 128 * (j + 1)),
            np.arange(offs["gkl"], offs["gkl"] + 16),
            np.arange(offs["vm"] + 256 * j, offs["vm"] + 256 * (j + 1)),
            np.arange(offs["o"] + 256 * j, offs["o"] + 256 * (j + 1)),
            np.arange(offs["zm"] + 256 * j, offs["zm"] + 256 * (j + 1)),
            np.arange(offs["vg"] + 256 * j, offs["vg"] + 256 * (j + 1)),
            np.arange(offs["zg"] + 256 * j, offs["zg"] + 256 * (j + 1)),
            np.array([offs["i"] + j, offs["f"] + j]),
        ])
        assert cols.size == NCOL
        w_in_c = np.ascontiguousarray(w_in[:, cols])
        wup_c = np.ascontiguousarray(np.asarray(w_gk_up, f)[:, 128 * j:128 * (j + 1)])
        rows = np.concatenate([np.concatenate([np.arange(256 * r, 256 * (r + 1)),
                                               np.arange(1024 + 256 * r, 1024 + 256 * (r + 1))])
                               for r in range(4)])
        wout_c = np.ascontiguousarray(w_out[rows][:, 256 * j:256 * (j + 1)])
        xres_c = np.ascontiguousarray(x[b][:, 256 * j:256 * (j + 1)])
        small = np.zeros((128, NSMALL), f)
        small[:, SP_NORMW:SP_NORMW + 8] = np.asarray(norm_w, f).reshape(8, 128).T
        small[:, SP_CONVW:SP_CONVW + 4] = conv_w[:, 128 * j:128 * (j + 1)].T
        small[:, SP_CONVW + 4:SP_CONVW + 8] = conv_w[:, 512 + 128 * j:512 + 128 * (j + 1)].T
        small[:, SP_CONVB] = conv_b[128 * j:128 * (j + 1)]
        small[:, SP_CONVB + 1] = conv_b[512 + 128 * j:512 + 128 * (j + 1)]
        small[:, SP_BGK] = np.asarray(b_gk, f)[128 * j:128 * (j + 1)]
        small[:, SP_BI] = np.asarray(b_igate, f)[j]
        small[:, SP_BF] = np.asarray(b_fgate, f)[j]
        small[:, SP_MNW:SP_MNW + 256] = np.asarray(mlstm_norm_w, f)[256 * j:256 * (j + 1)][None, :]
        small[:, SP_GNW:SP_GNW + 256] = np.asarray(gla_norm_w, f)[256 * j:256 * (j + 1)][None, :]
        small[:, SP_FNW:SP_FNW + 256] = np.asarray(final_norm_w, f)[256 * j:256 * (j + 1)][None, :]
        small[:, SP_NWROW:SP_NWROW + 1024] = np.asarray(norm_w, f)[None, :]
        in_maps.append({"x_b": np.ascontiguousarray(x[b]), "w_in_c": w_in_c, "wup_c": wup_c,
                        "wout_c": wout_c, "xres_c": xres_c, "small_c": small})
    return in_maps


def run(inputs, NSEG=None, trace=False):
    x = np.asarray(inputs["x"])
    B, S, _ = x.shape
    assert B == 2
    if NSEG is None:
        NSEG = S // 512
    nc = build_nc(S, NSEG)
    in_maps = _prep_inputs(**inputs)
    res = run_bass_kernel_spmd(nc, in_maps, core_ids=list(range(8)), trace=trace)
    out = np.empty((B, S, D), np.float32)
    for c in range(8):
        b, j = c // 4, c % 4
        out[b, :, 256 * j:256 * (j + 1)] = np.asarray(res.results[c]["y_c"], np.float32)
    return out, res


def kernel(**inputs):
    out, _ = run(inputs)
    return out
```
